# Optimizing a Trainium2 kernel written in Bass

```python
import math
import jax, jax.numpy as jnp
from jax import lax
import numpy as np

D_MODEL = 1024
BATCH = 8
SEQ = 2048
DEPTH = 4
DEC_BATCH = 32
DEC_SEQ = 1
PAST_LEN = 8192
PAGE_SIZE = 128

N_A = DEPTH // 2
N_B = DEPTH - N_A
CONV_CH = 3 * D_MODEL // 4
CONV_WIDTH = 31
SB_HEAD_DIM = 64
SB_HEADS = CONV_CH // SB_HEAD_DIM
SB_WIDTH = SB_HEADS * SB_HEAD_DIM
SB_BIAS_INIT = -6.0
MEM_HEADS = 4
MEM_HEAD_DIM = 64
MEM_WIDTH = MEM_HEADS * MEM_HEAD_DIM
N_MEM = 256
MIX_WIDTH = CONV_CH + MEM_WIDTH
D_FF = ((8 * D_MODEL + 3 * 256 - 1) // (3 * 256)) * 256
Q_BLOCK = 128
RMS_EPS = 1e-6
LN_EPS = 1e-5

kernel_name = "yoco_conformer_stickbreaking_decode_step"


def rms_norm(x, g):
    xf = x.astype(jnp.float32)
    y = xf * lax.rsqrt(jnp.mean(xf * xf, axis=-1, keepdims=True) + RMS_EPS) * g.astype(jnp.float32)
    return y.astype(x.dtype)


def conv_module(a, gate, state, w, b, ln_g, ln_b):
    u = a * jax.nn.sigmoid(gate)
    ext = jnp.concatenate([state.astype(u.dtype), u], axis=1)
    new_state = ext[:, ext.shape[1] - (CONV_WIDTH - 1):]
    y = lax.conv_general_dilated(ext, w.astype(ext.dtype)[:, None, :], window_strides=(1,),
                                 padding='VALID', dimension_numbers=('NWC', 'WIO', 'NWC'),
                                 feature_group_count=u.shape[-1])
    y = y.astype(jnp.float32) + b.astype(jnp.float32)
    mu = jnp.mean(y, axis=-1, keepdims=True)
    var = jnp.mean(jnp.square(y - mu), axis=-1, keepdims=True)
    y = (y - mu) * lax.rsqrt(var + LN_EPS) * ln_g.astype(jnp.float32) + ln_b.astype(jnp.float32)
    y = y * jax.nn.sigmoid(y)
    return y.astype(u.dtype), new_state


def stick_breaking_attention(q, k, v, bias, q_offset):
    B, Tq, H, dh = q.shape
    Tk = k.shape[1]
    blk = min(Tq, Q_BLOCK)
    nb = -(-Tq // blk)
    pad = nb * blk - Tq
    qp = jnp.pad(q, ((0, 0), (0, pad), (0, 0), (0, 0)))
    qb = qp.reshape(B, nb, blk, H, dh).transpose(1, 0, 2, 3, 4)
    kf = k.astype(jnp.float32)
    vf = v.astype(jnp.float32)
    bf = bias.astype(jnp.float32)[None, :, None, None]
    k_pos = jnp.arange(Tk)
    scale = 1.0 / math.sqrt(dh)

    def one_block(args):
        qblk, start = args
        z = jnp.einsum('bqhd,bkhd->bhqk', qblk.astype(jnp.float32), kf) * scale + bf
        q_pos = q_offset + start + jnp.arange(blk)
        causal = k_pos[None, :] < q_pos[:, None]
        log_keep = jnp.where(causal, jax.nn.log_sigmoid(-z), 0.0)
        after = lax.cumsum(log_keep, axis=3, reverse=True) - log_keep
        w = jnp.where(causal, jnp.exp(jax.nn.log_sigmoid(z) + after), 0.0)
        return jnp.einsum('bhqk,bkhd->bqhd', w, vf)

    out = lax.map(one_block, (qb, jnp.arange(nb) * blk))
    out = out.transpose(1, 0, 2, 3, 4).reshape(B, nb * blk, H, dh)[:, :Tq]
    return out.astype(q.dtype)


def memory_attention(q, mk, mv):
    s = jnp.einsum('bthd,bmhd->bhtm', q.astype(jnp.float32), mk.astype(jnp.float32)) / math.sqrt(MEM_HEAD_DIM)
    p = jax.nn.softmax(s, axis=-1)
    return jnp.einsum('bhtm,bmhd->bthd', p, mv.astype(jnp.float32)).astype(q.dtype)


def run_trunk(x, mem_k, mem_v, conv_state, past_k, past_v,
              norm_mix_pre, norm_mix_post, norm_ffn_pre, norm_ffn_post,
              w_in_a, conv_w, conv_b, conv_ln_g, conv_ln_b, w_in_b, sb_bias, kv_norm_g, w_kv,
              w_out, w_ffn_up, w_ffn_down):
    B, T, _ = x.shape
    past = past_k.shape[1]
    new_conv = []
    k_new = v_new = k_all = v_all = None
    for l in range(DEPTH):
        h = rms_norm(x, norm_mix_pre[l])
        if l < N_A:
            proj = h @ w_in_a[l]
            a, gate, q_mem = jnp.split(proj, [CONV_CH, 2 * CONV_CH], axis=-1)
            mix, st = conv_module(a, gate, conv_state[l], conv_w[l], conv_b[l], conv_ln_g[l], conv_ln_b[l])
            new_conv.append(st)
        else:
            proj = h @ w_in_b[l - N_A]
            q_sb, q_mem = jnp.split(proj, [SB_WIDTH], axis=-1)
            mix = stick_breaking_attention(q_sb.reshape(B, T, SB_HEADS, SB_HEAD_DIM), k_all, v_all,
                                           sb_bias[l - N_A], past)
            mix = mix.reshape(B, T, SB_WIDTH)
        mo = memory_attention(q_mem.reshape(B, T, MEM_HEADS, MEM_HEAD_DIM), mem_k[l], mem_v[l]).reshape(B, T, MEM_WIDTH)
        x = x + rms_norm(jnp.concatenate([mix, mo], axis=-1) @ w_out[l], norm_mix_post[l])
        h = rms_norm(x, norm_ffn_pre[l])
        g, u = jnp.split(h @ w_ffn_up[l], 2, axis=-1)
        x = x + rms_norm((jax.nn.silu(g) * u) @ w_ffn_down[l], norm_ffn_post[l])
        if l == N_A - 1:
            kv = rms_norm(x, kv_norm_g) @ w_kv
            k_new, v_new = jnp.split(kv, 2, axis=-1)
            k_new = k_new.reshape(B, T, SB_HEADS, SB_HEAD_DIM)
            v_new = v_new.reshape(B, T, SB_HEADS, SB_HEAD_DIM)
            k_all = jnp.concatenate([past_k.astype(k_new.dtype), k_new], axis=1)
            v_all = jnp.concatenate([past_v.astype(v_new.dtype), v_new], axis=1)
    return x, jnp.stack(new_conv), k_new, v_new


def setup_inputs(seed: int = 0) -> dict:
    key = jax.random.key(seed)
    ks = jax.random.split(key, 32)
    n_pages = PAST_LEN // PAGE_SIZE
    n_phys = (DEC_BATCH * n_pages * 5 + 3) // 4
    f32 = jnp.float32

    def nrm(k, shape, scale=1.0):
        return jax.random.normal(k, shape, f32) * scale

    def gain(k, shape):
        return 1.0 + 0.05 * jax.random.normal(k, shape, f32)

    perm = jax.random.permutation(ks[0], n_phys)
    page_table = perm[:DEC_BATCH * n_pages].reshape(DEC_BATCH, n_pages).astype(jnp.int32)
    return {
        'x_prompt': nrm(ks[1], (BATCH, SEQ, D_MODEL)),
        'x_sample': nrm(ks[2], (DEC_BATCH, DEC_SEQ, D_MODEL)),
        'cache_k': nrm(ks[3], (n_phys, PAGE_SIZE, SB_HEADS, SB_HEAD_DIM)),
        'cache_v': nrm(ks[4], (n_phys, PAGE_SIZE, SB_HEADS, SB_HEAD_DIM)),
        'state_conv': nrm(ks[5], (N_A, DEC_BATCH, CONV_WIDTH - 1, CONV_CH), 0.5),
        'cache_mem_k': nrm(ks[6], (DEPTH, DEC_BATCH, N_MEM, MEM_HEADS, MEM_HEAD_DIM)),
        'cache_mem_v': nrm(ks[7], (DEPTH, DEC_BATCH, N_MEM, MEM_HEADS, MEM_HEAD_DIM)),
        'page_table': page_table,
        'mem_prompt': nrm(ks[8], (BATCH, N_MEM, D_MODEL)),
        'norm_mix_pre': gain(ks[9], (DEPTH, D_MODEL)),
        'norm_mix_post': gain(ks[10], (DEPTH, D_MODEL)),
        'norm_ffn_pre': gain(ks[11], (DEPTH, D_MODEL)),
        'norm_ffn_post': gain(ks[12], (DEPTH, D_MODEL)),
        'w_in_a': nrm(ks[13], (N_A, D_MODEL, 2 * CONV_CH + MEM_WIDTH), D_MODEL ** -0.5),
        'conv_w': nrm(ks[14], (N_A, CONV_WIDTH, CONV_CH), CONV_WIDTH ** -0.5),
        'conv_b': nrm(ks[15], (N_A, CONV_CH), 0.02),
        'conv_ln_g': gain(ks[16], (N_A, CONV_CH)),
        'conv_ln_b': nrm(ks[17], (N_A, CONV_CH), 0.02),
        'w_in_b': nrm(ks[18], (N_B, D_MODEL, SB_WIDTH + MEM_WIDTH), D_MODEL ** -0.5),
        'sb_bias': SB_BIAS_INIT + nrm(ks[26], (N_B, SB_HEADS), 0.1),
        'kv_norm_g': gain(ks[19], (D_MODEL,)),
        'w_kv': nrm(ks[20], (D_MODEL, 2 * SB_WIDTH), D_MODEL ** -0.5),
        'mem_norm_g': gain(ks[21], (DEPTH, D_MODEL)),
        'w_mem_kv': nrm(ks[22], (DEPTH, D_MODEL, 2 * MEM_WIDTH), D_MODEL ** -0.5),
        'w_out': nrm(ks[23], (DEPTH, MIX_WIDTH, D_MODEL), MIX_WIDTH ** -0.5),
        'w_ffn_up': nrm(ks[24], (DEPTH, D_MODEL, 2 * D_FF), D_MODEL ** -0.5),
        'w_ffn_down': nrm(ks[25], (DEPTH, D_FF, D_MODEL), D_FF ** -0.5),
    }


def reference(x_prompt, x_sample, cache_k, cache_v, state_conv, cache_mem_k, cache_mem_v,
              page_table, mem_prompt, norm_mix_pre, norm_mix_post, norm_ffn_pre, norm_ffn_post,
              w_in_a, conv_w, conv_b, conv_ln_g, conv_ln_b, w_in_b, sb_bias, kv_norm_g, w_kv,
              mem_norm_g, w_mem_kv, w_out, w_ffn_up, w_ffn_down):
    memn = rms_norm(mem_prompt[None], mem_norm_g[:, None, None, :])
    mkv = jnp.einsum('lbmd,lde->lbme', memn, w_mem_kv)
    mk_p, mv_p = jnp.split(mkv, 2, axis=-1)
    mk_p = mk_p.reshape(DEPTH, BATCH, N_MEM, MEM_HEADS, MEM_HEAD_DIM)
    mv_p = mv_p.reshape(DEPTH, BATCH, N_MEM, MEM_HEADS, MEM_HEAD_DIM)
    zero_conv = jnp.zeros((N_A, BATCH, CONV_WIDTH - 1, CONV_CH), x_prompt.dtype)
    empty = jnp.zeros((BATCH, 0, SB_HEADS, SB_HEAD_DIM), x_prompt.dtype)
    y_prompt, conv_prompt, k_prompt, v_prompt = run_trunk(
        x_prompt, mk_p, mv_p, zero_conv, empty, empty,
        norm_mix_pre, norm_mix_post, norm_ffn_pre, norm_ffn_post,
        w_in_a, conv_w, conv_b, conv_ln_g, conv_ln_b, w_in_b, sb_bias, kv_norm_g, w_kv,
        w_out, w_ffn_up, w_ffn_down)
    n_pages = page_table.shape[1]
    past_k = cache_k[page_table].reshape(DEC_BATCH, n_pages * PAGE_SIZE, SB_HEADS, SB_HEAD_DIM)
    past_v = cache_v[page_table].reshape(DEC_BATCH, n_pages * PAGE_SIZE, SB_HEADS, SB_HEAD_DIM)
    y_sample, conv_sample, k_sample, v_sample = run_trunk(
        x_sample, cache_mem_k, cache_mem_v, state_conv, past_k, past_v,
        norm_mix_pre, norm_mix_post, norm_ffn_pre, norm_ffn_post,
        w_in_a, conv_w, conv_b, conv_ln_g, conv_ln_b, w_in_b, sb_bias, kv_norm_g, w_kv,
        w_out, w_ffn_up, w_ffn_down)
    return (y_prompt, y_sample, k_prompt, v_prompt, conv_prompt, mk_p, mv_p, k_sample, v_sample, conv_sample)
```

```python
import contextlib
import math
import sys
import numpy as np
import concourse.bass as bass
import concourse.mybir as mybir
from concourse.bass_utils import run_bass_kernel_spmd

F32 = mybir.dt.float32
F32R = mybir.dt.float32r
BF16 = mybir.dt.bfloat16
I32 = mybir.dt.int32
AF = mybir.ActivationFunctionType
ALU = mybir.AluOpType
AX = mybir.AxisListType

ENGS = ("pe", "act", "dve", "pool", "sp")
DBG_SITES = None
SEM_EPOCH = 24000
N_DMA_SEMS = {"sp": 32, "pool": 16, "act": 4, "pe": 1, "dve": 1}


class Buf:
    __slots__ = ("name", "w", "r", "excl")

    def __init__(self, name, excl=False):
        self.name = name
        self.w = None
        self.r = []
        self.excl = excl


class Op:
    __slots__ = ("eng", "calls", "deps", "raw", "idx", "dma", "sig", "sem", "val", "site")


class _Rec:
    def __init__(self):
        self.calls = []

    def __getattr__(self, name):
        def f(*a, **k):
            self.calls.append((name, a, k))
            return None
        return f


class Sched:
    def __init__(self, nc):
        self.nc = nc
        self.q = {e: [] for e in ENGS}

    def op(self, eng, fn, r=(), w=(), dma=False):
        o = Op()
        rec = _Rec()
        fn(rec)
        o.site = sys._getframe(1).f_lineno
        o.eng, o.calls, o.dma = eng, rec.calls, dma
        o.sig = dma
        o.sem = o.val = o.raw = None
        xr = [b for b in r if b.excl and b not in w]
        if xr:
            r = [b for b in r if not b.excl]
            w = list(w) + xr
        deps, raw = set(), set()
        for b in r:
            if b.w is not None:
                deps.add(b.w)
                raw.add(b.w)
        for b in w:
            if b.w is not None:
                deps.add(b.w)
            deps.update(b.r)
        for b in r:
            b.r.append(o)
        for b in w:
            b.w = o
            b.r = []
        o.idx = len(self.q[eng])
        kept = []
        for d in deps:
            if d is o:
                continue
            if d.eng == eng and not d.dma:
                if eng != "pe":
                    kept.append(d)
            else:
                kept.append(d)
        for d in kept:
            d.sig = True
        o.deps = kept
        self.q[eng].append(o)
        return o

    def emit(self, stack):
        nc = self.nc
        for e in ENGS:
            cnt = 0
            sems = []
            for o in self.q[e]:
                if o.dma or not o.sig:
                    continue
                ep = cnt // SEM_EPOCH
                if ep >= len(sems):
                    sems.append(stack.enter_context(nc.semaphore(f"s_{e}{ep}")))
                o.sem = sems[ep]
                o.val = cnt - ep * SEM_EPOCH + 1
                cnt += 1
        final_waits = []
        for e in ENGS:
            n = sum(1 for o in self.q[e] if o.dma)
            if n == 0:
                continue
            ns = min(n, N_DMA_SEMS[e])
            dma_sems = [stack.enter_context(nc.semaphore(f"s_dma_{e}{i}")) for i in range(ns)]
            dma_tot = [0] * ns
            prev = {}
            rr = 0
            for o in self.q[e]:
                if o.dma:
                    i = rr % ns
                    rr += 1
                    o.sem = dma_sems[i]
                    dma_tot[i] += 16
                    o.val = dma_tot[i]
                    o.raw = prev.get(i)
                    prev[i] = o
            final_waits += [(dma_sems[i], dma_tot[i]) for i in range(ns)]

        def run(e, eng, last=False):
            waited = {}

            def wait(sem, val):
                k = id(sem)
                if waited.get(k, 0) >= val:
                    return
                waited[k] = val
                eng.wait_ge(sem, val)

            for o in self.q[e]:
                for d in o.deps:
                    wait(d.sem, d.val)
                if o.dma and o.raw is not None:
                    wait(o.raw.sem, o.raw.val)
                ins = None
                for name, a, k in o.calls:
                    ins = getattr(eng, name)(*a, **k)
                    if DBG_SITES is not None:
                        try:
                            DBG_SITES[ins.ins.name] = o.site
                        except Exception:
                            pass
                if o.sig:
                    ins.then_inc(o.sem, 16 if o.dma else 1)
            if last:
                for sem, val in final_waits:
                    wait(sem, val)

        with nc.Block() as block:
            @block.tensor
            def _(eng):
                run("pe", eng)

            @block.scalar
            def _(eng):
                run("act", eng)

            @block.vector
            def _(eng):
                run("dve", eng)

            @block.gpsimd
            def _(eng):
                run("pool", eng)

            @block.sync
            def _(eng):
                run("sp", eng, last=True)


D = 1024
KC = 8
T = 2048
TW = 512
NTILE = 4
C = 768
CC = 6
CW = 31
FF = 2816
FC = 22
NMEM = 256
L = 4
NA = 2
NH = 12
NS = 4
NPAGE = 64
PAGE = 128
NPHYS = 2560
RMS_EPS = 1e-6
LN_EPS = 1e-5
NR = 22
NP = 17
WBUF_COLS = 2048
NWBUF = 4
SAMPLE = True
DBG_NTILE = NTILE
DBG_L = L
DBG_TAP = None
DBG_STAGE = 9
DBG_CORES = 8
DBG_SUB = 9


class Slot:
    def __init__(self, t, i):
        self.t = t
        self.buf = Buf(f"slot{i}")

    def f(self, w=512):
        return self.t[:, 0:w]

    def r(self, w=512):
        return self.t[:, 0:w].bitcast(F32R)

    def b(self, w=512):
        return self.t[:].bitcast(BF16)[:, 0:w]

    def b2(self, w=512):
        return self.t[:].bitcast(BF16)[:, 512:512 + w]


class Bank:
    def __init__(self, t, i):
        self.t = t
        self.ap = t[:]
        self.buf = Buf(f"bank{i}", excl=True)


class WB:
    def __init__(self, t, i):
        self.t = t
        self.buf = Buf(f"wb{i}")


class Builder:
    def __init__(self, nc, st):
        self.nc = nc
        self.st = st
        self.S = Sched(nc)
        self.free = []
        self.freeR = []
        for i in range(NP):
            s = Slot(self.sb(f"sl{i}", [128, 512], F32), i)
            s.pool = self.free
            self.free.append(s)
        for i in range(NR):
            s = Slot(self.sb(f"sr{i}", [128, 512], F32), 100 + i)
            s.pool = self.freeR
            self.freeR.append(s)
        self.banks = [Bank(st.enter_context(nc.psum_tensor(f"ps{i}", [128, 512], F32)), i) for i in range(8)]
        self.rb = 0
        self.wbufs = [WB(self.sb(f"wb{i}", [128, WBUF_COLS], F32R), i) for i in range(NWBUF)]
        self.wbi = 0
        self.tog = 0

    def sb(self, name, shape, dt):
        return self.st.enter_context(self.nc.sbuf_tensor(name, shape, dt))

    def alloc(self, n=None):
        if n is None:
            return self.free.pop(0)
        return [self.free.pop(0) for _ in range(n)]

    def allocR(self, n=None):
        if n is None:
            return self.freeR.pop(0)
        return [self.freeR.pop(0) for _ in range(n)]

    def release(self, s):
        if isinstance(s, (list, tuple)):
            for x in s:
                x.pool.append(x)
        else:
            s.pool.append(s)

    def bank(self, lo=0, hi=6):
        b = self.banks[lo + self.rb % (hi - lo)]
        self.rb += 1
        return b

    def wbuf(self):
        w = self.wbufs[self.wbi % NWBUF]
        self.wbi += 1
        return w

    def ev(self):
        self.tog ^= 1
        return "act" if self.tog else "dve"


def _copy(eng_name, out, in_):
    if eng_name == "act":
        return lambda e: e.activation(out=out, in_=in_, func=AF.Copy)
    return lambda e: e.tensor_copy(out=out, in_=in_)


def build_program():
    nc = bass.Bass("TRN2", target_bir_lowering=False)
    nc.dge_precook = False
    dt_in = {}

    def din(name, shape, dt=F32):
        dt_in[name] = nc.dram_tensor(name, list(shape), dt, kind="ExternalInput").ap()
        return dt_in[name]

    def dout(name, shape):
        return nc.dram_tensor(name, list(shape), F32, kind="ExternalOutput").ap()

    xp = din("xp", [T, D])
    if SAMPLE:
        xs = din("xs", [NS, D])
        cache_k = din("cache_k", [NPHYS, PAGE * C])
        cache_v = din("cache_v", [NPHYS, PAGE * C])
        state_conv = din("state_conv", [NA, NS, CW - 1, C])
        cmk = din("cmk", [L, NS, NMEM, 256])
        cmv = din("cmv", [L, NS, NMEM, 256])
        ptab = din("ptab", [NS * NPAGE, 1], I32)
    memp = din("memp", [NMEM, D])
    g_mix_pre = din("norm_mix_pre", [L, D])
    g_mix_post = din("norm_mix_post", [L, D])
    g_ffn_pre = din("norm_ffn_pre", [L, D])
    g_ffn_post = din("norm_ffn_post", [L, D])
    w_in_a = din("w_in_a", [NA, D, 2 * C + 256], F32R)
    conv_w = din("conv_w", [NA, CW, C])
    conv_b = din("conv_b", [NA, C])
    conv_ln_g = din("conv_ln_g", [NA, C])
    conv_ln_b = din("conv_ln_b", [NA, C])
    w_in_b = din("w_in_b", [L - NA, D, D], F32R)
    sb_bias = din("sb_bias", [1, (L - NA) * NH])
    kv_norm_g = din("kv_norm_g", [1, D])
    w_kv = din("w_kv", [D, 2 * C], F32R)
    mem_norm_g = din("mem_norm_g", [L, D])
    w_mem_kv = din("w_mem_kv", [L, D, 512], F32R)
    w_out = din("w_out", [L, D, D], F32R)
    w_up = din("w_ffn_up", [L, D, 2 * FF], F32R)
    w_down = din("w_ffn_down", [L, FF, D], F32R)

    y_prompt = dout("y_prompt", [T, D])
    y_sample = dout("y_sample", [NS, D])
    k_prompt = dout("k_prompt", [T, C])
    v_prompt = dout("v_prompt", [T, C])
    conv_prompt = dout("conv_prompt", [NA, CW - 1, C])
    mk_p = dout("mk_p", [L, NMEM, 256])
    mv_p = dout("mv_p", [L, NMEM, 256])
    k_sample = dout("k_sample", [NS, C])
    v_sample = dout("v_sample", [NS, C])
    conv_sample = dout("conv_sample", [NA, NS, CW - 1, C])
    dbg = dout("dbg", [TW, D]) if DBG_TAP is not None else None

    with contextlib.ExitStack() as st:
        B = Builder(nc, st)
        S = B.S
        banks = B.banks

        ones_r = B.sb("ones_r", [128, 128], F32R)
        ones_f = B.sb("ones_f", [128, 128], F32)
        ident = B.sb("ident", [128, 128], F32)
        ident_b = B.sb("ident_b", [128, 128], BF16)
        u_incl = B.sb("u_incl", [128, 128], BF16)
        u_lt = B.sb("u_lt", [128, 128], BF16)
        masks = B.sb("masks", [128, 4, 512], BF16)
        bconst = Buf("const")
        S.op("pool", lambda e: e.memset(ones_f[:], 1.0), w=[bconst])
        S.op("dve", lambda e: e.tensor_copy(out=ones_r[:], in_=ones_f[:]), r=[bconst], w=[bconst])
        S.op("pool", lambda e: e.memset(ident[:], 0.0), w=[bconst])
        S.op("pool", lambda e: e.affine_select(out=ident[:], in_=ident[:], pattern=[[-1, 128]], compare_op=ALU.not_equal,
                                               fill=1.0, base=0, channel_multiplier=1), r=[bconst], w=[bconst])
        S.op("pool", lambda e: e.tensor_copy(out=ident_b[:], in_=ident[:]), r=[bconst], w=[bconst])
        S.op("pool", lambda e: e.memset(u_incl[:], 1.0), w=[bconst])
        S.op("pool", lambda e: e.affine_select(out=u_incl[:], in_=u_incl[:], pattern=[[-1, 128]], compare_op=ALU.is_ge,
                                               fill=0.0, base=0, channel_multiplier=1), r=[bconst], w=[bconst])
        S.op("pool", lambda e: e.memset(u_lt[:], 1.0), w=[bconst])
        S.op("pool", lambda e: e.affine_select(out=u_lt[:], in_=u_lt[:], pattern=[[1, 128]], compare_op=ALU.is_gt,
                                               fill=0.0, base=0, channel_multiplier=-1), r=[bconst], w=[bconst])
        S.op("pool", lambda e: e.memset(masks[:], 1.0), w=[bconst])
        for dd in range(4):
            S.op("pool", lambda e, dd=dd: e.affine_select(out=masks[:, dd, :], in_=masks[:, dd, :], pattern=[[1, 512]],
                                                          compare_op=ALU.is_gt, fill=0.0, base=-128 * dd,
                                                          channel_multiplier=-1), r=[bconst], w=[bconst])

        vec_srcs = [
            ("mix_pre", g_mix_pre.rearrange("l (kc p) -> (l kc) p", p=128)),
            ("mix_post", g_mix_post.rearrange("l (kc p) -> (l kc) p", p=128)),
            ("ffn_pre", g_ffn_pre.rearrange("l (kc p) -> (l kc) p", p=128)),
            ("ffn_post", g_ffn_post.rearrange("l (kc p) -> (l kc) p", p=128)),
            ("mem_g", mem_norm_g.rearrange("l (kc p) -> (l kc) p", p=128)),
            ("kv_g", kv_norm_g.rearrange("l (kc p) -> (l kc) p", p=128)),
            ("conv_b", conv_b.rearrange("l (cc p) -> (l cc) p", p=128)),
            ("ln_g", conv_ln_g.rearrange("l (cc p) -> (l cc) p", p=128)),
            ("ln_b", conv_ln_b.rearrange("l (cc p) -> (l cc) p", p=128)),
            ("conv_w", conv_w.rearrange("l k (cc p) -> (l k cc) p", p=128)),
        ]
        nvec = sum(a.shape[0] for _, a in vec_srcs)
        G = B.sb("G", [128, nvec], F32)
        bG = Buf("G")
        voff = {}
        vstage = B.sb("vstage", [128, 128], F32)
        bvs = Buf("vstage")
        off = 0
        for name, a in (vec_srcs if DBG_STAGE >= 1 else []):
            voff[name] = off
            nrows = a.shape[0]
            for r0 in range(0, nrows, 128):
                n = min(128, nrows - r0)
                bk = B.bank()
                S.op("sp", lambda e, a=a, r0=r0, n=n: e.dma_start(out=vstage[0:n, :], in_=a[r0:r0 + n, :]), w=[bvs], dma=True)
                S.op("pe", lambda e, bk=bk, n=n: e.transpose(out=bk.ap[:, 0:n], in_=vstage[0:n, :], identity=ident[0:n, 0:n]),
                     r=[bvs, bconst], w=[bk.buf])
                S.op("dve", lambda e, bk=bk, n=n, o=off + r0: e.tensor_copy(out=G[:, o:o + n], in_=bk.ap[:, 0:n]),
                     r=[bk.buf], w=[bG])
            off += nrows

        def gcol(name, idx):
            o = voff[name] + idx
            return G[:, o:o + 1]

        sbb_row = B.sb("sbb_row", [1, (L - NA) * NH], F32)
        sbb = B.sb("sbb", [128, (L - NA) * NH], F32)
        bsbb = Buf("sbb")
        if DBG_STAGE >= 2:
            S.op("sp", lambda e: e.dma_start(out=sbb_row[:], in_=sb_bias[:, :]), w=[bsbb], dma=True)
            bk = B.bank()
            S.op("pe", lambda e, bk=bk: e.matmul(bk.ap[:, 0:(L - NA) * NH], lhsT=ones_f[0:1, :], rhs=sbb_row[:], start=True, stop=True),
                 r=[bsbb, bconst], w=[bk.buf])
            S.op("dve", lambda e, bk=bk: e.tensor_copy(out=sbb[:], in_=bk.ap[:, 0:(L - NA) * NH]), r=[bk.buf], w=[bsbb])

        KT = B.sb("KT", [128, CC, T], BF16)
        VV = B.sb("VV", [128, T // 128, C], BF16)
        bKT = [Buf(f"KT{t}") for t in range(NTILE)]
        bVV = [Buf(f"VV{t}") for t in range(NTILE)]
        mkT = B.sb("mkT", [128, L, 2, NMEM], BF16)
        mvv = B.sb("mvv", [128, L, 2, 256], BF16)
        bmk = [Buf(f"mk{l}") for l in range(L)]
        bmv = [Buf(f"mv{l}") for l in range(L)]
        uext = B.sb("uext", [128, CC, 30 + TW], F32)
        buext = [Buf(f"uext{c}") for c in range(CC)]
        halo = B.sb("halo", [128, NA, CC, 30], F32)
        bhalo = [Buf(f"halo{l}") for l in range(NA)]
        small = B.sb("small", [128, 16], F32)
        bsmall = [Buf(f"small{i}") for i in range(4)]

        xres = []
        for kc in range(KC):
            sx = Slot(B.sb(f"xres{kc}", [128, 512], F32), 200 + kc)
            sx.pool = []
            xres.append(sx)

        def load_xT(rows_ap, ntok, w, xT=None):
            nsub = (w + 127) // 128
            stage = B.alloc(2 * nsub)
            for s in range(nsub):
                n = min(128, w - s * 128)
                for hf in range(2):
                    sl = stage[2 * s + hf]
                    S.op("sp", lambda e, sl=sl, s=s, n=n, hf=hf: e.dma_start(
                        out=sl.f()[0:n, :], in_=rows_ap[s * 128:s * 128 + n, hf * 512:(hf + 1) * 512]), w=[sl.buf], dma=True)
            if xT is None:
                xT = B.alloc(KC)
            for kc in range(KC):
                bk = B.bank()

                def tr(e, kc=kc, bk=bk):
                    ins = None
                    for s in range(nsub):
                        n = min(128, w - s * 128)
                        sl = stage[2 * s + kc // 4]
                        ins = e.transpose(out=bk.ap[:, s * 128:s * 128 + n], in_=sl.f()[0:n, (kc % 4) * 128:(kc % 4) * 128 + 128],
                                          identity=ident[0:n, 0:n])
                    return ins
                S.op("pe", tr, r=[stage[2 * s + kc // 4].buf for s in range(nsub)] + [bconst], w=[bk.buf])
                en = B.ev()
                S.op(en, _copy(en, xT[kc].f(w), bk.ap[:, 0:w]), r=[bk.buf], w=[xT[kc].buf])
            B.release(stage)
            return xT

        def store_T(src_slots, w, dst_rows_ap, ncols):
            nsub = (w + 127) // 128
            nch = ncols // 128
            for s in range(nsub):
                n = min(128, w - s * 128)
                for g0 in range(0, nch, 4):
                    ng = min(4, nch - g0)
                    bk = B.bank()

                    def tr(e, s=s, n=n, g0=g0, ng=ng, bk=bk):
                        ins = None
                        for j in range(ng):
                            ins = e.transpose(out=bk.ap[0:n, j * 128:(j + 1) * 128],
                                              in_=src_slots[g0 + j].f(w)[:, s * 128:s * 128 + n], identity=ident[:])
                        return ins
                    S.op("pe", tr, r=[src_slots[g0 + j].buf for j in range(ng)] + [bconst], w=[bk.buf])
                    stg = B.alloc()
                    en = B.ev()
                    S.op(en, _copy(en, stg.f()[0:n, 0:ng * 128], bk.ap[0:n, 0:ng * 128]), r=[bk.buf], w=[stg.buf])
                    S.op("sp", lambda e, stg=stg, s=s, n=n, g0=g0, ng=ng: e.dma_start(
                        out=dst_rows_ap[s * 128:s * 128 + n, g0 * 128:(g0 + ng) * 128], in_=stg.f()[0:n, 0:ng * 128]),
                        r=[stg.buf], dma=True)
                    B.release(stg)

        def sumsq_bank(srcs, w, acc):
            n = len(srcs)
            tmp = B.allocR(2)
            for i, (ap, buf) in enumerate(srcs):
                t = tmp[i % 2]
                S.op("act", lambda e, t=t, ap=ap: e.activation(out=t.r(w), in_=ap, func=AF.Square), r=[buf], w=[t.buf])
                S.op("pe", lambda e, t=t, i=i: e.matmul(acc.ap[:, 0:w], lhsT=ones_r[:], rhs=t.r(w), start=(i == 0), stop=(i == n - 1)),
                     r=[t.buf, bconst], w=[acc.buf])
            B.release(tmp)

        def rstd_of(acc, w, dim, eps):
            r = B.alloc()
            S.op("act", lambda e: e.activation(out=r.f(w), in_=acc.ap[:, 0:w], func=AF.Sqrt, scale=1.0 / dim, bias=eps),
                 r=[acc.buf], w=[r.buf])
            S.op("dve", lambda e: e.reciprocal(out=r.f(w), in_=r.f(w)), r=[r.buf], w=[r.buf])
            return r

        def rmsnorm(xT, w, gname, gidx0, rstd=None):
            if rstd is None:
                acc = banks[6]
                sumsq_bank([(s.f(w), s.buf) for s in xT], w, acc)
                rstd = rstd_of(acc, w, D, RMS_EPS)
            h = B.allocR(len(xT))
            for kc in range(len(xT)):
                en = "dve"
                S.op(en, lambda e, kc=kc: e.scalar_tensor_tensor(out=h[kc].r(w), in0=xT[kc].f(w), scalar=gcol(gname, gidx0 + kc),
                                                                  in1=rstd.f(w), op0=ALU.mult, op1=ALU.mult),
                     r=[xT[kc].buf, rstd.buf, bG], w=[h[kc].buf])
            return h, rstd

        def linear(xin, w, Wd, col0, noc, consume):
            KCn = len(xin)
            grp = max(1, min(4, WBUF_COLS // (KCn * 128)))
            for g0 in range(0, noc, grp):
                ng = min(grp, noc - g0)
                wb = B.wbuf()
                ncols = ng * 128
                src = Wd[:, col0 + g0 * 128: col0 + g0 * 128 + ncols].rearrange("(kc p) n -> p kc n", p=128)
                dst = wb.t[:, 0:KCn * ncols].rearrange("p (kc n) -> p kc n", kc=KCn)
                S.op("sp", lambda e, dst=dst, src=src: e.dma_start(out=dst, in_=src), w=[wb.buf], dma=True)
                for j in range(ng):
                    bk = B.bank()

                    def mm(e, dst=dst, j=j, bk=bk):
                        ins = None
                        for kc in range(KCn):
                            ins = e.matmul(bk.ap[:, 0:w], lhsT=dst[:, kc, j * 128:(j + 1) * 128], rhs=xin[kc].r(w),
                                           start=(kc == 0), stop=(kc == KCn - 1))
                        return ins
                    S.op("pe", mm, r=[wb.buf] + [s.buf for s in xin], w=[bk.buf])
                    consume(g0 + j, bk)

        def linear_tok(xin, w, Wd, col0, ncols, consume):
            KCn = len(xin)
            wb = B.wbuf()
            src = Wd[:, col0:col0 + ncols].rearrange("(kc p) n -> p kc n", p=128)
            dst = wb.t[:, 0:KCn * ncols].rearrange("p (kc n) -> p kc n", kc=KCn)
            S.op("sp", lambda e: e.dma_start(out=dst, in_=src), w=[wb.buf], dma=True)
            nsub = (w + 127) // 128
            for s in range(nsub):
                n = min(128, w - s * 128)
                bk = B.bank()

                def mm(e, s=s, n=n, bk=bk):
                    ins = None
                    for kc in range(KCn):
                        ins = e.matmul(bk.ap[0:n, 0:ncols], lhsT=xin[kc].r(w)[:, s * 128:s * 128 + n], rhs=dst[:, kc, :],
                                       start=(kc == 0), stop=(kc == KCn - 1))
                    return ins
                S.op("pe", mm, r=[wb.buf] + [x.buf for x in xin], w=[bk.buf])
                consume(s, n, bk)

        def post_norm_add(xT, w, r, gname, gidx0):
            acc = banks[7]
            sumsq_bank([(s_.f(w), s_.buf) for s_ in r], w, acc)
            rstd = rstd_of(acc, w, D, RMS_EPS)
            for kc in range(KC):
                S.op("dve", lambda e, kc=kc: e.scalar_tensor_tensor(out=r[kc].f(w), in0=r[kc].f(w), scalar=gcol(gname, gidx0 + kc),
                                                                     in1=rstd.f(w), op0=ALU.mult, op1=ALU.mult),
                     r=[r[kc].buf, rstd.buf, bG], w=[r[kc].buf])
                S.op("pool", lambda e, kc=kc: e.tensor_tensor(out=xT[kc].f(w), in0=xT[kc].f(w), in1=r[kc].f(w), op=ALU.add),
                     r=[xT[kc].buf, r[kc].buf], w=[xT[kc].buf])
            B.release(rstd)

        def proj_post_add(xT, w, mixin, Wd, gname, gidx0):
            r = B.alloc(KC)

            def cons(oc, bk):
                en = B.ev()
                S.op(en, _copy(en, r[oc].f(w), bk.ap[:, 0:w]), r=[bk.buf], w=[r[oc].buf])
            linear(mixin, w, Wd, 0, KC, cons)
            post_norm_add(xT, w, r, gname, gidx0)
            B.release(r)

        def ffn(xT, w, l):
            h, rstd = rmsnorm(xT, w, "ffn_pre", l * KC)
            B.release(rstd)
            r = B.alloc(KC)
            HF = FC // 2
            for half in range(2):
                act = B.allocR(HF)
                pend = {}

                def cons(oc, bk, act=act, pend=pend):
                    if oc < HF:
                        t = B.alloc()
                        S.op("act", lambda e: e.activation(out=t.f(w), in_=bk.ap[:, 0:w], func=AF.Silu), r=[bk.buf], w=[t.buf])
                        pend[oc] = t
                    else:
                        j = oc - HF
                        t = pend.pop(j)
                        S.op("dve", lambda e: e.tensor_tensor(out=act[j].r(w), in0=t.f(w), in1=bk.ap[:, 0:w], op=ALU.mult),
                             r=[t.buf, bk.buf], w=[act[j].buf])
                        B.release(t)
                c0 = half * HF * 128
                for g0 in range(0, HF, 4):
                    ng = min(4, HF - g0)
                    linear(h, w, w_up[l], c0 + g0 * 128, ng, lambda oc, bk, g0=g0, cons=cons: cons(g0 + oc, bk))
                    linear(h, w, w_up[l], FF + c0 + g0 * 128, ng, lambda oc, bk, g0=g0, cons=cons: cons(HF + g0 + oc, bk))

                def consd(oc, bk, half=half):
                    if half == 0:
                        en = B.ev()
                        S.op(en, _copy(en, r[oc].f(w), bk.ap[:, 0:w]), r=[bk.buf], w=[r[oc].buf])
                    else:
                        S.op("dve", lambda e: e.tensor_tensor(out=r[oc].f(w), in0=r[oc].f(w), in1=bk.ap[:, 0:w], op=ALU.add),
                             r=[r[oc].buf, bk.buf], w=[r[oc].buf])
                linear(act, w, w_down[l][c0:c0 + HF * 128, :], 0, KC, consd)
                B.release(act)
            B.release(h)
            post_norm_add(xT, w, r, "ffn_post", l * KC)
            B.release(r)

        def mem_attention(qm, w, mk_ap, mv_ap, bufs_mkv, out_slots, cols=None):
            if cols is None:
                cols = [(s_ * 128, min(128, w - s_ * 128)) for s_ in range((w + 127) // 128)]
            for (q0, n) in cols:
                obk = [banks[6], banks[7]]
                for hh in range(4):
                    c, po = hh // 2, (hh % 2) * 64
                    sbk = B.bank()
                    S.op("pe", lambda e, c=c, po=po, sbk=sbk, q0=q0, n=n: e.matmul(
                        sbk.ap[0:n, 0:NMEM], lhsT=qm[c].b(w)[po:po + 64, q0:q0 + n], rhs=mk_ap(c)[po:po + 64, :],
                        start=True, stop=True), r=[qm[c].buf] + bufs_mkv, w=[sbk.buf])
                    sm = bsmall[hh]
                    c0 = hh * 4
                    S.op("dve", lambda e, sbk=sbk, n=n, c0=c0: e.reduce_max(out=small[0:n, c0:c0 + 1], in_=sbk.ap[0:n, 0:NMEM], axis=AX.X),
                         r=[sbk.buf], w=[sm])
                    S.op("dve", lambda e, n=n, c0=c0: e.tensor_scalar(out=small[0:n, c0 + 1:c0 + 2], in0=small[0:n, c0:c0 + 1],
                                                                      scalar1=-0.125, scalar2=None, op0=ALU.mult),
                         r=[sm], w=[sm])
                    p = B.alloc()
                    S.op("act", lambda e, sbk=sbk, n=n, c0=c0, p=p: e.activation(
                        out=p.f()[0:n, 0:NMEM], in_=sbk.ap[0:n, 0:NMEM], func=AF.Exp, bias=small[0:n, c0 + 1:c0 + 2], scale=0.125,
                        accum_out=small[0:n, c0 + 2:c0 + 3]), r=[sbk.buf, sm], w=[p.buf, sm])
                    S.op("dve", lambda e, n=n, c0=c0: e.reciprocal(out=small[0:n, c0 + 3:c0 + 4], in_=small[0:n, c0 + 2:c0 + 3]),
                         r=[sm], w=[sm])
                    pn = B.alloc()
                    S.op("dve", lambda e, n=n, c0=c0, p=p, pn=pn: e.tensor_scalar(
                        out=pn.b()[0:n, 0:NMEM], in0=p.f()[0:n, 0:NMEM], scalar1=small[0:n, c0 + 3:c0 + 4], scalar2=None, op0=ALU.mult),
                        r=[p.buf, sm], w=[pn.buf])
                    B.release(p)
                    tbk = B.bank()
                    tb = tbk.ap.bitcast(BF16)

                    def tr(e, pn=pn, tb=tb, n=n):
                        e.transpose(out=tb[:, 0:n], in_=pn.b()[0:n, 0:128], identity=ident_b[0:n, 0:n])
                        return e.transpose(out=tb[:, 128:128 + n], in_=pn.b()[0:n, 128:256], identity=ident_b[0:n, 0:n])
                    S.op("pe", tr, r=[pn.buf, bconst], w=[tbk.buf])
                    pT = B.alloc()
                    en = B.ev()
                    S.op(en, _copy(en, pT.b()[:, 0:256].rearrange("p (a b) -> p a b", a=2)[:, :, 0:n],
                                   tb[:, 0:256].rearrange("p (a b) -> p a b", a=2)[:, :, 0:n]), r=[tbk.buf], w=[pT.buf])
                    B.release(pn)

                    def pv(e, pT=pT, c=c, po=po, hh=hh, n=n, obk=obk):
                        e.matmul(obk[c].ap[po:po + 64, 0:n], lhsT=mv_ap(0)[:, hh * 64:(hh + 1) * 64], rhs=pT.b()[:, 0:n],
                                 start=True, stop=False)
                        return e.matmul(obk[c].ap[po:po + 64, 0:n], lhsT=mv_ap(1)[:, hh * 64:(hh + 1) * 64], rhs=pT.b()[:, 128:128 + n],
                                        start=False, stop=True)
                    S.op("pe", pv, r=[pT.buf] + bufs_mkv, w=[obk[c].buf])
                    B.release(pT)
                for c in range(2):
                    en = B.ev()
                    S.op(en, _copy(en, out_slots[c].r(w)[:, q0:q0 + n], obk[c].ap[:, 0:n]), r=[obk[c].buf], w=[out_slots[c].buf])

        def conv_ln(yr, yc, w, l, mix):
            acc1, acc2 = banks[6], banks[7]
            for c in range(CC):
                S.op("pe", lambda e, c=c: e.matmul(acc1.ap[:, 0:w], lhsT=ones_r[:], rhs=yr[c].r(w), start=(c == 0), stop=(c == CC - 1)),
                     r=[yr[c].buf, bconst], w=[acc1.buf])
            sumsq_bank([(s_.f(w), s_.buf) for s_ in yr], w, acc2)
            mean = B.alloc()
            msq = B.alloc()
            var = B.alloc()
            S.op("act", lambda e: e.activation(out=mean.f(w), in_=acc1.ap[:, 0:w], func=AF.Copy, scale=1.0 / C), r=[acc1.buf], w=[mean.buf])
            S.op("pool", lambda e: e.tensor_tensor(out=msq.f(w), in0=mean.f(w), in1=mean.f(w), op=ALU.mult), r=[mean.buf], w=[msq.buf])
            S.op("dve", lambda e: e.scalar_tensor_tensor(out=var.f(w), in0=acc2.ap[:, 0:w], scalar=1.0 / C, in1=msq.f(w),
                                                         op0=ALU.mult, op1=ALU.subtract), r=[acc2.buf, msq.buf], w=[var.buf])
            S.op("act", lambda e: e.activation(out=var.f(w), in_=var.f(w), func=AF.Sqrt, scale=1.0, bias=LN_EPS), r=[var.buf], w=[var.buf])
            S.op("dve", lambda e: e.reciprocal(out=var.f(w), in_=var.f(w)), r=[var.buf], w=[var.buf])
            for c in range(CC):
                en = "pool"
                S.op(en, lambda e, c=c: e.tensor_tensor(out=yc[c].f(w), in0=yr[c].f(w), in1=mean.f(w), op=ALU.subtract),
                     r=[yr[c].buf, mean.buf], w=[yc[c].buf])
                en2 = "dve"
                S.op(en2, lambda e, c=c, l=l: e.scalar_tensor_tensor(out=yc[c].f(w), in0=yc[c].f(w), scalar=gcol("ln_g", l * CC + c),
                                                                     in1=var.f(w), op0=ALU.mult, op1=ALU.mult),
                     r=[yc[c].buf, var.buf, bG], w=[yc[c].buf])
                S.op("act", lambda e, c=c, l=l: e.activation(out=mix[c].r(w), in_=yc[c].f(w), func=AF.Silu,
                                                             bias=gcol("ln_b", l * CC + c), scale=1.0),
                     r=[yc[c].buf, bG], w=[mix[c].buf])
            B.release([mean, msq, var])

        if DBG_STAGE >= 3:
            memT = load_xT(memp, NMEM, NMEM)
        if DBG_STAGE >= 4:
            accm = banks[6]
            sumsq_bank([(s.f(NMEM), s.buf) for s in memT], NMEM, accm)
            rstd_m = rstd_of(accm, NMEM, D, RMS_EPS)
        for l in range((L if DBG_STAGE >= 7 else 1) if DBG_STAGE >= 5 else 0):
            mn, _ = rmsnorm(memT, NMEM, "mem_g", l * KC, rstd=rstd_m)

            def cons_mk(oc, bk, l=l):
                en = B.ev()
                S.op(en, _copy(en, mkT[:, l, oc, :], bk.ap[:, 0:NMEM]), r=[bk.buf], w=[bmk[l]])
            linear(mn, NMEM, w_mem_kv[l], 0, 2, cons_mk)

            for grp in range((2 if DBG_SUB >= 2 else 1) if DBG_STAGE >= 6 else 0):
                def cons_tok(s, n, bk, l=l, grp=grp):
                    stg = B.alloc()
                    S.op("act", lambda e: e.activation(out=stg.f(256), in_=bk.ap[:, 0:256], func=AF.Copy), r=[bk.buf], w=[stg.buf])
                    dstd = mk_p if grp == 0 else mv_p
                    if DBG_SUB >= 1:
                        S.op("sp", lambda e: e.dma_start(out=dstd[l, s * 128:(s + 1) * 128, :], in_=stg.f(256)), r=[stg.buf], dma=True)
                    if grp == 1:
                        S.op("dve", lambda e: e.tensor_copy(out=mvv[:, l, s, :], in_=stg.f(256)), r=[stg.buf], w=[bmv[l]])
                    B.release(stg)
                linear_tok(mn, NMEM, w_mem_kv[l], grp * 256, 256, cons_tok)
            B.release(mn)
        if DBG_STAGE >= 5:
            B.release(rstd_m)
            B.release(memT)

        for ti in range(DBG_NTILE):
            t0 = ti * TW
            w = TW
            xT = xres
            load_xT(xp[t0:t0 + TW, :], TW, TW, xT=xres)
            for l in range(DBG_L):
                h, rstd = rmsnorm(xT, w, "mix_pre", l * KC)
                B.release(rstd)
                mix = B.allocR(KC)
                qm = B.alloc(2)
                if l < NA:
                    sig = B.alloc(CC)

                    def cons_gate(oc, bk, sig=sig):
                        S.op("act", lambda e: e.activation(out=sig[oc].f(w), in_=bk.ap[:, 0:w], func=AF.Sigmoid), r=[bk.buf], w=[sig[oc].buf])
                    linear(h, w, w_in_a[l], C, CC, cons_gate)
                    if ti == 0:
                        S.op("pool", lambda e: e.memset(uext[:, :, 0:30], 0.0), w=buext)
                    else:
                        S.op("pool", lambda e, l=l: e.tensor_copy(out=uext[:, :, 0:30], in_=halo[:, l, :, :]), r=[bhalo[l]], w=buext)

                    def cons_a(oc, bk, sig=sig):
                        S.op("dve", lambda e: e.tensor_tensor(out=uext[:, oc, 30:30 + w], in0=sig[oc].f(w), in1=bk.ap[:, 0:w], op=ALU.mult),
                             r=[sig[oc].buf, bk.buf], w=[buext[oc]])
                    linear(h, w, w_in_a[l], 0, CC, cons_a)
                    B.release(sig)

                    def cons_qm(oc, bk, qm=qm):
                        en = B.ev()
                        S.op(en, _copy(en, qm[oc].b(w), bk.ap[:, 0:w]), r=[bk.buf], w=[qm[oc].buf])
                    linear(h, w, w_in_a[l], 2 * C, 2, cons_qm)
                    B.release(h)
                    S.op("pool", lambda e, l=l: e.tensor_copy(out=halo[:, l, :, :], in_=uext[:, :, w:w + 30]), r=buext, w=[bhalo[l]])
                    if ti == NTILE - 1:
                        bk = B.bank()

                        bk2 = B.bank()

                        def trh1(e, l=l, bk=bk):
                            ins = None
                            for c in range(4):
                                ins = e.transpose(out=bk.ap[0:30, c * 128:(c + 1) * 128], in_=halo[:, l, c, :], identity=ident[:])
                            return ins

                        def trh2(e, l=l, bk2=bk2):
                            ins = None
                            for c in range(4, 6):
                                ins = e.transpose(out=bk2.ap[0:30, (c - 4) * 128:(c - 3) * 128], in_=halo[:, l, c, :], identity=ident[:])
                            return ins
                        S.op("pe", trh1, r=[bhalo[l], bconst], w=[bk.buf])
                        S.op("pe", trh2, r=[bhalo[l], bconst], w=[bk2.buf])
                        stg = B.alloc(2)
                        S.op("dve", lambda e, bk=bk, stg=stg: e.tensor_copy(out=stg[0].f()[0:30, :], in_=bk.ap[0:30, :]), r=[bk.buf], w=[stg[0].buf])
                        S.op("dve", lambda e, bk2=bk2, stg=stg: e.tensor_copy(out=stg[1].f()[0:30, 0:256], in_=bk2.ap[0:30, 0:256]), r=[bk2.buf], w=[stg[1].buf])
                        S.op("sp", lambda e, l=l, stg=stg: e.dma_start(out=conv_prompt[l, :, 0:512], in_=stg[0].f()[0:30, :]), r=[stg[0].buf], dma=True)
                        S.op("sp", lambda e, l=l, stg=stg: e.dma_start(out=conv_prompt[l, :, 512:768], in_=stg[1].f()[0:30, 0:256]), r=[stg[1].buf], dma=True)
                        B.release(stg)
                    yc = B.alloc(CC)
                    yr = B.allocR(CC)
                    for k in range(CW):
                        for c in range(CC):
                            en = "dve"
                            wc = gcol("conv_w", (l * CW + k) * CC + c)
                            if k == 0:
                                S.op(en, lambda e, c=c, wc=wc, l=l: e.tensor_scalar(out=yc[c].f(w), in0=uext[:, c, 0:w], scalar1=wc,
                                                                                    scalar2=gcol("conv_b", l * CC + c), op0=ALU.mult, op1=ALU.add),
                                     r=[buext[c], bG], w=[yc[c].buf])
                            else:
                                outv = yr[c].r(w) if k == CW - 1 else yc[c].f(w)
                                S.op(en, lambda e, c=c, wc=wc, k=k, outv=outv: e.scalar_tensor_tensor(
                                    out=outv, in0=uext[:, c, k:k + w], scalar=wc, in1=yc[c].f(w), op0=ALU.mult, op1=ALU.add),
                                    r=[buext[c], yc[c].buf, bG], w=[yr[c].buf if k == CW - 1 else yc[c].buf])
                    conv_ln(yr, yc, w, l, mix)
                    B.release(yc)
                    B.release(yr)
                else:
                    lb = l - NA
                    qT = B.alloc(CC)

                    def cons_q(oc, bk, qT=qT, qm=qm):
                        dst = qT[oc] if oc < CC else qm[oc - CC]
                        en = B.ev()
                        S.op(en, _copy(en, dst.b(w), bk.ap[:, 0:w]), r=[bk.buf], w=[dst.buf])
                    linear(h, w, w_in_b[lb], 0, CC + 2, cons_q)
                    B.release(h)
                    kb_hi = (t0 + w) // 128 - 1
                    kvbufs = [bKT[t] for t in range(ti + 1)] + [bVV[t] for t in range(ti + 1)]

                    def head_stream(hd, ibk, obk, zbanks):
                        c, po = hd // 2, (hd % 2) * 64
                        bias = sbb[:, lb * NH + hd: lb * NH + hd + 1]
                        first = True
                        for kb in range(kb_hi, -1, -1):
                            diag = (kb + 1) * 128 > t0
                            dd = kb - t0 // 128
                            zbk = zbanks
                            S.op("pe", lambda e, kb=kb, zbk=zbk: e.matmul(
                                zbk.ap[:, 0:w], lhsT=KT[po:po + 64, c, kb * 128:(kb + 1) * 128], rhs=qT[c].b(w)[po:po + 64, :],
                                start=True, stop=True), r=[qT[c].buf] + kvbufs, w=[zbk.buf])
                            s1 = B.alloc()
                            s2 = B.alloc()
                            ee, wt, sp, ex = s1.b(w), s1.b2(w), s2.b(w), s2.b2(w)
                            S.op("act", lambda e, zbk=zbk, ee=ee: e.activation(out=ee, in_=zbk.ap[:, 0:w], func=AF.Exp, bias=bias, scale=0.125),
                                 r=[zbk.buf, bsbb], w=[s1.buf])
                            if diag:
                                S.op("dve", lambda e, ee=ee, dd=dd: e.tensor_tensor(out=ee, in0=ee, in1=masks[:, dd, 0:w], op=ALU.mult),
                                     r=[s1.buf, bconst], w=[s1.buf])
                            S.op("act", lambda e, ee=ee, sp=sp: e.activation(out=sp, in_=ee, func=AF.Ln, bias=1.0, scale=1.0),
                                 r=[s1.buf], w=[s2.buf])
                            yield
                            S.op("pe", lambda e, sp=sp, first=first: e.matmul(ibk.ap[:, 0:w], lhsT=u_incl[:], rhs=sp, start=first, stop=False,
                                                                             skip_group_check=True),
                                 r=[s2.buf, bconst], w=[ibk.buf])
                            yield
                            S.op("act", lambda e, ex=ex: e.activation(out=ex, in_=ibk.ap[:, 0:w], func=AF.Exp, scale=-1.0),
                                 r=[ibk.buf], w=[s2.buf])
                            S.op("dve", lambda e, ee=ee, ex=ex, wt=wt: e.tensor_tensor(out=wt, in0=ee, in1=ex, op=ALU.mult),
                                 r=[s1.buf, s2.buf], w=[s1.buf])
                            yield
                            if kb > 0:
                                S.op("pe", lambda e, sp=sp: e.matmul(ibk.ap[:, 0:w], lhsT=u_lt[:], rhs=sp, start=False, stop=False,
                                                                     skip_group_check=True),
                                     r=[s2.buf, bconst], w=[ibk.buf])
                            S.op("pe", lambda e, wt=wt, kb=kb, first=first: e.matmul(
                                obk.ap[po:po + 64, 0:w], lhsT=VV[:, kb, hd * 64:(hd + 1) * 64], rhs=wt, start=first, stop=(kb == 0),
                                skip_group_check=True), r=[s1.buf] + kvbufs, w=[obk.buf])
                            B.release([s1, s2])
                            first = False
                            yield

                    for rnd in range(3):
                        heads = [rnd * 4 + i for i in range(4)]
                        obks = [banks[4], banks[5]]
                        zb = [banks[6], banks[7]]
                        gens = [head_stream(hd, banks[i], obks[i // 2], zb[i % 2]) for i, hd in enumerate(heads)]
                        alive = list(gens)
                        while alive:
                            nxt = []
                            for g in alive:
                                try:
                                    next(g)
                                    nxt.append(g)
                                except StopIteration:
                                    pass
                            alive = nxt
                        for i in range(2):
                            cch = rnd * 2 + i
                            en = B.ev()
                            S.op(en, _copy(en, mix[cch].r(w), obks[i].ap[:, 0:w]), r=[obks[i].buf], w=[mix[cch].buf])
                    B.release(qT)
                mem_attention(qm, w, lambda c, l=l: mkT[:, l, c, :], lambda mh, l=l: mvv[:, l, mh, :], [bmk[l], bmv[l]], mix[CC:CC + 2])
                B.release(qm)
                if DBG_TAP is not None and DBG_TAP == (ti, l):
                    store_T(mix, w, dbg, D)
                proj_post_add(xT, w, mix, w_out[l], "mix_post", l * KC)
                B.release(mix)
                ffn(xT, w, l)
                if l == NA - 1:
                    hk, rstd = rmsnorm(xT, w, "kv_g", 0)
                    B.release(rstd)

                    def cons_kT(oc, bk, ti=ti, t0=t0):
                        en = B.ev()
                        S.op(en, _copy(en, KT[:, oc, t0:t0 + w], bk.ap[:, 0:w]), r=[bk.buf], w=[bKT[ti]])
                    linear(hk, w, w_kv, 0, CC, cons_kT)
                    for grp in range(6):
                        def cons_kv(s, n, bk, grp=grp, ti=ti, t0=t0):
                            stg = B.alloc()
                            S.op("act", lambda e: e.activation(out=stg.f(256), in_=bk.ap[:, 0:256], func=AF.Copy), r=[bk.buf], w=[stg.buf])
                            r0 = t0 + s * 128
                            dstd = k_prompt if grp < 3 else v_prompt
                            c0 = (grp % 3) * 256
                            S.op("sp", lambda e: e.dma_start(out=dstd[r0:r0 + 128, c0:c0 + 256], in_=stg.f(256)), r=[stg.buf], dma=True)
                            if grp >= 3:
                                S.op("dve", lambda e: e.tensor_copy(out=VV[:, ti * 4 + s, c0:c0 + 256], in_=stg.f(256)), r=[stg.buf], w=[bVV[ti]])
                            B.release(stg)
                        linear_tok(hk, w, w_kv, grp * 256, 256, cons_kv)
                    B.release(hk)
            store_T(xT, w, y_prompt[t0:t0 + w, :], D)


        if SAMPLE:
            w = NS
            xsT = []
            for kc in range(KC):
                sx = Slot(B.sb(f"xsT{kc}", [128, 16], F32), 300 + kc)
                sx.pool = []
                xsT.append(sx)
            idx = B.sb("pidx", [128, 2], I32)
            bidx = Buf("pidx")
            for g in range(2):
                S.op("sp", lambda e, g=g: e.dma_start(out=idx[:, g:g + 1], in_=ptab[g * 128:(g + 1) * 128, :]), w=[bidx], dma=True)
            idxf = B.sb("pidxf", [128, 2], F32)
            iof = B.sb("iotaf", [128, PAGE // 2], F32)
            idxallf = B.sb("idxallf", [128, 2, PAGE // 2], F32)
            idxall = B.sb("idxall", [128, 2, PAGE // 2], I32)
            S.op("pool", lambda e: e.iota(out=iof[:], pattern=[[1, PAGE // 2]], base=0, channel_multiplier=0,
                                          allow_small_or_imprecise_dtypes=True), w=[bconst])
            S.op("dve", lambda e: e.tensor_copy(out=idxf[:], in_=idx[:]), r=[bidx], w=[bidx])
            S.op("dve", lambda e: e.tensor_scalar(out=idxf[:], in0=idxf[:], scalar1=float(PAGE // 2), scalar2=None, op0=ALU.mult), r=[bidx], w=[bidx])
            for g in range(2):
                S.op("dve", lambda e, g=g: e.tensor_scalar(out=idxallf[:, g, :], in0=iof[:], scalar1=idxf[:, g:g + 1], scalar2=None, op0=ALU.add),
                     r=[bidx, bconst], w=[bidx])
            S.op("dve", lambda e: e.tensor_copy(out=idxall[:], in_=idxallf[:]), r=[bidx], w=[bidx])
            ck2 = cache_k.rearrange("n (ch f) -> (n ch) f", f=2 * C)
            cv2 = cache_v.rearrange("n (ch f) -> (n ch) f", f=2 * C)
            tri = B.sb("tri", [128, 128], F32)
            sel = B.sb("sel", [128, 2, NS], F32)
            negc = B.sb("negc", [128, 16], F32)
            bnegc = Buf("negc")
            S.op("pool", lambda e: e.memset(tri[:], 1.0), w=[bconst])
            S.op("pool", lambda e: e.affine_select(out=tri[:], in_=tri[:], pattern=[[-1, 128]], compare_op=ALU.is_gt, fill=0.0,
                                                   base=0, channel_multiplier=1), r=[bconst], w=[bconst])
            S.op("pool", lambda e: e.memset(tri[64:128, 0:64], 0.0), r=[bconst], w=[bconst])
            S.op("pool", lambda e: e.memset(sel[:], 0.0), r=[bconst], w=[bconst])
            for g in range(2):
                S.op("pool", lambda e, g=g: e.memset(sel[0:64, g, 2 * g:2 * g + 1], 1.0), r=[bconst], w=[bconst])
                S.op("pool", lambda e, g=g: e.memset(sel[64:128, g, 2 * g + 1:2 * g + 2], 1.0), r=[bconst], w=[bconst])
            KTf = KT[:].rearrange("p c t -> p (c t)").bitcast(F32)
            VVf = VV[:].rearrange("p t c -> p (t c)").bitcast(F32)
            Kb = [KTf[:, 0:1536], KTf[:, 1536:3072]]
            EE = KTf[:, 3072:4608]
            SX = KTf[:, 4608:6144]
            Vb = [VVf[:, 0:1536], VVf[:, 1536:3072]]
            ring = Kb + Vb
            uxf = uext[:].rearrange("p c t -> p (c t)").bitcast(BF16)
            vbb = [uxf[:, i * 1536:(i + 1) * 1536] for i in range(4)]
            bvbb = [Buf(f"vbb{i}") for i in range(4)]
            sel_b = B.sb("sel_b", [128, 2, NS], BF16)
            S.op("pool", lambda e: e.tensor_copy(out=sel_b[:], in_=sel[:]), r=[bconst], w=[bconst])
            S2 = VVf[:, 3072:4608]
            QB = VVf[:, 4608:5376]
            bKb = [Buf("Kb0"), Buf("Kb1")]
            bVb = [Buf("Vb0"), Buf("Vb1")]
            bring = bKb + bVb
            bEE, bSX, bS2, bQB = Buf("EE"), Buf("SX"), Buf("S2"), Buf("QB")
            S.op("pool", lambda e: e.memset(QB, 0.0), w=bKb + bVb + bvbb + [bEE, bSX, bS2, bQB] + bKT + bVV + buext)

            load_xT(xs, NS, NS, xT=xsT)
            for l in range(L):
                h, rstd = rmsnorm(xsT, w, "mix_pre", l * KC)
                B.release(rstd)
                mix = B.allocR(KC)
                qm = B.alloc(2)
                if l < NA:
                    sig = B.alloc(CC)
                    us = B.alloc(CC)

                    def cons_gate(oc, bk, sig=sig):
                        S.op("act", lambda e: e.activation(out=sig[oc].f(w), in_=bk.ap[:, 0:w], func=AF.Sigmoid), r=[bk.buf], w=[sig[oc].buf])
                    linear(h, w, w_in_a[l], C, CC, cons_gate)

                    def cons_a(oc, bk, sig=sig, us=us):
                        S.op("dve", lambda e: e.tensor_tensor(out=us[oc].f(w), in0=sig[oc].f(w), in1=bk.ap[:, 0:w], op=ALU.mult),
                             r=[sig[oc].buf, bk.buf], w=[us[oc].buf])
                    linear(h, w, w_in_a[l], 0, CC, cons_a)
                    B.release(sig)

                    def cons_qm(oc, bk, qm=qm):
                        en = B.ev()
                        S.op(en, _copy(en, qm[oc].b(w), bk.ap[:, 0:w]), r=[bk.buf], w=[qm[oc].buf])
                    linear(h, w, w_in_a[l], 2 * C, 2, cons_qm)
                    B.release(h)
                    S.op("sp", lambda e, l=l: e.dma_start(out=conv_sample[l, :, 0:CW - 2, :], in_=state_conv[l, :, 1:CW - 1, :]), dma=True)
                    store_T(us, w, conv_sample[l, :, CW - 2, :], C)
                    stc = B.alloc(CC)
                    for b in range(NS):
                        stg = B.alloc(2)
                        S.op("sp", lambda e, l=l, b=b, stg=stg: e.dma_start(out=stg[0].f()[0:CW - 1, :], in_=state_conv[l, b, :, 0:512]),
                             w=[stg[0].buf], dma=True)
                        S.op("sp", lambda e, l=l, b=b, stg=stg: e.dma_start(out=stg[1].f()[0:CW - 1, 0:256], in_=state_conv[l, b, :, 512:768]),
                             w=[stg[1].buf], dma=True)
                        bka, bkb = B.bank(), B.bank()

                        def trs(e, stg=stg, bka=bka):
                            ins = None
                            for c in range(4):
                                ins = e.transpose(out=bka.ap[:, c * 32:c * 32 + 30], in_=stg[0].f()[0:CW - 1, c * 128:(c + 1) * 128],
                                                  identity=ident[0:CW - 1, 0:CW - 1])
                            return ins

                        def trs2(e, stg=stg, bkb=bkb):
                            ins = None
                            for c in range(2):
                                ins = e.transpose(out=bkb.ap[:, c * 32:c * 32 + 30], in_=stg[1].f()[0:CW - 1, c * 128:(c + 1) * 128],
                                                  identity=ident[0:CW - 1, 0:CW - 1])
                            return ins
                        S.op("pe", trs, r=[stg[0].buf, bconst], w=[bka.buf])
                        S.op("pe", trs2, r=[stg[1].buf, bconst], w=[bkb.buf])
                        for c in range(CC):
                            bk_ = bka if c < 4 else bkb
                            cl = c % 4
                            en = B.ev()
                            S.op(en, _copy(en, stc[c].f(120)[:, b * 30:(b + 1) * 30], bk_.ap[:, cl * 32:cl * 32 + 30]), r=[bk_.buf], w=[stc[c].buf])
                        B.release(stg)
                    yc = []
                    yr = B.allocR(CC)
                    o_w = voff["conv_w"] + l * CW * CC
                    for c in range(CC):
                        tmp = B.alloc()
                        ysm = B.alloc()
                        yc.append(B.alloc())
                        wck = G[:, o_w:o_w + CW * CC].rearrange("p (k c) -> p c k", c=CC)[:, c, 0:CW - 1]
                        stv = stc[c].f(120).rearrange("p (b k) -> p b k", b=NS)
                        tv = tmp.f(120).rearrange("p (b k) -> p b k", b=NS)
                        S.op("dve", lambda e, tv=tv, stv=stv, wck=wck: e.tensor_tensor(
                            out=tv, in0=stv, in1=wck.unsqueeze(1).broadcast_to([128, NS, CW - 1]), op=ALU.mult),
                            r=[stc[c].buf, bG], w=[tmp.buf])
                        S.op("dve", lambda e, tv=tv, ysm=ysm: e.reduce_sum(out=ysm.f(w), in_=tv, axis=AX.X), r=[tmp.buf], w=[ysm.buf])
                        S.op("dve", lambda e, c=c, l=l, ysm=ysm: e.scalar_tensor_tensor(
                            out=yc[c].f(w), in0=us[c].f(w), scalar=gcol("conv_w", (l * CW + CW - 1) * CC + c), in1=ysm.f(w),
                            op0=ALU.mult, op1=ALU.add), r=[us[c].buf, ysm.buf, bG], w=[yc[c].buf])
                        S.op("dve", lambda e, c=c, l=l: e.tensor_scalar(out=yr[c].r(w), in0=yc[c].f(w), scalar1=gcol("conv_b", l * CC + c),
                                                                        scalar2=None, op0=ALU.add), r=[yc[c].buf, bG], w=[yr[c].buf])
                        B.release([tmp, ysm, stc[c], us[c]])
                    conv_ln(yr, yc, w, l, mix)
                    B.release(yc)
                    B.release(yr)
                else:
                    lb = l - NA
                    qs = B.alloc(CC)

                    def cons_q(oc, bk, qs=qs, qm=qm):
                        en = B.ev()
                        if oc < CC:
                            S.op(en, _copy(en, qs[oc].f(w), bk.ap[:, 0:w]), r=[bk.buf], w=[qs[oc].buf])
                        else:
                            S.op(en, _copy(en, qm[oc - CC].b(w), bk.ap[:, 0:w]), r=[bk.buf], w=[qm[oc - CC].buf])
                    linear(h, w, w_in_b[lb], 0, CC + 2, cons_q)
                    B.release(h)
                    EEv = EE.rearrange("p (t h) -> p t h", h=NH)
                    ob = [banks[6], banks[7]]
                    nmm = 0
                    for g in range(2):
                        bqa, bqb = B.bank(), B.bank()
                        for c in range(CC):
                            qrep = B.alloc()
                            for hb in range(2):
                                S.op("dve", lambda e, c=c, hb=hb, g=g, qrep=qrep: e.tensor_scalar(
                                    out=qrep.f(128)[:, hb * 64:(hb + 1) * 64], in0=ones_f[:, 0:64],
                                    scalar1=qs[c].f(w)[:, 2 * g + hb:2 * g + hb + 1], scalar2=None, op0=ALU.mult),
                                    r=[qs[c].buf, bconst], w=[qrep.buf])
                            bq = bqa if c < 4 else bqb
                            S.op("pe", lambda e, c=c, bq=bq, qrep=qrep: e.matmul(bq.ap[:, (c % 4) * 128:(c % 4 + 1) * 128], lhsT=qrep.f(128),
                                                                                 rhs=ident[:], start=True, stop=True, skip_group_check=True),
                                 r=[qrep.buf, bconst], w=[bq.buf])
                            B.release(qrep)
                        S.op("dve", lambda e, bqa=bqa: e.tensor_copy(out=QB[:, 0:512], in_=bqa.ap[:, 0:512]), r=[bqa.buf], w=[bQB])
                        S.op("dve", lambda e, bqb=bqb: e.tensor_copy(out=QB[:, 512:768], in_=bqb.ap[:, 0:256]), r=[bqb.buf], w=[bQB])
                        for ch in range(PAGE // 2):
                            t0 = 2 * ch
                            kb, bkb_ = ring[ch % 4], bring[ch % 4]
                            S.op("pool", lambda e, kb=kb, ch=ch, g=g: e.indirect_dma_start(
                                out=kb, out_offset=None, in_=ck2[:, :],
                                in_offset=bass.IndirectOffsetOnAxis(ap=idxall[:, g, ch:ch + 1], axis=0)), r=[bidx], w=[bkb_], dma=True)
                            kv3 = kb.rearrange("p (t f) -> p t f", t=2)
                            S.op("pool", lambda e, kv3=kv3: e.tensor_tensor(out=kv3, in0=kv3, in1=QB.unsqueeze(1).broadcast_to([128, 2, C]),
                                                                            op=ALU.mult), r=[bkb_, bQB], w=[bkb_])
                            S.op("dve", lambda e, kb=kb, t0=t0: e.reduce_sum(out=EE[:, t0 * NH:(t0 + 2) * NH],
                                                                             in_=kb.rearrange("p (a d) -> p a d", d=64), axis=AX.X),
                                 r=[bkb_], w=[bEE])
                        for hd in range(NH):
                            S.op("act", lambda e, hd=hd: e.activation(out=EEv[:, :, hd], in_=EEv[:, :, hd], func=AF.Exp,
                                                                      bias=sbb[:, lb * NH + hd:lb * NH + hd + 1], scale=0.125),
                                 r=[bEE, bsbb], w=[bEE])
                        S.op("act", lambda e: e.activation(out=SX, in_=EE, func=AF.Ln, bias=1.0, scale=1.0), r=[bEE], w=[bSX])
                        cur, bcur, nxt, bnxt = SX, bSX, S2, bS2
                        for sft in (1, 2, 4, 8, 16, 32, 64):
                            cv = cur.rearrange("p (t h) -> p t h", h=NH)
                            nv = nxt.rearrange("p (t h) -> p t h", h=NH)
                            S.op("dve", lambda e, cv=cv, nv=nv, sft=sft: e.tensor_tensor(out=nv[:, 0:PAGE - sft, :], in0=cv[:, 0:PAGE - sft, :],
                                                                                       in1=cv[:, sft:PAGE, :], op=ALU.add),
                                 r=[bcur], w=[bnxt])
                            S.op("pool", lambda e, cv=cv, nv=nv, sft=sft: e.tensor_copy(out=nv[:, PAGE - sft:PAGE, :], in_=cv[:, PAGE - sft:PAGE, :]),
                                 r=[bcur], w=[bnxt])
                            cur, bcur, nxt, bnxt = nxt, bnxt, cur, bcur
                        cv = cur.rearrange("p (t h) -> p t h", h=NH)
                        nv = nxt.rearrange("p (t h) -> p t h", h=NH)
                        bc = B.bank()
                        S.op("pe", lambda e, bc=bc, cv=cv: e.matmul(bc.ap[:, 0:NH], lhsT=tri[:], rhs=cv[:, 0, :], start=True, stop=True),
                             r=[bcur, bconst], w=[bc.buf])
                        S.op("dve", lambda e, bc=bc: e.tensor_scalar(out=negc[:, 0:NH], in0=bc.ap[:, 0:NH], scalar1=-1.0, scalar2=None, op0=ALU.mult),
                             r=[bc.buf], w=[bnegc])
                        for hd in range(NH):
                            S.op("act", lambda e, hd=hd, cv=cv, nv=nv: e.activation(out=nv[:, :, hd], in_=cv[:, :, hd], func=AF.Exp, scale=-1.0,
                                                                                   bias=negc[:, hd:hd + 1]), r=[bcur, bnegc], w=[bnxt])
                        S.op("dve", lambda e, nxt=nxt: e.tensor_tensor(out=EE, in0=EE, in1=nxt, op=ALU.mult), r=[bEE, bnxt], w=[bEE])
                        for ch in range(PAGE // 2):
                            t0 = 2 * ch
                            vb, bvb_ = ring[ch % 4], bring[ch % 4]
                            vq, bvq = vbb[ch % 4], bvbb[ch % 4]
                            S.op("pool", lambda e, vb=vb, ch=ch, g=g: e.indirect_dma_start(
                                out=vb, out_offset=None, in_=cv2[:, :],
                                in_offset=bass.IndirectOffsetOnAxis(ap=idxall[:, g, ch:ch + 1], axis=0)), r=[bidx], w=[bvb_], dma=True)
                            v4 = vb.rearrange("p (t h d) -> p t h d", t=2, h=NH)
                            q4 = vq.rearrange("p (t h d) -> p t h d", t=2, h=NH)
                            S.op("dve", lambda e, v4=v4, q4=q4, t0=t0: e.tensor_tensor(
                                out=q4, in0=v4, in1=EEv[:, t0:t0 + 2, :].unsqueeze(3).broadcast_to([128, 2, NH, 64]), op=ALU.mult),
                                r=[bvb_, bEE], w=[bvq])

                            def pvmm(e, vq=vq, g=g, first=(nmm == 0), last=(g == 1 and ch == PAGE // 2 - 1)):
                                ins = None
                                for t in range(2):
                                    for hf in range(2):
                                        ins = e.matmul(ob[hf].ap[0:NS, 0:384], lhsT=sel_b[:, g, :], rhs=vq[:, t * C + hf * 384:t * C + (hf + 1) * 384],
                                                       start=(first and t == 0), stop=(last and t == 1), skip_group_check=True)
                                return ins
                            S.op("pe", pvmm, r=[bvq, bconst], w=[ob[0].buf, ob[1].buf])
                            nmm += 1
                    B.release(qs)
                    osb = B.alloc(2)
                    for hf in range(2):
                        S.op("dve", lambda e, hf=hf: e.tensor_copy(out=osb[hf].f()[0:NS, 0:384], in_=ob[hf].ap[0:NS, 0:384]), r=[ob[hf].buf], w=[osb[hf].buf])
                    bt = B.bank()

                    def tro(e, bt=bt, osb=osb):
                        ins = None
                        for c in range(CC):
                            ins = e.transpose(out=bt.ap[:, c * NS:(c + 1) * NS], in_=osb[c // 3].f()[0:NS, (c % 3) * 128:(c % 3 + 1) * 128],
                                              identity=ident[0:NS, 0:NS])
                        return ins
                    S.op("pe", tro, r=[osb[0].buf, osb[1].buf, bconst], w=[bt.buf])
                    for c in range(CC):
                        S.op("dve", lambda e, c=c, bt=bt: e.tensor_copy(out=mix[c].r(w), in_=bt.ap[:, c * NS:(c + 1) * NS]), r=[bt.buf], w=[mix[c].buf])
                    B.release(osb)
                for b in range(NS):
                    stgk, stgv, mks, mvs = B.alloc(), B.alloc(), B.alloc(), B.alloc()
                    S.op("sp", lambda e, l=l, b=b, stgk=stgk: e.dma_start(out=stgk.f().rearrange("p (mh f) -> p mh f", mh=2),
                                                                         in_=cmk[l, b].rearrange("(mh p) f -> p mh f", p=128)), w=[stgk.buf], dma=True)
                    S.op("sp", lambda e, l=l, b=b, stgv=stgv: e.dma_start(out=stgv.f().rearrange("p (mh f) -> p mh f", mh=2),
                                                                         in_=cmv[l, b].rearrange("(mh p) f -> p mh f", p=128)), w=[stgv.buf], dma=True)
                    for c in range(2):
                        bk = B.bank()

                        def trk(e, c=c, bk=bk, stgk=stgk):
                            ins = None
                            for mh in range(2):
                                ins = e.transpose(out=bk.ap[:, mh * 128:(mh + 1) * 128], in_=stgk.f()[:, mh * 256 + c * 128:mh * 256 + (c + 1) * 128],
                                                  identity=ident[:])
                            return ins
                        S.op("pe", trk, r=[stgk.buf, bconst], w=[bk.buf])
                        en = B.ev()
                        S.op(en, _copy(en, mks.b()[:, c * 256:(c + 1) * 256], bk.ap[:, 0:256]), r=[bk.buf], w=[mks.buf])
                    S.op("pool", lambda e, stgv=stgv, mvs=mvs: e.tensor_copy(out=mvs.b(), in_=stgv.f()), r=[stgv.buf], w=[mvs.buf])
                    mem_attention(qm, w, lambda c, mks=mks: mks.b()[:, c * 256:(c + 1) * 256], lambda mh, mvs=mvs: mvs.b()[:, mh * 256:(mh + 1) * 256],
                                  [mks.buf, mvs.buf], mix[CC:CC + 2], cols=[(b, 1)])
                    B.release([stgk, stgv, mks, mvs])
                B.release(qm)
                proj_post_add(xsT, w, mix, w_out[l], "mix_post", l * KC)
                B.release(mix)
                ffn(xsT, w, l)
                if l == NA - 1:
                    hk, rstd = rmsnorm(xsT, w, "kv_g", 0)
                    B.release(rstd)
                    for grp in range(6):
                        def cons_kvs(s_, n, bk, grp=grp):
                            stg = B.alloc()
                            S.op("act", lambda e: e.activation(out=stg.f(256)[0:n, :], in_=bk.ap[0:n, 0:256], func=AF.Copy), r=[bk.buf], w=[stg.buf])
                            dstd = k_sample if grp < 3 else v_sample
                            c0 = (grp % 3) * 256
                            S.op("sp", lambda e: e.dma_start(out=dstd[0:n, c0:c0 + 256], in_=stg.f(256)[0:n, :]), r=[stg.buf], dma=True)
                            B.release(stg)
                        linear_tok(hk, w, w_kv, grp * 256, 256, cons_kvs)
                    B.release(hk)
            store_T(xsT, w, y_sample, D)

        S.emit(st)
    return nc


_NC_CACHE = {}


def kernel(**inputs):
    n = 8
    if "nc" not in _NC_CACHE:
        _NC_CACHE["nc"] = build_program()
    nc = _NC_CACHE["nc"]
    f = lambda a: np.ascontiguousarray(np.asarray(a, dtype=np.float32))
    if SAMPLE:
        ck = f(inputs["cache_k"]).reshape(NPHYS, PAGE * C)
        cv = f(inputs["cache_v"]).reshape(NPHYS, PAGE * C)
    shared = {
        "norm_mix_pre": f(inputs["norm_mix_pre"]), "norm_mix_post": f(inputs["norm_mix_post"]),
        "norm_ffn_pre": f(inputs["norm_ffn_pre"]), "norm_ffn_post": f(inputs["norm_ffn_post"]),
        "w_in_a": f(inputs["w_in_a"]), "conv_w": f(inputs["conv_w"]), "conv_b": f(inputs["conv_b"]),
        "conv_ln_g": f(inputs["conv_ln_g"]), "conv_ln_b": f(inputs["conv_ln_b"]),
        "w_in_b": f(inputs["w_in_b"]), "sb_bias": f(inputs["sb_bias"]).reshape(1, -1),
        "kv_norm_g": f(inputs["kv_norm_g"]).reshape(1, D), "w_kv": f(inputs["w_kv"]),
        "mem_norm_g": f(inputs["mem_norm_g"]), "w_mem_kv": f(inputs["w_mem_kv"]), "w_out": f(inputs["w_out"]),
        "w_ffn_up": f(inputs["w_ffn_up"]), "w_ffn_down": f(inputs["w_ffn_down"]),
    }
    xpr = f(inputs["x_prompt"])
    xsm = f(inputs["x_sample"])
    stc = f(inputs["state_conv"])
    cmk = f(inputs["cache_mem_k"])
    cmv = f(inputs["cache_mem_v"])
    pt = np.ascontiguousarray(np.asarray(inputs["page_table"], dtype=np.int32))
    mp = f(inputs["mem_prompt"])
    in_maps = []
    for c in range(n):
        m = dict(shared)
        m["xp"] = xpr[c]
        if SAMPLE:
            m["cache_k"] = ck
            m["cache_v"] = cv
            m["xs"] = np.ascontiguousarray(xsm[4 * c:4 * c + 4, 0, :])
            m["state_conv"] = np.ascontiguousarray(stc[:, 4 * c:4 * c + 4])
            m["cmk"] = np.ascontiguousarray(cmk[:, 4 * c:4 * c + 4].reshape(L, NS, NMEM, 256))
            m["cmv"] = np.ascontiguousarray(cmv[:, 4 * c:4 * c + 4].reshape(L, NS, NMEM, 256))
            m["ptab"] = np.ascontiguousarray(pt[4 * c:4 * c + 4].reshape(NS * NPAGE, 1))
        m["memp"] = mp[c]
        in_maps.append(m)
    if DBG_CORES < n:
        res = run_bass_kernel_spmd(nc, in_maps[:DBG_CORES], core_ids=list(range(DBG_CORES)))
        R = list(res.results) + [res.results[0]] * (n - DBG_CORES)
    else:
        res = run_bass_kernel_spmd(nc, in_maps, core_ids=list(range(n)))
        R = res.results
    y_prompt = np.stack([R[c]["y_prompt"] for c in range(n)], 0)
    y_sample = np.concatenate([R[c]["y_sample"] for c in range(n)], 0).reshape(32, 1, D)
    k_prompt = np.stack([R[c]["k_prompt"] for c in range(n)], 0).reshape(n, T, NH, 64)
    v_prompt = np.stack([R[c]["v_prompt"] for c in range(n)], 0).reshape(n, T, NH, 64)
    conv_prompt = np.stack([R[c]["conv_prompt"] for c in range(n)], 1)
    mk = np.stack([R[c]["mk_p"] for c in range(n)], 1).reshape(L, n, NMEM, 4, 64)
    mv = np.stack([R[c]["mv_p"] for c in range(n)], 1).reshape(L, n, NMEM, 4, 64)
    k_sample = np.concatenate([R[c]["k_sample"] for c in range(n)], 0).reshape(32, 1, NH, 64)
    v_sample = np.concatenate([R[c]["v_sample"] for c in range(n)], 0).reshape(32, 1, NH, 64)
    conv_sample = np.concatenate([R[c]["conv_sample"] for c in range(n)], 1)
    return (y_prompt, y_sample, k_prompt, v_prompt, conv_prompt, mk, mv, k_sample, v_sample, conv_sample)
```

```python
import contextlib
import math
import sys
import numpy as np
import concourse.bass as bass
import concourse.mybir as mybir
from concourse.bass_utils import run_bass_kernel_spmd

F32 = mybir.dt.float32
F32R = mybir.dt.float32r
BF16 = mybir.dt.bfloat16
I32 = mybir.dt.int32
AF = mybir.ActivationFunctionType
ALU = mybir.AluOpType
AX = mybir.AxisListType

ENGS = ("pe", "act", "dve", "pool", "sp")
DBG_SITES = None
SEM_EPOCH = 24000
N_DMA_SEMS = {"sp": 32, "pool": 16, "act": 4, "pe": 1, "dve": 1}


class Buf:
    __slots__ = ("name", "w", "r", "excl")

    def __init__(self, name, excl=False):
        self.name = name
        self.w = None
        self.r = []
        self.excl = excl


class Op:
    __slots__ = ("eng", "calls", "deps", "raw", "idx", "dma", "sig", "sem", "val", "site")


class _Rec:
    def __init__(self):
        self.calls = []

    def __getattr__(self, name):
        def f(*a, **k):
            self.calls.append((name, a, k))
            return None
        return f


class Sched:
    def __init__(self, nc):
        self.nc = nc
        self.q = {e: [] for e in ENGS}

    def op(self, eng, fn, r=(), w=(), dma=False):
        o = Op()
        rec = _Rec()
        fn(rec)
        o.site = sys._getframe(1).f_lineno
        o.eng, o.calls, o.dma = eng, rec.calls, dma
        o.sig = dma
        o.sem = o.val = o.raw = None
        xr = [b for b in r if b.excl and b not in w]
        if xr:
            r = [b for b in r if not b.excl]
            w = list(w) + xr
        deps, raw = set(), set()
        for b in r:
            if b.w is not None:
                deps.add(b.w)
                raw.add(b.w)
        for b in w:
            if b.w is not None:
                deps.add(b.w)
            deps.update(b.r)
        for b in r:
            b.r.append(o)
        for b in w:
            b.w = o
            b.r = []
        o.idx = len(self.q[eng])
        kept = []
        for d in deps:
            if d is o:
                continue
            if d.eng == eng and not d.dma:
                if eng != "pe":
                    kept.append(d)
            else:
                kept.append(d)
        for d in kept:
            d.sig = True
        o.deps = kept
        self.q[eng].append(o)
        return o

    def emit(self, stack):
        nc = self.nc
        for e in ENGS:
            cnt = 0
            sems = []
            for o in self.q[e]:
                if o.dma or not o.sig:
                    continue
                ep = cnt // SEM_EPOCH
                if ep >= len(sems):
                    sems.append(stack.enter_context(nc.semaphore(f"s_{e}{ep}")))
                o.sem = sems[ep]
                o.val = cnt - ep * SEM_EPOCH + 1
                cnt += 1
        final_waits = []
        for e in ENGS:
            n = sum(1 for o in self.q[e] if o.dma)
            if n == 0:
                continue
            ns = min(n, N_DMA_SEMS[e])
            dma_sems = [stack.enter_context(nc.semaphore(f"s_dma_{e}{i}")) for i in range(ns)]
            dma_tot = [0] * ns
            prev = {}
            rr = 0
            for o in self.q[e]:
                if o.dma:
                    i = rr % ns
                    rr += 1
                    o.sem = dma_sems[i]
                    dma_tot[i] += 16
                    o.val = dma_tot[i]
                    o.raw = prev.get(i)
                    prev[i] = o
            final_waits += [(dma_sems[i], dma_tot[i]) for i in range(ns)]

        def run(e, eng, last=False):
            waited = {}

            def wait(sem, val):
                k = id(sem)
                if waited.get(k, 0) >= val:
                    return
                waited[k] = val
                eng.wait_ge(sem, val)

            for o in self.q[e]:
                for d in o.deps:
                    wait(d.sem, d.val)
                if o.dma and o.raw is not None:
                    wait(o.raw.sem, o.raw.val)
                ins = None
                for name, a, k in o.calls:
                    ins = getattr(eng, name)(*a, **k)
                    if DBG_SITES is not None:
                        try:
                            DBG_SITES[ins.ins.name] = o.site
                        except Exception:
                            pass
                if o.sig:
                    ins.then_inc(o.sem, 16 if o.dma else 1)
            if last:
                for sem, val in final_waits:
                    wait(sem, val)

        with nc.Block() as block:
            @block.tensor
            def _(eng):
                run("pe", eng)

            @block.scalar
            def _(eng):
                run("act", eng)

            @block.vector
            def _(eng):
                run("dve", eng)

            @block.gpsimd
            def _(eng):
                run("pool", eng)

            @block.sync
            def _(eng):
                run("sp", eng, last=True)


D = 1024
KC = 8
T = 2048
TW = 512
NTILE = 4
C = 768
CC = 6
CW = 31
FF = 2816
FC = 22
NMEM = 256
L = 4
NA = 2
NH = 12
NS = 4
NPAGE = 64
PAGE = 128
NPHYS = 2560
RMS_EPS = 1e-6
LN_EPS = 1e-5
NR = 22
NP = 17
WBUF_COLS = 2048
NWBUF = 4
SAMPLE = True
DBG_NTILE = NTILE
DBG_L = L
DBG_TAP = None
DBG_STAGE = 9
DBG_CORES = 8
DBG_SUB = 9


class Slot:
    def __init__(self, t, i):
        self.t = t
        self.buf = Buf(f"slot{i}")

    def f(self, w=512):
        return self.t[:, 0:w]

    def r(self, w=512):
        return self.t[:, 0:w].bitcast(F32R)

    def b(self, w=512):
        return self.t[:].bitcast(BF16)[:, 0:w]

    def b2(self, w=512):
        return self.t[:].bitcast(BF16)[:, 512:512 + w]


class Bank:
    def __init__(self, t, i):
        self.t = t
        self.ap = t[:]
        self.buf = Buf(f"bank{i}", excl=True)


class WB:
    def __init__(self, t, i):
        self.t = t
        self.buf = Buf(f"wb{i}")


class Builder:
    def __init__(self, nc, st):
        self.nc = nc
        self.st = st
        self.S = Sched(nc)
        self.free = []
        self.freeR = []
        for i in range(NP):
            s = Slot(self.sb(f"sl{i}", [128, 512], F32), i)
            s.pool = self.free
            self.free.append(s)
        for i in range(NR):
            s = Slot(self.sb(f"sr{i}", [128, 512], F32), 100 + i)
            s.pool = self.freeR
            self.freeR.append(s)
        self.banks = [Bank(st.enter_context(nc.psum_tensor(f"ps{i}", [128, 512], F32)), i) for i in range(8)]
        self.rb = 0
        self.wbufs = [WB(self.sb(f"wb{i}", [128, WBUF_COLS], F32R), i) for i in range(NWBUF)]
        self.wbi = 0
        self.tog = 0

    def sb(self, name, shape, dt):
        return self.st.enter_context(self.nc.sbuf_tensor(name, shape, dt))

    def alloc(self, n=None):
        if n is None:
            return self.free.pop(0)
        return [self.free.pop(0) for _ in range(n)]

    def allocR(self, n=None):
        if n is None:
            return self.freeR.pop(0)
        return [self.freeR.pop(0) for _ in range(n)]

    def release(self, s):
        if isinstance(s, (list, tuple)):
            for x in s:
                x.pool.append(x)
        else:
            s.pool.append(s)

    def bank(self, lo=0, hi=6):
        b = self.banks[lo + self.rb % (hi - lo)]
        self.rb += 1
        return b

    def wbuf(self):
        w = self.wbufs[self.wbi % NWBUF]
        self.wbi += 1
        return w

    def ev(self):
        self.tog ^= 1
        return "act" if self.tog else "dve"


def _copy(eng_name, out, in_):
    if eng_name == "act":
        return lambda e: e.activation(out=out, in_=in_, func=AF.Copy)
    return lambda e: e.tensor_copy(out=out, in_=in_)


def build_program():
    nc = bass.Bass("TRN2", target_bir_lowering=False)
    nc.dge_precook = False
    dt_in = {}

    def din(name, shape, dt=F32):
        dt_in[name] = nc.dram_tensor(name, list(shape), dt, kind="ExternalInput").ap()
        return dt_in[name]

    def dout(name, shape):
        return nc.dram_tensor(name, list(shape), F32, kind="ExternalOutput").ap()

    xp = din("xp", [T, D])
    if SAMPLE:
        xs = din("xs", [NS, D])
        cache_k = din("cache_k", [NPHYS, PAGE * C])
        cache_v = din("cache_v", [NPHYS, PAGE * C])
        state_conv = din("state_conv", [NA, NS, CW - 1, C])
        cmk = din("cmk", [L, NS, NMEM, 256])
        cmv = din("cmv", [L, NS, NMEM, 256])
        ptab = din("ptab", [NS * NPAGE, 1], I32)
    memp = din("memp", [NMEM, D])
    g_mix_pre = din("norm_mix_pre", [L, D])
    g_mix_post = din("norm_mix_post", [L, D])
    g_ffn_pre = din("norm_ffn_pre", [L, D])
    g_ffn_post = din("norm_ffn_post", [L, D])
    w_in_a = din("w_in_a", [NA, D, 2 * C + 256], F32R)
    conv_w = din("conv_w", [NA, CW, C])
    conv_b = din("conv_b", [NA, C])
    conv_ln_g = din("conv_ln_g", [NA, C])
    conv_ln_b = din("conv_ln_b", [NA, C])
    w_in_b = din("w_in_b", [L - NA, D, D], F32R)
    sb_bias = din("sb_bias", [1, (L - NA) * NH])
    kv_norm_g = din("kv_norm_g", [1, D])
    w_kv = din("w_kv", [D, 2 * C], F32R)
    mem_norm_g = din("mem_norm_g", [L, D])
    w_mem_kv = din("w_mem_kv", [L, D, 512], F32R)
    w_out = din("w_out", [L, D, D], F32R)
    w_up = din("w_ffn_up", [L, D, 2 * FF], F32R)
    w_down = din("w_ffn_down", [L, FF, D], F32R)

    y_prompt = dout("y_prompt", [T, D])
    y_sample = dout("y_sample", [NS, D])
    k_prompt = dout("k_prompt", [T, C])
    v_prompt = dout("v_prompt", [T, C])
    conv_prompt = dout("conv_prompt", [NA, CW - 1, C])
    mk_p = dout("mk_p", [L, NMEM, 256])
    mv_p = dout("mv_p", [L, NMEM, 256])
    k_sample = dout("k_sample", [NS, C])
    v_sample = dout("v_sample", [NS, C])
    conv_sample = dout("conv_sample", [NA, NS, CW - 1, C])
    dbg = dout("dbg", [TW, D]) if DBG_TAP is not None else None

    with contextlib.ExitStack() as st:
        B = Builder(nc, st)
        S = B.S
        banks = B.banks

        ones_r = B.sb("ones_r", [128, 128], F32R)
        ones_f = B.sb("ones_f", [128, 128], F32)
        ident = B.sb("ident", [128, 128], F32)
        ident_b = B.sb("ident_b", [128, 128], BF16)
        u_incl = B.sb("u_incl", [128, 128], BF16)
        u_lt = B.sb("u_lt", [128, 128], BF16)
        masks = B.sb("masks", [128, 4, 512], BF16)
        bconst = Buf("const")
        S.op("pool", lambda e: e.memset(ones_f[:], 1.0), w=[bconst])
        S.op("dve", lambda e: e.tensor_copy(out=ones_r[:], in_=ones_f[:]), r=[bconst], w=[bconst])
        S.op("pool", lambda e: e.memset(ident[:], 0.0), w=[bconst])
        S.op("pool", lambda e: e.affine_select(out=ident[:], in_=ident[:], pattern=[[-1, 128]], compare_op=ALU.not_equal,
                                               fill=1.0, base=0, channel_multiplier=1), r=[bconst], w=[bconst])
        S.op("pool", lambda e: e.tensor_copy(out=ident_b[:], in_=ident[:]), r=[bconst], w=[bconst])
        S.op("pool", lambda e: e.memset(u_incl[:], 1.0), w=[bconst])
        S.op("pool", lambda e: e.affine_select(out=u_incl[:], in_=u_incl[:], pattern=[[-1, 128]], compare_op=ALU.is_ge,
                                               fill=0.0, base=0, channel_multiplier=1), r=[bconst], w=[bconst])
        S.op("pool", lambda e: e.memset(u_lt[:], 1.0), w=[bconst])
        S.op("pool", lambda e: e.affine_select(out=u_lt[:], in_=u_lt[:], pattern=[[1, 128]], compare_op=ALU.is_gt,
                                               fill=0.0, base=0, channel_multiplier=-1), r=[bconst], w=[bconst])
        S.op("pool", lambda e: e.memset(masks[:], 1.0), w=[bconst])
        for dd in range(4):
            S.op("pool", lambda e, dd=dd: e.affine_select(out=masks[:, dd, :], in_=masks[:, dd, :], pattern=[[1, 512]],
                                                          compare_op=ALU.is_gt, fill=0.0, base=-128 * dd,
                                                          channel_multiplier=-1), r=[bconst], w=[bconst])

        vec_srcs = [
            ("mix_pre", g_mix_pre.rearrange("l (kc p) -> (l kc) p", p=128)),
            ("mix_post", g_mix_post.rearrange("l (kc p) -> (l kc) p", p=128)),
            ("ffn_pre", g_ffn_pre.rearrange("l (kc p) -> (l kc) p", p=128)),
            ("ffn_post", g_ffn_post.rearrange("l (kc p) -> (l kc) p", p=128)),
            ("mem_g", mem_norm_g.rearrange("l (kc p) -> (l kc) p", p=128)),
            ("kv_g", kv_norm_g.rearrange("l (kc p) -> (l kc) p", p=128)),
            ("conv_b", conv_b.rearrange("l (cc p) -> (l cc) p", p=128)),
            ("ln_g", conv_ln_g.rearrange("l (cc p) -> (l cc) p", p=128)),
            ("ln_b", conv_ln_b.rearrange("l (cc p) -> (l cc) p", p=128)),
            ("conv_w", conv_w.rearrange("l k (cc p) -> (l k cc) p", p=128)),
        ]
        nvec = sum(a.shape[0] for _, a in vec_srcs)
        G = B.sb("G", [128, nvec], F32)
        bG = Buf("G")
        voff = {}
        vstage = B.sb("vstage", [128, 128], F32)
        bvs = Buf("vstage")
        off = 0
        for name, a in (vec_srcs if DBG_STAGE >= 1 else []):
            voff[name] = off
            nrows = a.shape[0]
            for r0 in range(0, nrows, 128):
                n = min(128, nrows - r0)
                bk = B.bank()
                S.op("sp", lambda e, a=a, r0=r0, n=n: e.dma_start(out=vstage[0:n, :], in_=a[r0:r0 + n, :]), w=[bvs], dma=True)
                S.op("pe", lambda e, bk=bk, n=n: e.transpose(out=bk.ap[:, 0:n], in_=vstage[0:n, :], identity=ident[0:n, 0:n]),
                     r=[bvs, bconst], w=[bk.buf])
                S.op("dve", lambda e, bk=bk, n=n, o=off + r0: e.tensor_copy(out=G[:, o:o + n], in_=bk.ap[:, 0:n]),
                     r=[bk.buf], w=[bG])
            off += nrows

        def gcol(name, idx):
            o = voff[name] + idx
            return G[:, o:o + 1]

        sbb_row = B.sb("sbb_row", [1, (L - NA) * NH], F32)
        sbb = B.sb("sbb", [128, (L - NA) * NH], F32)
        bsbb = Buf("sbb")
        if DBG_STAGE >= 2:
            S.op("sp", lambda e: e.dma_start(out=sbb_row[:], in_=sb_bias[:, :]), w=[bsbb], dma=True)
            bk = B.bank()
            S.op("pe", lambda e, bk=bk: e.matmul(bk.ap[:, 0:(L - NA) * NH], lhsT=ones_f[0:1, :], rhs=sbb_row[:], start=True, stop=True),
                 r=[bsbb, bconst], w=[bk.buf])
            S.op("dve", lambda e, bk=bk: e.tensor_copy(out=sbb[:], in_=bk.ap[:, 0:(L - NA) * NH]), r=[bk.buf], w=[bsbb])

        KT = B.sb("KT", [128, CC, T], BF16)
        VV = B.sb("VV", [128, T // 128, C], BF16)
        bKT = [Buf(f"KT{t}") for t in range(NTILE)]
        bVV = [Buf(f"VV{t}") for t in range(NTILE)]
        mkT = B.sb("mkT", [128, L, 2, NMEM], BF16)
        mvv = B.sb("mvv", [128, L, 2, 256], BF16)
        bmk = [Buf(f"mk{l}") for l in range(L)]
        bmv = [Buf(f"mv{l}") for l in range(L)]
        uext = B.sb("uext", [128, CC, 30 + TW], F32)
        buext = [Buf(f"uext{c}") for c in range(CC)]
        halo = B.sb("halo", [128, NA, CC, 30], F32)
        bhalo = [Buf(f"halo{l}") for l in range(NA)]
        small = B.sb("small", [128, 16], F32)
        bsmall = [Buf(f"small{i}") for i in range(4)]

        xres = []
        for kc in range(KC):
            sx = Slot(B.sb(f"xres{kc}", [128, 512], F32), 200 + kc)
            sx.pool = []
            xres.append(sx)

        def load_xT(rows_ap, ntok, w, xT=None):
            nsub = (w + 127) // 128
            stage = B.alloc(2 * nsub)
            for s in range(nsub):
                n = min(128, w - s * 128)
                for hf in range(2):
                    sl = stage[2 * s + hf]
                    S.op("sp", lambda e, sl=sl, s=s, n=n, hf=hf: e.dma_start(
                        out=sl.f()[0:n, :], in_=rows_ap[s * 128:s * 128 + n, hf * 512:(hf + 1) * 512]), w=[sl.buf], dma=True)
            if xT is None:
                xT = B.alloc(KC)
            for kc in range(KC):
                bk = B.bank()

                def tr(e, kc=kc, bk=bk):
                    ins = None
                    for s in range(nsub):
                        n = min(128, w - s * 128)
                        sl = stage[2 * s + kc // 4]
                        ins = e.transpose(out=bk.ap[:, s * 128:s * 128 + n], in_=sl.f()[0:n, (kc % 4) * 128:(kc % 4) * 128 + 128],
                                          identity=ident[0:n, 0:n])
                    return ins
                S.op("pe", tr, r=[stage[2 * s + kc // 4].buf for s in range(nsub)] + [bconst], w=[bk.buf])
                en = B.ev()
                S.op(en, _copy(en, xT[kc].f(w), bk.ap[:, 0:w]), r=[bk.buf], w=[xT[kc].buf])
            B.release(stage)
            return xT

        def store_T(src_slots, w, dst_rows_ap, ncols):
            nsub = (w + 127) // 128
            nch = ncols // 128
            for s in range(nsub):
                n = min(128, w - s * 128)
                for g0 in range(0, nch, 4):
                    ng = min(4, nch - g0)
                    bk = B.bank()

                    def tr(e, s=s, n=n, g0=g0, ng=ng, bk=bk):
                        ins = None
                        for j in range(ng):
                            ins = e.transpose(out=bk.ap[0:n, j * 128:(j + 1) * 128],
                                              in_=src_slots[g0 + j].f(w)[:, s * 128:s * 128 + n], identity=ident[:])
                        return ins
                    S.op("pe", tr, r=[src_slots[g0 + j].buf for j in range(ng)] + [bconst], w=[bk.buf])
                    stg = B.alloc()
                    en = B.ev()
                    S.op(en, _copy(en, stg.f()[0:n, 0:ng * 128], bk.ap[0:n, 0:ng * 128]), r=[bk.buf], w=[stg.buf])
                    S.op("sp", lambda e, stg=stg, s=s, n=n, g0=g0, ng=ng: e.dma_start(
                        out=dst_rows_ap[s * 128:s * 128 + n, g0 * 128:(g0 + ng) * 128], in_=stg.f()[0:n, 0:ng * 128]),
                        r=[stg.buf], dma=True)
                    B.release(stg)

        def sumsq_bank(srcs, w, acc):
            n = len(srcs)
            tmp = B.allocR(2)
            for i, (ap, buf) in enumerate(srcs):
                t = tmp[i % 2]
                S.op("act", lambda e, t=t, ap=ap: e.activation(out=t.r(w), in_=ap, func=AF.Square), r=[buf], w=[t.buf])
                S.op("pe", lambda e, t=t, i=i: e.matmul(acc.ap[:, 0:w], lhsT=ones_r[:], rhs=t.r(w), start=(i == 0), stop=(i == n - 1)),
                     r=[t.buf, bconst], w=[acc.buf])
            B.release(tmp)

        def rstd_of(acc, w, dim, eps):
            r = B.alloc()
            S.op("act", lambda e: e.activation(out=r.f(w), in_=acc.ap[:, 0:w], func=AF.Sqrt, scale=1.0 / dim, bias=eps),
                 r=[acc.buf], w=[r.buf])
            S.op("dve", lambda e: e.reciprocal(out=r.f(w), in_=r.f(w)), r=[r.buf], w=[r.buf])
            return r

        def rmsnorm(xT, w, gname, gidx0, rstd=None):
            if rstd is None:
                acc = banks[6]
                sumsq_bank([(s.f(w), s.buf) for s in xT], w, acc)
                rstd = rstd_of(acc, w, D, RMS_EPS)
            h = B.allocR(len(xT))
            for kc in range(len(xT)):
                en = "dve"
                S.op(en, lambda e, kc=kc: e.scalar_tensor_tensor(out=h[kc].r(w), in0=xT[kc].f(w), scalar=gcol(gname, gidx0 + kc),
                                                                  in1=rstd.f(w), op0=ALU.mult, op1=ALU.mult),
                     r=[xT[kc].buf, rstd.buf, bG], w=[h[kc].buf])
            return h, rstd

        def linear(xin, w, Wd, col0, noc, consume):
            KCn = len(xin)
            grp = max(1, min(4, WBUF_COLS // (KCn * 128)))
            for g0 in range(0, noc, grp):
                ng = min(grp, noc - g0)
                wb = B.wbuf()
                ncols = ng * 128
                src = Wd[:, col0 + g0 * 128: col0 + g0 * 128 + ncols].rearrange("(kc p) n -> p kc n", p=128)
                dst = wb.t[:, 0:KCn * ncols].rearrange("p (kc n) -> p kc n", kc=KCn)
                S.op("sp", lambda e, dst=dst, src=src: e.dma_start(out=dst, in_=src), w=[wb.buf], dma=True)
                for j in range(ng):
                    bk = B.bank()

                    def mm(e, dst=dst, j=j, bk=bk):
                        ins = None
                        for kc in range(KCn):
                            ins = e.matmul(bk.ap[:, 0:w], lhsT=dst[:, kc, j * 128:(j + 1) * 128], rhs=xin[kc].r(w),
                                           start=(kc == 0), stop=(kc == KCn - 1))
                        return ins
                    S.op("pe", mm, r=[wb.buf] + [s.buf for s in xin], w=[bk.buf])
                    consume(g0 + j, bk)

        def linear_tok(xin, w, Wd, col0, ncols, consume):
            KCn = len(xin)
            wb = B.wbuf()
            src = Wd[:, col0:col0 + ncols].rearrange("(kc p) n -> p kc n", p=128)
            dst = wb.t[:, 0:KCn * ncols].rearrange("p (kc n) -> p kc n", kc=KCn)
            S.op("sp", lambda e: e.dma_start(out=dst, in_=src), w=[wb.buf], dma=True)
            nsub = (w + 127) // 128
            for s in range(nsub):
                n = min(128, w - s * 128)
                bk = B.bank()

                def mm(e, s=s, n=n, bk=bk):
                    ins = None
                    for kc in range(KCn):
                        ins = e.matmul(bk.ap[0:n, 0:ncols], lhsT=xin[kc].r(w)[:, s * 128:s * 128 + n], rhs=dst[:, kc, :],
                                       start=(kc == 0), stop=(kc == KCn - 1))
                    return ins
                S.op("pe", mm, r=[wb.buf] + [x.buf for x in xin], w=[bk.buf])
                consume(s, n, bk)

        def post_norm_add(xT, w, r, gname, gidx0):
            acc = banks[7]
            sumsq_bank([(s_.f(w), s_.buf) for s_ in r], w, acc)
            rstd = rstd_of(acc, w, D, RMS_EPS)
            for kc in range(KC):
                S.op("dve", lambda e, kc=kc: e.scalar_tensor_tensor(out=r[kc].f(w), in0=r[kc].f(w), scalar=gcol(gname, gidx0 + kc),
                                                                     in1=rstd.f(w), op0=ALU.mult, op1=ALU.mult),
                     r=[r[kc].buf, rstd.buf, bG], w=[r[kc].buf])
                S.op("pool", lambda e, kc=kc: e.tensor_tensor(out=xT[kc].f(w), in0=xT[kc].f(w), in1=r[kc].f(w), op=ALU.add),
                     r=[xT[kc].buf, r[kc].buf], w=[xT[kc].buf])
            B.release(rstd)

        def proj_post_add(xT, w, mixin, Wd, gname, gidx0):
            r = B.alloc(KC)

            def cons(oc, bk):
                en = B.ev()
                S.op(en, _copy(en, r[oc].f(w), bk.ap[:, 0:w]), r=[bk.buf], w=[r[oc].buf])
            linear(mixin, w, Wd, 0, KC, cons)
            post_norm_add(xT, w, r, gname, gidx0)
            B.release(r)

        def ffn(xT, w, l):
            h, rstd = rmsnorm(xT, w, "ffn_pre", l * KC)
            B.release(rstd)
            r = B.alloc(KC)
            HF = FC // 2
            for half in range(2):
                act = B.allocR(HF)
                pend = {}

                def cons(oc, bk, act=act, pend=pend):
                    if oc < HF:
                        t = B.alloc()
                        S.op("act", lambda e: e.activation(out=t.f(w), in_=bk.ap[:, 0:w], func=AF.Silu), r=[bk.buf], w=[t.buf])
                        pend[oc] = t
                    else:
                        j = oc - HF
                        t = pend.pop(j)
                        S.op("dve", lambda e: e.tensor_tensor(out=act[j].r(w), in0=t.f(w), in1=bk.ap[:, 0:w], op=ALU.mult),
                             r=[t.buf, bk.buf], w=[act[j].buf])
                        B.release(t)
                c0 = half * HF * 128
                for g0 in range(0, HF, 4):
                    ng = min(4, HF - g0)
                    linear(h, w, w_up[l], c0 + g0 * 128, ng, lambda oc, bk, g0=g0, cons=cons: cons(g0 + oc, bk))
                    linear(h, w, w_up[l], FF + c0 + g0 * 128, ng, lambda oc, bk, g0=g0, cons=cons: cons(HF + g0 + oc, bk))

                def consd(oc, bk, half=half):
                    if half == 0:
                        en = B.ev()
                        S.op(en, _copy(en, r[oc].f(w), bk.ap[:, 0:w]), r=[bk.buf], w=[r[oc].buf])
                    else:
                        S.op("dve", lambda e: e.tensor_tensor(out=r[oc].f(w), in0=r[oc].f(w), in1=bk.ap[:, 0:w], op=ALU.add),
                             r=[r[oc].buf, bk.buf], w=[r[oc].buf])
                linear(act, w, w_down[l][c0:c0 + HF * 128, :], 0, KC, consd)
                B.release(act)
            B.release(h)
            post_norm_add(xT, w, r, "ffn_post", l * KC)
            B.release(r)

        def mem_attention(qm, w, mk_ap, mv_ap, bufs_mkv, out_slots, cols=None):
            if cols is None:
                cols = [(s_ * 128, min(128, w - s_ * 128)) for s_ in range((w + 127) // 128)]
            for (q0, n) in cols:
                obk = [banks[6], banks[7]]
                for hh in range(4):
                    c, po = hh // 2, (hh % 2) * 64
                    sbk = B.bank()
                    S.op("pe", lambda e, c=c, po=po, sbk=sbk, q0=q0, n=n: e.matmul(
                        sbk.ap[0:n, 0:NMEM], lhsT=qm[c].b(w)[po:po + 64, q0:q0 + n], rhs=mk_ap(c)[po:po + 64, :],
                        start=True, stop=True), r=[qm[c].buf] + bufs_mkv, w=[sbk.buf])
                    sm = bsmall[hh]
                    c0 = hh * 4
                    S.op("dve", lambda e, sbk=sbk, n=n, c0=c0: e.reduce_max(out=small[0:n, c0:c0 + 1], in_=sbk.ap[0:n, 0:NMEM], axis=AX.X),
                         r=[sbk.buf], w=[sm])
                    S.op("dve", lambda e, n=n, c0=c0: e.tensor_scalar(out=small[0:n, c0 + 1:c0 + 2], in0=small[0:n, c0:c0 + 1],
                                                                      scalar1=-0.125, scalar2=None, op0=ALU.mult),
                         r=[sm], w=[sm])
                    p = B.alloc()
                    S.op("act", lambda e, sbk=sbk, n=n, c0=c0, p=p: e.activation(
                        out=p.f()[0:n, 0:NMEM], in_=sbk.ap[0:n, 0:NMEM], func=AF.Exp, bias=small[0:n, c0 + 1:c0 + 2], scale=0.125,
                        accum_out=small[0:n, c0 + 2:c0 + 3]), r=[sbk.buf, sm], w=[p.buf, sm])
                    S.op("dve", lambda e, n=n, c0=c0: e.reciprocal(out=small[0:n, c0 + 3:c0 + 4], in_=small[0:n, c0 + 2:c0 + 3]),
                         r=[sm], w=[sm])
                    pn = B.alloc()
                    S.op("dve", lambda e, n=n, c0=c0, p=p, pn=pn: e.tensor_scalar(
                        out=pn.b()[0:n, 0:NMEM], in0=p.f()[0:n, 0:NMEM], scalar1=small[0:n, c0 + 3:c0 + 4], scalar2=None, op0=ALU.mult),
                        r=[p.buf, sm], w=[pn.buf])
                    B.release(p)
                    tbk = B.bank()
                    tb = tbk.ap.bitcast(BF16)

                    def tr(e, pn=pn, tb=tb, n=n):
                        e.transpose(out=tb[:, 0:n], in_=pn.b()[0:n, 0:128], identity=ident_b[0:n, 0:n])
                        return e.transpose(out=tb[:, 128:128 + n], in_=pn.b()[0:n, 128:256], identity=ident_b[0:n, 0:n])
                    S.op("pe", tr, r=[pn.buf, bconst], w=[tbk.buf])
                    pT = B.alloc()
                    en = B.ev()
                    S.op(en, _copy(en, pT.b()[:, 0:256].rearrange("p (a b) -> p a b", a=2)[:, :, 0:n],
                                   tb[:, 0:256].rearrange("p (a b) -> p a b", a=2)[:, :, 0:n]), r=[tbk.buf], w=[pT.buf])
                    B.release(pn)

                    def pv(e, pT=pT, c=c, po=po, hh=hh, n=n, obk=obk):
                        e.matmul(obk[c].ap[po:po + 64, 0:n], lhsT=mv_ap(0)[:, hh * 64:(hh + 1) * 64], rhs=pT.b()[:, 0:n],
                                 start=True, stop=False)
                        return e.matmul(obk[c].ap[po:po + 64, 0:n], lhsT=mv_ap(1)[:, hh * 64:(hh + 1) * 64], rhs=pT.b()[:, 128:128 + n],
                                        start=False, stop=True)
                    S.op("pe", pv, r=[pT.buf] + bufs_mkv, w=[obk[c].buf])
                    B.release(pT)
                for c in range(2):
                    en = B.ev()
                    S.op(en, _copy(en, out_slots[c].r(w)[:, q0:q0 + n], obk[c].ap[:, 0:n]), r=[obk[c].buf], w=[out_slots[c].buf])

        def conv_ln(yr, yc, w, l, mix):
            acc1, acc2 = banks[6], banks[7]
            for c in range(CC):
                S.op("pe", lambda e, c=c: e.matmul(acc1.ap[:, 0:w], lhsT=ones_r[:], rhs=yr[c].r(w), start=(c == 0), stop=(c == CC - 1)),
                     r=[yr[c].buf, bconst], w=[acc1.buf])
            sumsq_bank([(s_.f(w), s_.buf) for s_ in yr], w, acc2)
            mean = B.alloc()
            msq = B.alloc()
            var = B.alloc()
            S.op("act", lambda e: e.activation(out=mean.f(w), in_=acc1.ap[:, 0:w], func=AF.Copy, scale=1.0 / C), r=[acc1.buf], w=[mean.buf])
            S.op("pool", lambda e: e.tensor_tensor(out=msq.f(w), in0=mean.f(w), in1=mean.f(w), op=ALU.mult), r=[mean.buf], w=[msq.buf])
            S.op("dve", lambda e: e.scalar_tensor_tensor(out=var.f(w), in0=acc2.ap[:, 0:w], scalar=1.0 / C, in1=msq.f(w),
                                                         op0=ALU.mult, op1=ALU.subtract), r=[acc2.buf, msq.buf], w=[var.buf])
            S.op("act", lambda e: e.activation(out=var.f(w), in_=var.f(w), func=AF.Sqrt, scale=1.0, bias=LN_EPS), r=[var.buf], w=[var.buf])
            S.op("dve", lambda e: e.reciprocal(out=var.f(w), in_=var.f(w)), r=[var.buf], w=[var.buf])
            for c in range(CC):
                en = "pool"
                S.op(en, lambda e, c=c: e.tensor_tensor(out=yc[c].f(w), in0=yr[c].f(w), in1=mean.f(w), op=ALU.subtract),
                     r=[yr[c].buf, mean.buf], w=[yc[c].buf])
                en2 = "dve"
                S.op(en2, lambda e, c=c, l=l: e.scalar_tensor_tensor(out=yc[c].f(w), in0=yc[c].f(w), scalar=gcol("ln_g", l * CC + c),
                                                                     in1=var.f(w), op0=ALU.mult, op1=ALU.mult),
                     r=[yc[c].buf, var.buf, bG], w=[yc[c].buf])
                S.op("act", lambda e, c=c, l=l: e.activation(out=mix[c].r(w), in_=yc[c].f(w), func=AF.Silu,
                                                             bias=gcol("ln_b", l * CC + c), scale=1.0),
                     r=[yc[c].buf, bG], w=[mix[c].buf])
            B.release([mean, msq, var])

        if DBG_STAGE >= 3:
            memT = load_xT(memp, NMEM, NMEM)
        if DBG_STAGE >= 4:
            accm = banks[6]
            sumsq_bank([(s.f(NMEM), s.buf) for s in memT], NMEM, accm)
            rstd_m = rstd_of(accm, NMEM, D, RMS_EPS)
        for l in range((L if DBG_STAGE >= 7 else 1) if DBG_STAGE >= 5 else 0):
            mn, _ = rmsnorm(memT, NMEM, "mem_g", l * KC, rstd=rstd_m)

            def cons_mk(oc, bk, l=l):
                en = B.ev()
                S.op(en, _copy(en, mkT[:, l, oc, :], bk.ap[:, 0:NMEM]), r=[bk.buf], w=[bmk[l]])
            linear(mn, NMEM, w_mem_kv[l], 0, 2, cons_mk)

            for grp in range((2 if DBG_SUB >= 2 else 1) if DBG_STAGE >= 6 else 0):
                def cons_tok(s, n, bk, l=l, grp=grp):
                    stg = B.alloc()
                    S.op("act", lambda e: e.activation(out=stg.f(256), in_=bk.ap[:, 0:256], func=AF.Copy), r=[bk.buf], w=[stg.buf])
                    dstd = mk_p if grp == 0 else mv_p
                    if DBG_SUB >= 1:
                        S.op("sp", lambda e: e.dma_start(out=dstd[l, s * 128:(s + 1) * 128, :], in_=stg.f(256)), r=[stg.buf], dma=True)
                    if grp == 1:
                        S.op("dve", lambda e: e.tensor_copy(out=mvv[:, l, s, :], in_=stg.f(256)), r=[stg.buf], w=[bmv[l]])
                    B.release(stg)
                linear_tok(mn, NMEM, w_mem_kv[l], grp * 256, 256, cons_tok)
            B.release(mn)
        if DBG_STAGE >= 5:
            B.release(rstd_m)
            B.release(memT)

        for ti in range(DBG_NTILE):
            t0 = ti * TW
            w = TW
            xT = xres
            load_xT(xp[t0:t0 + TW, :], TW, TW, xT=xres)
            for l in range(DBG_L):
                h, rstd = rmsnorm(xT, w, "mix_pre", l * KC)
                B.release(rstd)
                mix = B.allocR(KC)
                qm = B.alloc(2)
                if l < NA:
                    sig = B.alloc(CC)

                    def cons_gate(oc, bk, sig=sig):
                        S.op("act", lambda e: e.activation(out=sig[oc].f(w), in_=bk.ap[:, 0:w], func=AF.Sigmoid), r=[bk.buf], w=[sig[oc].buf])
                    linear(h, w, w_in_a[l], C, CC, cons_gate)
                    if ti == 0:
                        S.op("pool", lambda e: e.memset(uext[:, :, 0:30], 0.0), w=buext)
                    else:
                        S.op("pool", lambda e, l=l: e.tensor_copy(out=uext[:, :, 0:30], in_=halo[:, l, :, :]), r=[bhalo[l]], w=buext)

                    def cons_a(oc, bk, sig=sig):
                        S.op("dve", lambda e: e.tensor_tensor(out=uext[:, oc, 30:30 + w], in0=sig[oc].f(w), in1=bk.ap[:, 0:w], op=ALU.mult),
                             r=[sig[oc].buf, bk.buf], w=[buext[oc]])
                    linear(h, w, w_in_a[l], 0, CC, cons_a)
                    B.release(sig)

                    def cons_qm(oc, bk, qm=qm):
                        en = B.ev()
                        S.op(en, _copy(en, qm[oc].b(w), bk.ap[:, 0:w]), r=[bk.buf], w=[qm[oc].buf])
                    linear(h, w, w_in_a[l], 2 * C, 2, cons_qm)
                    B.release(h)
                    S.op("pool", lambda e, l=l: e.tensor_copy(out=halo[:, l, :, :], in_=uext[:, :, w:w + 30]), r=buext, w=[bhalo[l]])
                    if ti == NTILE - 1:
                        bk = B.bank()

                        bk2 = B.bank()

                        def trh1(e, l=l, bk=bk):
                            ins = None
                            for c in range(4):
                                ins = e.transpose(out=bk.ap[0:30, c * 128:(c + 1) * 128], in_=halo[:, l, c, :], identity=ident[:])
                            return ins

                        def trh2(e, l=l, bk2=bk2):
                            ins = None
                            for c in range(4, 6):
                                ins = e.transpose(out=bk2.ap[0:30, (c - 4) * 128:(c - 3) * 128], in_=halo[:, l, c, :], identity=ident[:])
                            return ins
                        S.op("pe", trh1, r=[bhalo[l], bconst], w=[bk.buf])
                        S.op("pe", trh2, r=[bhalo[l], bconst], w=[bk2.buf])
                        stg = B.alloc(2)
                        S.op("dve", lambda e, bk=bk, stg=stg: e.tensor_copy(out=stg[0].f()[0:30, :], in_=bk.ap[0:30, :]), r=[bk.buf], w=[stg[0].buf])
                        S.op("dve", lambda e, bk2=bk2, stg=stg: e.tensor_copy(out=stg[1].f()[0:30, 0:256], in_=bk2.ap[0:30, 0:256]), r=[bk2.buf], w=[stg[1].buf])
                        S.op("sp", lambda e, l=l, stg=stg: e.dma_start(out=conv_prompt[l, :, 0:512], in_=stg[0].f()[0:30, :]), r=[stg[0].buf], dma=True)
                        S.op("sp", lambda e, l=l, stg=stg: e.dma_start(out=conv_prompt[l, :, 512:768], in_=stg[1].f()[0:30, 0:256]), r=[stg[1].buf], dma=True)
                        B.release(stg)
                    yc = B.alloc(CC)
                    yr = B.allocR(CC)
                    for k in range(CW):
                        for c in range(CC):
                            en = "dve"
                            wc = gcol("conv_w", (l * CW + k) * CC + c)
                            if k == 0:
                                S.op(en, lambda e, c=c, wc=wc, l=l: e.tensor_scalar(out=yc[c].f(w), in0=uext[:, c, 0:w], scalar1=wc,
                                                                                    scalar2=gcol("conv_b", l * CC + c), op0=ALU.mult, op1=ALU.add),
                                     r=[buext[c], bG], w=[yc[c].buf])
                            else:
                                outv = yr[c].r(w) if k == CW - 1 else yc[c].f(w)
                                S.op(en, lambda e, c=c, wc=wc, k=k, outv=outv: e.scalar_tensor_tensor(
                                    out=outv, in0=uext[:, c, k:k + w], scalar=wc, in1=yc[c].f(w), op0=ALU.mult, op1=ALU.add),
                                    r=[buext[c], yc[c].buf, bG], w=[yr[c].buf if k == CW - 1 else yc[c].buf])
                    conv_ln(yr, yc, w, l, mix)
                    B.release(yc)
                    B.release(yr)
                else:
                    lb = l - NA
                    qT = B.alloc(CC)

                    def cons_q(oc, bk, qT=qT, qm=qm):
                        dst = qT[oc] if oc < CC else qm[oc - CC]
                        en = B.ev()
                        S.op(en, _copy(en, dst.b(w), bk.ap[:, 0:w]), r=[bk.buf], w=[dst.buf])
                    linear(h, w, w_in_b[lb], 0, CC + 2, cons_q)
                    B.release(h)
                    kb_hi = (t0 + w) // 128 - 1
                    kvbufs = [bKT[t] for t in range(ti + 1)] + [bVV[t] for t in range(ti + 1)]

                    def head_stream(hd, ibk, obk, zbanks):
                        c, po = hd // 2, (hd % 2) * 64
                        bias = sbb[:, lb * NH + hd: lb * NH + hd + 1]
                        first = True
                        for kb in range(kb_hi, -1, -1):
                            diag = (kb + 1) * 128 > t0
                            dd = kb - t0 // 128
                            zbk = zbanks
                            S.op("pe", lambda e, kb=kb, zbk=zbk: e.matmul(
                                zbk.ap[:, 0:w], lhsT=KT[po:po + 64, c, kb * 128:(kb + 1) * 128], rhs=qT[c].b(w)[po:po + 64, :],
                                start=True, stop=True), r=[qT[c].buf] + kvbufs, w=[zbk.buf])
                            s1 = B.alloc()
                            s2 = B.alloc()
                            ee, wt, sp, ex = s1.b(w), s1.b2(w), s2.b(w), s2.b2(w)
                            S.op("act", lambda e, zbk=zbk, ee=ee: e.activation(out=ee, in_=zbk.ap[:, 0:w], func=AF.Exp, bias=bias, scale=0.125),
                                 r=[zbk.buf, bsbb], w=[s1.buf])
                            if diag:
                                S.op("dve", lambda e, ee=ee, dd=dd: e.tensor_tensor(out=ee, in0=ee, in1=masks[:, dd, 0:w], op=ALU.mult),
                                     r=[s1.buf, bconst], w=[s1.buf])
                            S.op("act", lambda e, ee=ee, sp=sp: e.activation(out=sp, in_=ee, func=AF.Ln, bias=1.0, scale=1.0),
                                 r=[s1.buf], w=[s2.buf])
                            yield
                            S.op("pe", lambda e, sp=sp, first=first: e.matmul(ibk.ap[:, 0:w], lhsT=u_incl[:], rhs=sp, start=first, stop=False,
                                                                             skip_group_check=True),
                                 r=[s2.buf, bconst], w=[ibk.buf])
                            yield
                            S.op("act", lambda e, ex=ex: e.activation(out=ex, in_=ibk.ap[:, 0:w], func=AF.Exp, scale=-1.0),
                                 r=[ibk.buf], w=[s2.buf])
                            S.op("dve", lambda e, ee=ee, ex=ex, wt=wt: e.tensor_tensor(out=wt, in0=ee, in1=ex, op=ALU.mult),
                                 r=[s1.buf, s2.buf], w=[s1.buf])
                            yield
                            if kb > 0:
                                S.op("pe", lambda e, sp=sp: e.matmul(ibk.ap[:, 0:w], lhsT=u_lt[:], rhs=sp, start=False, stop=False,
                                                                     skip_group_check=True),
                                     r=[s2.buf, bconst], w=[ibk.buf])
                            S.op("pe", lambda e, wt=wt, kb=kb, first=first: e.matmul(
                                obk.ap[po:po + 64, 0:w], lhsT=VV[:, kb, hd * 64:(hd + 1) * 64], rhs=wt, start=first, stop=(kb == 0),
                                skip_group_check=True), r=[s1.buf] + kvbufs, w=[obk.buf])
                            B.release([s1, s2])
                            first = False
                            yield

                    for rnd in range(3):
                        heads = [rnd * 4 + i for i in range(4)]
                        obks = [banks[4], banks[5]]
                        zb = [banks[6], banks[7]]
                        gens = [head_stream(hd, banks[i], obks[i // 2], zb[i % 2]) for i, hd in enumerate(heads)]
                        alive = list(gens)
                        for g_ in gens[2:]:
                            next(g_)
                            next(g_)
                        while alive:
                            nxt = []
                            for g in alive:
                                try:
                                    next(g)
                                    nxt.append(g)
                                except StopIteration:
                                    pass
                            alive = nxt
                        for i in range(2):
                            cch = rnd * 2 + i
                            en = B.ev()
                            S.op(en, _copy(en, mix[cch].r(w), obks[i].ap[:, 0:w]), r=[obks[i].buf], w=[mix[cch].buf])
                    B.release(qT)
                mem_attention(qm, w, lambda c, l=l: mkT[:, l, c, :], lambda mh, l=l: mvv[:, l, mh, :], [bmk[l], bmv[l]], mix[CC:CC + 2])
                B.release(qm)
                if DBG_TAP is not None and DBG_TAP == (ti, l):
                    store_T(mix, w, dbg, D)
                proj_post_add(xT, w, mix, w_out[l], "mix_post", l * KC)
                B.release(mix)
                ffn(xT, w, l)
                if l == NA - 1:
                    hk, rstd = rmsnorm(xT, w, "kv_g", 0)
                    B.release(rstd)

                    def cons_kT(oc, bk, ti=ti, t0=t0):
                        en = B.ev()
                        S.op(en, _copy(en, KT[:, oc, t0:t0 + w], bk.ap[:, 0:w]), r=[bk.buf], w=[bKT[ti]])
                    linear(hk, w, w_kv, 0, CC, cons_kT)
                    for grp in range(6):
                        def cons_kv(s, n, bk, grp=grp, ti=ti, t0=t0):
                            stg = B.alloc()
                            S.op("act", lambda e: e.activation(out=stg.f(256), in_=bk.ap[:, 0:256], func=AF.Copy), r=[bk.buf], w=[stg.buf])
                            r0 = t0 + s * 128
                            dstd = k_prompt if grp < 3 else v_prompt
                            c0 = (grp % 3) * 256
                            S.op("sp", lambda e: e.dma_start(out=dstd[r0:r0 + 128, c0:c0 + 256], in_=stg.f(256)), r=[stg.buf], dma=True)
                            if grp >= 3:
                                S.op("dve", lambda e: e.tensor_copy(out=VV[:, ti * 4 + s, c0:c0 + 256], in_=stg.f(256)), r=[stg.buf], w=[bVV[ti]])
                            B.release(stg)
                        linear_tok(hk, w, w_kv, grp * 256, 256, cons_kv)
                    B.release(hk)
            store_T(xT, w, y_prompt[t0:t0 + w, :], D)


        if SAMPLE:
            w = NS
            xsT = []
            for kc in range(KC):
                sx = Slot(B.sb(f"xsT{kc}", [128, 16], F32), 300 + kc)
                sx.pool = []
                xsT.append(sx)
            idx = B.sb("pidx", [128, 2], I32)
            bidx = Buf("pidx")
            for g in range(2):
                S.op("sp", lambda e, g=g: e.dma_start(out=idx[:, g:g + 1], in_=ptab[g * 128:(g + 1) * 128, :]), w=[bidx], dma=True)
            idxf = B.sb("pidxf", [128, 2], F32)
            iof = B.sb("iotaf", [128, PAGE // 2], F32)
            idxallf = B.sb("idxallf", [128, 2, PAGE // 2], F32)
            idxall = B.sb("idxall", [128, 2, PAGE // 2], I32)
            S.op("pool", lambda e: e.iota(out=iof[:], pattern=[[1, PAGE // 2]], base=0, channel_multiplier=0,
                                          allow_small_or_imprecise_dtypes=True), w=[bconst])
            S.op("dve", lambda e: e.tensor_copy(out=idxf[:], in_=idx[:]), r=[bidx], w=[bidx])
            S.op("dve", lambda e: e.tensor_scalar(out=idxf[:], in0=idxf[:], scalar1=float(PAGE // 2), scalar2=None, op0=ALU.mult), r=[bidx], w=[bidx])
            for g in range(2):
                S.op("dve", lambda e, g=g: e.tensor_scalar(out=idxallf[:, g, :], in0=iof[:], scalar1=idxf[:, g:g + 1], scalar2=None, op0=ALU.add),
                     r=[bidx, bconst], w=[bidx])
            S.op("dve", lambda e: e.tensor_copy(out=idxall[:], in_=idxallf[:]), r=[bidx], w=[bidx])
            ck2 = cache_k.rearrange("n (ch f) -> (n ch) f", f=2 * C)
            cv2 = cache_v.rearrange("n (ch f) -> (n ch) f", f=2 * C)
            tri = B.sb("tri", [128, 128], F32)
            sel = B.sb("sel", [128, 2, NS], F32)
            negc = B.sb("negc", [128, 16], F32)
            bnegc = Buf("negc")
            S.op("pool", lambda e: e.memset(tri[:], 1.0), w=[bconst])
            S.op("pool", lambda e: e.affine_select(out=tri[:], in_=tri[:], pattern=[[-1, 128]], compare_op=ALU.is_gt, fill=0.0,
                                                   base=0, channel_multiplier=1), r=[bconst], w=[bconst])
            S.op("pool", lambda e: e.memset(tri[64:128, 0:64], 0.0), r=[bconst], w=[bconst])
            S.op("pool", lambda e: e.memset(sel[:], 0.0), r=[bconst], w=[bconst])
            for g in range(2):
                S.op("pool", lambda e, g=g: e.memset(sel[0:64, g, 2 * g:2 * g + 1], 1.0), r=[bconst], w=[bconst])
                S.op("pool", lambda e, g=g: e.memset(sel[64:128, g, 2 * g + 1:2 * g + 2], 1.0), r=[bconst], w=[bconst])
            KTf = KT[:].rearrange("p c t -> p (c t)").bitcast(F32)
            VVf = VV[:].rearrange("p t c -> p (t c)").bitcast(F32)
            Kb = [KTf[:, 0:1536], KTf[:, 1536:3072]]
            EE = KTf[:, 3072:4608]
            SX = KTf[:, 4608:6144]
            Vb = [VVf[:, 0:1536], VVf[:, 1536:3072]]
            ring = Kb + Vb
            uxf = uext[:].rearrange("p c t -> p (c t)").bitcast(BF16)
            vbb = [uxf[:, i * 1536:(i + 1) * 1536] for i in range(4)]
            bvbb = [Buf(f"vbb{i}") for i in range(4)]
            sel_b = B.sb("sel_b", [128, 2, NS], BF16)
            S.op("pool", lambda e: e.tensor_copy(out=sel_b[:], in_=sel[:]), r=[bconst], w=[bconst])
            S2 = VVf[:, 3072:4608]
            QB = VVf[:, 4608:5376]
            bKb = [Buf("Kb0"), Buf("Kb1")]
            bVb = [Buf("Vb0"), Buf("Vb1")]
            bring = bKb + bVb
            bEE, bSX, bS2, bQB = Buf("EE"), Buf("SX"), Buf("S2"), Buf("QB")
            S.op("pool", lambda e: e.memset(QB, 0.0), w=bKb + bVb + bvbb + [bEE, bSX, bS2, bQB] + bKT + bVV + buext)

            load_xT(xs, NS, NS, xT=xsT)
            for l in range(L):
                h, rstd = rmsnorm(xsT, w, "mix_pre", l * KC)
                B.release(rstd)
                mix = B.allocR(KC)
                qm = B.alloc(2)
                if l < NA:
                    sig = B.alloc(CC)
                    us = B.alloc(CC)

                    def cons_gate(oc, bk, sig=sig):
                        S.op("act", lambda e: e.activation(out=sig[oc].f(w), in_=bk.ap[:, 0:w], func=AF.Sigmoid), r=[bk.buf], w=[sig[oc].buf])
                    linear(h, w, w_in_a[l], C, CC, cons_gate)

                    def cons_a(oc, bk, sig=sig, us=us):
                        S.op("dve", lambda e: e.tensor_tensor(out=us[oc].f(w), in0=sig[oc].f(w), in1=bk.ap[:, 0:w], op=ALU.mult),
                             r=[sig[oc].buf, bk.buf], w=[us[oc].buf])
                    linear(h, w, w_in_a[l], 0, CC, cons_a)
                    B.release(sig)

                    def cons_qm(oc, bk, qm=qm):
                        en = B.ev()
                        S.op(en, _copy(en, qm[oc].b(w), bk.ap[:, 0:w]), r=[bk.buf], w=[qm[oc].buf])
                    linear(h, w, w_in_a[l], 2 * C, 2, cons_qm)
                    B.release(h)
                    S.op("sp", lambda e, l=l: e.dma_start(out=conv_sample[l, :, 0:CW - 2, :], in_=state_conv[l, :, 1:CW - 1, :]), dma=True)
                    store_T(us, w, conv_sample[l, :, CW - 2, :], C)
                    stc = B.alloc(CC)
                    for b in range(NS):
                        stg = B.alloc(2)
                        S.op("sp", lambda e, l=l, b=b, stg=stg: e.dma_start(out=stg[0].f()[0:CW - 1, :], in_=state_conv[l, b, :, 0:512]),
                             w=[stg[0].buf], dma=True)
                        S.op("sp", lambda e, l=l, b=b, stg=stg: e.dma_start(out=stg[1].f()[0:CW - 1, 0:256], in_=state_conv[l, b, :, 512:768]),
                             w=[stg[1].buf], dma=True)
                        bka, bkb = B.bank(), B.bank()

                        def trs(e, stg=stg, bka=bka):
                            ins = None
                            for c in range(4):
                                ins = e.transpose(out=bka.ap[:, c * 32:c * 32 + 30], in_=stg[0].f()[0:CW - 1, c * 128:(c + 1) * 128],
                                                  identity=ident[0:CW - 1, 0:CW - 1])
                            return ins

                        def trs2(e, stg=stg, bkb=bkb):
                            ins = None
                            for c in range(2):
                                ins = e.transpose(out=bkb.ap[:, c * 32:c * 32 + 30], in_=stg[1].f()[0:CW - 1, c * 128:(c + 1) * 128],
                                                  identity=ident[0:CW - 1, 0:CW - 1])
                            return ins
                        S.op("pe", trs, r=[stg[0].buf, bconst], w=[bka.buf])
                        S.op("pe", trs2, r=[stg[1].buf, bconst], w=[bkb.buf])
                        for c in range(CC):
                            bk_ = bka if c < 4 else bkb
                            cl = c % 4
                            en = B.ev()
                            S.op(en, _copy(en, stc[c].f(120)[:, b * 30:(b + 1) * 30], bk_.ap[:, cl * 32:cl * 32 + 30]), r=[bk_.buf], w=[stc[c].buf])
                        B.release(stg)
                    yc = []
                    yr = B.allocR(CC)
                    o_w = voff["conv_w"] + l * CW * CC
                    for c in range(CC):
                        tmp = B.alloc()
                        ysm = B.alloc()
                        yc.append(B.alloc())
                        wck = G[:, o_w:o_w + CW * CC].rearrange("p (k c) -> p c k", c=CC)[:, c, 0:CW - 1]
                        stv = stc[c].f(120).rearrange("p (b k) -> p b k", b=NS)
                        tv = tmp.f(120).rearrange("p (b k) -> p b k", b=NS)
                        S.op("dve", lambda e, tv=tv, stv=stv, wck=wck: e.tensor_tensor(
                            out=tv, in0=stv, in1=wck.unsqueeze(1).broadcast_to([128, NS, CW - 1]), op=ALU.mult),
                            r=[stc[c].buf, bG], w=[tmp.buf])
                        S.op("dve", lambda e, tv=tv, ysm=ysm: e.reduce_sum(out=ysm.f(w), in_=tv, axis=AX.X), r=[tmp.buf], w=[ysm.buf])
                        S.op("dve", lambda e, c=c, l=l, ysm=ysm: e.scalar_tensor_tensor(
                            out=yc[c].f(w), in0=us[c].f(w), scalar=gcol("conv_w", (l * CW + CW - 1) * CC + c), in1=ysm.f(w),
                            op0=ALU.mult, op1=ALU.add), r=[us[c].buf, ysm.buf, bG], w=[yc[c].buf])
                        S.op("dve", lambda e, c=c, l=l: e.tensor_scalar(out=yr[c].r(w), in0=yc[c].f(w), scalar1=gcol("conv_b", l * CC + c),
                                                                        scalar2=None, op0=ALU.add), r=[yc[c].buf, bG], w=[yr[c].buf])
                        B.release([tmp, ysm, stc[c], us[c]])
                    conv_ln(yr, yc, w, l, mix)
                    B.release(yc)
                    B.release(yr)
                else:
                    lb = l - NA
                    qs = B.alloc(CC)

                    def cons_q(oc, bk, qs=qs, qm=qm):
                        en = B.ev()
                        if oc < CC:
                            S.op(en, _copy(en, qs[oc].f(w), bk.ap[:, 0:w]), r=[bk.buf], w=[qs[oc].buf])
                        else:
                            S.op(en, _copy(en, qm[oc - CC].b(w), bk.ap[:, 0:w]), r=[bk.buf], w=[qm[oc - CC].buf])
                    linear(h, w, w_in_b[lb], 0, CC + 2, cons_q)
                    B.release(h)
                    EEv = EE.rearrange("p (t h) -> p t h", h=NH)
                    ob = [banks[6], banks[7]]
                    nmm = 0
                    for g in range(2):
                        bqa, bqb = B.bank(), B.bank()
                        for c in range(CC):
                            qrep = B.alloc()
                            for hb in range(2):
                                S.op("dve", lambda e, c=c, hb=hb, g=g, qrep=qrep: e.tensor_scalar(
                                    out=qrep.f(128)[:, hb * 64:(hb + 1) * 64], in0=ones_f[:, 0:64],
                                    scalar1=qs[c].f(w)[:, 2 * g + hb:2 * g + hb + 1], scalar2=None, op0=ALU.mult),
                                    r=[qs[c].buf, bconst], w=[qrep.buf])
                            bq = bqa if c < 4 else bqb
                            S.op("pe", lambda e, c=c, bq=bq, qrep=qrep: e.matmul(bq.ap[:, (c % 4) * 128:(c % 4 + 1) * 128], lhsT=qrep.f(128),
                                                                                 rhs=ident[:], start=True, stop=True, skip_group_check=True),
                                 r=[qrep.buf, bconst], w=[bq.buf])
                            B.release(qrep)
                        S.op("dve", lambda e, bqa=bqa: e.tensor_copy(out=QB[:, 0:512], in_=bqa.ap[:, 0:512]), r=[bqa.buf], w=[bQB])
                        S.op("dve", lambda e, bqb=bqb: e.tensor_copy(out=QB[:, 512:768], in_=bqb.ap[:, 0:256]), r=[bqb.buf], w=[bQB])
                        def k_gather(ch2, g=g):
                            S.op("pool", lambda e: e.indirect_dma_start(
                                out=ring[ch2 % 4], out_offset=None, in_=ck2[:, :],
                                in_offset=bass.IndirectOffsetOnAxis(ap=idxall[:, g, ch2:ch2 + 1], axis=0)), r=[bidx], w=[bring[ch2 % 4]], dma=True)
                        for ch2 in range(3):
                            k_gather(ch2)
                        for ch in range(PAGE // 2):
                            t0 = 2 * ch
                            kb, bkb_ = ring[ch % 4], bring[ch % 4]
                            if ch + 3 < PAGE // 2:
                                k_gather(ch + 3)
                            kv3 = kb.rearrange("p (t f) -> p t f", t=2)
                            S.op("pool", lambda e, kv3=kv3: e.tensor_tensor(out=kv3, in0=kv3, in1=QB.unsqueeze(1).broadcast_to([128, 2, C]),
                                                                            op=ALU.mult), r=[bkb_, bQB], w=[bkb_])
                            S.op("dve", lambda e, kb=kb, t0=t0: e.reduce_sum(out=EE[:, t0 * NH:(t0 + 2) * NH],
                                                                             in_=kb.rearrange("p (a d) -> p a d", d=64), axis=AX.X),
                                 r=[bkb_], w=[bEE])
                        for hd in range(NH):
                            S.op("act", lambda e, hd=hd: e.activation(out=EEv[:, :, hd], in_=EEv[:, :, hd], func=AF.Exp,
                                                                      bias=sbb[:, lb * NH + hd:lb * NH + hd + 1], scale=0.125),
                                 r=[bEE, bsbb], w=[bEE])
                        S.op("act", lambda e: e.activation(out=SX, in_=EE, func=AF.Ln, bias=1.0, scale=1.0), r=[bEE], w=[bSX])
                        cur, bcur, nxt, bnxt = SX, bSX, S2, bS2
                        for sft in (1, 2, 4, 8, 16, 32, 64):
                            cv = cur.rearrange("p (t h) -> p t h", h=NH)
                            nv = nxt.rearrange("p (t h) -> p t h", h=NH)
                            S.op("dve", lambda e, cv=cv, nv=nv, sft=sft: e.tensor_tensor(out=nv[:, 0:PAGE - sft, :], in0=cv[:, 0:PAGE - sft, :],
                                                                                       in1=cv[:, sft:PAGE, :], op=ALU.add),
                                 r=[bcur], w=[bnxt])
                            S.op("pool", lambda e, cv=cv, nv=nv, sft=sft: e.tensor_copy(out=nv[:, PAGE - sft:PAGE, :], in_=cv[:, PAGE - sft:PAGE, :]),
                                 r=[bcur], w=[bnxt])
                            cur, bcur, nxt, bnxt = nxt, bnxt, cur, bcur
                        cv = cur.rearrange("p (t h) -> p t h", h=NH)
                        nv = nxt.rearrange("p (t h) -> p t h", h=NH)
                        bc = B.bank()
                        S.op("pe", lambda e, bc=bc, cv=cv: e.matmul(bc.ap[:, 0:NH], lhsT=tri[:], rhs=cv[:, 0, :], start=True, stop=True),
                             r=[bcur, bconst], w=[bc.buf])
                        S.op("dve", lambda e, bc=bc: e.tensor_scalar(out=negc[:, 0:NH], in0=bc.ap[:, 0:NH], scalar1=-1.0, scalar2=None, op0=ALU.mult),
                             r=[bc.buf], w=[bnegc])
                        for hd in range(NH):
                            S.op("act", lambda e, hd=hd, cv=cv, nv=nv: e.activation(out=nv[:, :, hd], in_=cv[:, :, hd], func=AF.Exp, scale=-1.0,
                                                                                   bias=negc[:, hd:hd + 1]), r=[bcur, bnegc], w=[bnxt])
                        S.op("dve", lambda e, nxt=nxt: e.tensor_tensor(out=EE, in0=EE, in1=nxt, op=ALU.mult), r=[bEE, bnxt], w=[bEE])
                        for ch in range(PAGE // 2):
                            t0 = 2 * ch
                            vb, bvb_ = ring[ch % 4], bring[ch % 4]
                            vq, bvq = vbb[ch % 4], bvbb[ch % 4]
                            S.op("pool", lambda e, vb=vb, ch=ch, g=g: e.indirect_dma_start(
                                out=vb, out_offset=None, in_=cv2[:, :],
                                in_offset=bass.IndirectOffsetOnAxis(ap=idxall[:, g, ch:ch + 1], axis=0)), r=[bidx], w=[bvb_], dma=True)
                            v4 = vb.rearrange("p (t h d) -> p t h d", t=2, h=NH)
                            q4 = vq.rearrange("p (t h d) -> p t h d", t=2, h=NH)
                            S.op("dve", lambda e, v4=v4, q4=q4, t0=t0: e.tensor_tensor(
                                out=q4, in0=v4, in1=EEv[:, t0:t0 + 2, :].unsqueeze(3).broadcast_to([128, 2, NH, 64]), op=ALU.mult),
                                r=[bvb_, bEE], w=[bvq])

                            def pvmm(e, vq=vq, g=g, first=(nmm == 0), last=(g == 1 and ch == PAGE // 2 - 1)):
                                ins = None
                                for t in range(2):
                                    for hf in range(2):
                                        ins = e.matmul(ob[hf].ap[0:NS, 0:384], lhsT=sel_b[:, g, :], rhs=vq[:, t * C + hf * 384:t * C + (hf + 1) * 384],
                                                       start=(first and t == 0), stop=(last and t == 1), skip_group_check=True)
                                return ins
                            S.op("pe", pvmm, r=[bvq, bconst], w=[ob[0].buf, ob[1].buf])
                            nmm += 1
                    B.release(qs)
                    osb = B.alloc(2)
                    for hf in range(2):
                        S.op("dve", lambda e, hf=hf: e.tensor_copy(out=osb[hf].f()[0:NS, 0:384], in_=ob[hf].ap[0:NS, 0:384]), r=[ob[hf].buf], w=[osb[hf].buf])
                    bt = B.bank()

                    def tro(e, bt=bt, osb=osb):
                        ins = None
                        for c in range(CC):
                            ins = e.transpose(out=bt.ap[:, c * NS:(c + 1) * NS], in_=osb[c // 3].f()[0:NS, (c % 3) * 128:(c % 3 + 1) * 128],
                                              identity=ident[0:NS, 0:NS])
                        return ins
                    S.op("pe", tro, r=[osb[0].buf, osb[1].buf, bconst], w=[bt.buf])
                    for c in range(CC):
                        S.op("dve", lambda e, c=c, bt=bt: e.tensor_copy(out=mix[c].r(w), in_=bt.ap[:, c * NS:(c + 1) * NS]), r=[bt.buf], w=[mix[c].buf])
                    B.release(osb)
                for b in range(NS):
                    stgk, stgv, mks, mvs = B.alloc(), B.alloc(), B.alloc(), B.alloc()
                    S.op("sp", lambda e, l=l, b=b, stgk=stgk: e.dma_start(out=stgk.f().rearrange("p (mh f) -> p mh f", mh=2),
                                                                         in_=cmk[l, b].rearrange("(mh p) f -> p mh f", p=128)), w=[stgk.buf], dma=True)
                    S.op("sp", lambda e, l=l, b=b, stgv=stgv: e.dma_start(out=stgv.f().rearrange("p (mh f) -> p mh f", mh=2),
                                                                         in_=cmv[l, b].rearrange("(mh p) f -> p mh f", p=128)), w=[stgv.buf], dma=True)
                    for c in range(2):
                        bk = B.bank()

                        def trk(e, c=c, bk=bk, stgk=stgk):
                            ins = None
                            for mh in range(2):
                                ins = e.transpose(out=bk.ap[:, mh * 128:(mh + 1) * 128], in_=stgk.f()[:, mh * 256 + c * 128:mh * 256 + (c + 1) * 128],
                                                  identity=ident[:])
                            return ins
                        S.op("pe", trk, r=[stgk.buf, bconst], w=[bk.buf])
                        en = B.ev()
                        S.op(en, _copy(en, mks.b()[:, c * 256:(c + 1) * 256], bk.ap[:, 0:256]), r=[bk.buf], w=[mks.buf])
                    S.op("pool", lambda e, stgv=stgv, mvs=mvs: e.tensor_copy(out=mvs.b(), in_=stgv.f()), r=[stgv.buf], w=[mvs.buf])
                    mem_attention(qm, w, lambda c, mks=mks: mks.b()[:, c * 256:(c + 1) * 256], lambda mh, mvs=mvs: mvs.b()[:, mh * 256:(mh + 1) * 256],
                                  [mks.buf, mvs.buf], mix[CC:CC + 2], cols=[(b, 1)])
                    B.release([stgk, stgv, mks, mvs])
                B.release(qm)
                proj_post_add(xsT, w, mix, w_out[l], "mix_post", l * KC)
                B.release(mix)
                ffn(xsT, w, l)
                if l == NA - 1:
                    hk, rstd = rmsnorm(xsT, w, "kv_g", 0)
                    B.release(rstd)
                    for grp in range(6):
                        def cons_kvs(s_, n, bk, grp=grp):
                            stg = B.alloc()
                            S.op("act", lambda e: e.activation(out=stg.f(256)[0:n, :], in_=bk.ap[0:n, 0:256], func=AF.Copy), r=[bk.buf], w=[stg.buf])
                            dstd = k_sample if grp < 3 else v_sample
                            c0 = (grp % 3) * 256
                            S.op("sp", lambda e: e.dma_start(out=dstd[0:n, c0:c0 + 256], in_=stg.f(256)[0:n, :]), r=[stg.buf], dma=True)
                            B.release(stg)
                        linear_tok(hk, w, w_kv, grp * 256, 256, cons_kvs)
                    B.release(hk)
            store_T(xsT, w, y_sample, D)

        S.emit(st)
    return nc


_NC_CACHE = {}


def kernel(**inputs):
    n = 8
    if "nc" not in _NC_CACHE:
        _NC_CACHE["nc"] = build_program()
    nc = _NC_CACHE["nc"]
    f = lambda a: np.ascontiguousarray(np.asarray(a, dtype=np.float32))
    if SAMPLE:
        ck = f(inputs["cache_k"]).reshape(NPHYS, PAGE * C)
        cv = f(inputs["cache_v"]).reshape(NPHYS, PAGE * C)
    shared = {
        "norm_mix_pre": f(inputs["norm_mix_pre"]), "norm_mix_post": f(inputs["norm_mix_post"]),
        "norm_ffn_pre": f(inputs["norm_ffn_pre"]), "norm_ffn_post": f(inputs["norm_ffn_post"]),
        "w_in_a": f(inputs["w_in_a"]), "conv_w": f(inputs["conv_w"]), "conv_b": f(inputs["conv_b"]),
        "conv_ln_g": f(inputs["conv_ln_g"]), "conv_ln_b": f(inputs["conv_ln_b"]),
        "w_in_b": f(inputs["w_in_b"]), "sb_bias": f(inputs["sb_bias"]).reshape(1, -1),
        "kv_norm_g": f(inputs["kv_norm_g"]).reshape(1, D), "w_kv": f(inputs["w_kv"]),
        "mem_norm_g": f(inputs["mem_norm_g"]), "w_mem_kv": f(inputs["w_mem_kv"]), "w_out": f(inputs["w_out"]),
        "w_ffn_up": f(inputs["w_ffn_up"]), "w_ffn_down": f(inputs["w_ffn_down"]),
    }
    xpr = f(inputs["x_prompt"])
    xsm = f(inputs["x_sample"])
    stc = f(inputs["state_conv"])
    cmk = f(inputs["cache_mem_k"])
    cmv = f(inputs["cache_mem_v"])
    pt = np.ascontiguousarray(np.asarray(inputs["page_table"], dtype=np.int32))
    mp = f(inputs["mem_prompt"])
    in_maps = []
    for c in range(n):
        m = dict(shared)
        m["xp"] = xpr[c]
        if SAMPLE:
            m["cache_k"] = ck
            m["cache_v"] = cv
            m["xs"] = np.ascontiguousarray(xsm[4 * c:4 * c + 4, 0, :])
            m["state_conv"] = np.ascontiguousarray(stc[:, 4 * c:4 * c + 4])
            m["cmk"] = np.ascontiguousarray(cmk[:, 4 * c:4 * c + 4].reshape(L, NS, NMEM, 256))
            m["cmv"] = np.ascontiguousarray(cmv[:, 4 * c:4 * c + 4].reshape(L, NS, NMEM, 256))
            m["ptab"] = np.ascontiguousarray(pt[4 * c:4 * c + 4].reshape(NS * NPAGE, 1))
        m["memp"] = mp[c]
        in_maps.append(m)
    if DBG_CORES < n:
        res = run_bass_kernel_spmd(nc, in_maps[:DBG_CORES], core_ids=list(range(DBG_CORES)))
        R = list(res.results) + [res.results[0]] * (n - DBG_CORES)
    else:
        res = run_bass_kernel_spmd(nc, in_maps, core_ids=list(range(n)))
        R = res.results
    y_prompt = np.stack([R[c]["y_prompt"] for c in range(n)], 0)
    y_sample = np.concatenate([R[c]["y_sample"] for c in range(n)], 0).reshape(32, 1, D)
    k_prompt = np.stack([R[c]["k_prompt"] for c in range(n)], 0).reshape(n, T, NH, 64)
    v_prompt = np.stack([R[c]["v_prompt"] for c in range(n)], 0).reshape(n, T, NH, 64)
    conv_prompt = np.stack([R[c]["conv_prompt"] for c in range(n)], 1)
    mk = np.stack([R[c]["mk_p"] for c in range(n)], 1).reshape(L, n, NMEM, 4, 64)
    mv = np.stack([R[c]["mv_p"] for c in range(n)], 1).reshape(L, n, NMEM, 4, 64)
    k_sample = np.concatenate([R[c]["k_sample"] for c in range(n)], 0).reshape(32, 1, NH, 64)
    v_sample = np.concatenate([R[c]["v_sample"] for c in range(n)], 0).reshape(32, 1, NH, 64)
    conv_sample = np.concatenate([R[c]["conv_sample"] for c in range(n)], 1)
    return (y_prompt, y_sample, k_prompt, v_prompt, conv_prompt, mk, mv, k_sample, v_sample, conv_sample)
```

```python
import contextlib
import math
import sys
import numpy as np
import concourse.bass as bass
import concourse.mybir as mybir
from concourse.bass_utils import run_bass_kernel_spmd

F32 = mybir.dt.float32
F32R = mybir.dt.float32r
BF16 = mybir.dt.bfloat16
I32 = mybir.dt.int32
AF = mybir.ActivationFunctionType
ALU = mybir.AluOpType
AX = mybir.AxisListType

ENGS = ("pe", "act", "dve", "pool", "sp")
DBG_SITES = None
SEM_EPOCH = 24000
N_DMA_SEMS = {"sp": 32, "pool": 16, "act": 4, "pe": 1, "dve": 1}


class Buf:
    __slots__ = ("name", "w", "r", "excl")

    def __init__(self, name, excl=False):
        self.name = name
        self.w = None
        self.r = []
        self.excl = excl


class Op:
    __slots__ = ("eng", "calls", "deps", "raw", "idx", "dma", "sig", "sem", "val", "site")


class _Rec:
    def __init__(self):
        self.calls = []

    def __getattr__(self, name):
        def f(*a, **k):
            self.calls.append((name, a, k))
            return None
        return f


class Sched:
    def __init__(self, nc):
        self.nc = nc
        self.q = {e: [] for e in ENGS}

    def op(self, eng, fn, r=(), w=(), dma=False):
        o = Op()
        rec = _Rec()
        fn(rec)
        o.site = sys._getframe(1).f_lineno
        o.eng, o.calls, o.dma = eng, rec.calls, dma
        o.sig = dma
        o.sem = o.val = o.raw = None
        xr = [b for b in r if b.excl and b not in w]
        if xr:
            r = [b for b in r if not b.excl]
            w = list(w) + xr
        deps, raw = set(), set()
        for b in r:
            if b.w is not None:
                deps.add(b.w)
                raw.add(b.w)
        for b in w:
            if b.w is not None:
                deps.add(b.w)
            deps.update(b.r)
        for b in r:
            b.r.append(o)
        for b in w:
            b.w = o
            b.r = []
        o.idx = len(self.q[eng])
        kept = []
        for d in deps:
            if d is o:
                continue
            if d.eng == eng and not d.dma:
                if eng != "pe":
                    kept.append(d)
            else:
                kept.append(d)
        for d in kept:
            d.sig = True
        o.deps = kept
        self.q[eng].append(o)
        return o

    def emit(self, stack):
        nc = self.nc
        for e in ENGS:
            cnt = 0
            sems = []
            for o in self.q[e]:
                if o.dma or not o.sig:
                    continue
                ep = cnt // SEM_EPOCH
                if ep >= len(sems):
                    sems.append(stack.enter_context(nc.semaphore(f"s_{e}{ep}")))
                o.sem = sems[ep]
                o.val = cnt - ep * SEM_EPOCH + 1
                cnt += 1
        final_waits = []
        for e in ENGS:
            n = sum(1 for o in self.q[e] if o.dma)
            if n == 0:
                continue
            ns = min(n, N_DMA_SEMS[e])
            dma_sems = [stack.enter_context(nc.semaphore(f"s_dma_{e}{i}")) for i in range(ns)]
            dma_tot = [0] * ns
            prev = {}
            rr = 0
            for o in self.q[e]:
                if o.dma:
                    i = rr % ns
                    rr += 1
                    o.sem = dma_sems[i]
                    dma_tot[i] += 16
                    o.val = dma_tot[i]
                    o.raw = prev.get(i)
                    prev[i] = o
            final_waits += [(dma_sems[i], dma_tot[i]) for i in range(ns)]

        def run(e, eng, last=False):
            waited = {}

            def wait(sem, val):
                k = id(sem)
                if waited.get(k, 0) >= val:
                    return
                waited[k] = val
                eng.wait_ge(sem, val)

            for o in self.q[e]:
                for d in o.deps:
                    wait(d.sem, d.val)
                if o.dma and o.raw is not None:
                    wait(o.raw.sem, o.raw.val)
                ins = None
                for name, a, k in o.calls:
                    ins = getattr(eng, name)(*a, **k)
                    if DBG_SITES is not None:
                        try:
                            DBG_SITES[ins.ins.name] = o.site
                        except Exception:
                            pass
                if o.sig:
                    ins.then_inc(o.sem, 16 if o.dma else 1)
            if last:
                for sem, val in final_waits:
                    wait(sem, val)

        with nc.Block() as block:
            @block.tensor
            def _(eng):
                run("pe", eng)

            @block.scalar
            def _(eng):
                run("act", eng)

            @block.vector
            def _(eng):
                run("dve", eng)

            @block.gpsimd
            def _(eng):
                run("pool", eng)

            @block.sync
            def _(eng):
                run("sp", eng, last=True)


D = 1024
KC = 8
T = 2048
TW = 512
NTILE = 4
C = 768
CC = 6
CW = 31
FF = 2816
FC = 22
NMEM = 256
L = 4
NA = 2
NH = 12
NS = 4
NPAGE = 64
PAGE = 128
NPHYS = 2560
RMS_EPS = 1e-6
LN_EPS = 1e-5
NR = 22
NP = 17
WBUF_COLS = 2048
NWBUF = 4
SAMPLE = True
DBG_NTILE = NTILE
DBG_L = L
DBG_TAP = None
DBG_STAGE = 9
DBG_CORES = 8
DBG_SUB = 9


class Slot:
    def __init__(self, t, i):
        self.t = t
        self.buf = Buf(f"slot{i}")

    def f(self, w=512):
        return self.t[:, 0:w]

    def r(self, w=512):
        return self.t[:, 0:w].bitcast(F32R)

    def b(self, w=512):
        return self.t[:].bitcast(BF16)[:, 0:w]

    def b2(self, w=512):
        return self.t[:].bitcast(BF16)[:, 512:512 + w]


class Bank:
    def __init__(self, t, i):
        self.t = t
        self.ap = t[:]
        self.buf = Buf(f"bank{i}", excl=True)


class WB:
    def __init__(self, t, i):
        self.t = t
        self.buf = Buf(f"wb{i}")


class Builder:
    def __init__(self, nc, st):
        self.nc = nc
        self.st = st
        self.S = Sched(nc)
        self.free = []
        self.freeR = []
        for i in range(NP):
            s = Slot(self.sb(f"sl{i}", [128, 512], F32), i)
            s.pool = self.free
            self.free.append(s)
        for i in range(NR):
            s = Slot(self.sb(f"sr{i}", [128, 512], F32), 100 + i)
            s.pool = self.freeR
            self.freeR.append(s)
        self.banks = [Bank(st.enter_context(nc.psum_tensor(f"ps{i}", [128, 512], F32)), i) for i in range(8)]
        self.rb = 0
        self.wbufs = [WB(self.sb(f"wb{i}", [128, WBUF_COLS], F32R), i) for i in range(NWBUF)]
        self.wbi = 0
        self.tog = 0

    def sb(self, name, shape, dt):
        return self.st.enter_context(self.nc.sbuf_tensor(name, shape, dt))

    def alloc(self, n=None):
        if n is None:
            return self.free.pop(0)
        return [self.free.pop(0) for _ in range(n)]

    def allocR(self, n=None):
        if n is None:
            return self.freeR.pop(0)
        return [self.freeR.pop(0) for _ in range(n)]

    def release(self, s):
        if isinstance(s, (list, tuple)):
            for x in s:
                x.pool.append(x)
        else:
            s.pool.append(s)

    def bank(self, lo=0, hi=6):
        b = self.banks[lo + self.rb % (hi - lo)]
        self.rb += 1
        return b

    def wbuf(self):
        w = self.wbufs[self.wbi % NWBUF]
        self.wbi += 1
        return w

    def ev(self):
        self.tog ^= 1
        return "act" if self.tog else "dve"


def _copy(eng_name, out, in_):
    if eng_name == "act":
        return lambda e: e.activation(out=out, in_=in_, func=AF.Copy)
    return lambda e: e.tensor_copy(out=out, in_=in_)


def build_program():
    nc = bass.Bass("TRN2", target_bir_lowering=False)
    nc.dge_precook = False
    dt_in = {}

    def din(name, shape, dt=F32):
        dt_in[name] = nc.dram_tensor(name, list(shape), dt, kind="ExternalInput").ap()
        return dt_in[name]

    def dout(name, shape):
        return nc.dram_tensor(name, list(shape), F32, kind="ExternalOutput").ap()

    xp = din("xp", [T, D])
    if SAMPLE:
        xs = din("xs", [NS, D])
        cache_k = din("cache_k", [NPHYS, PAGE * C])
        cache_v = din("cache_v", [NPHYS, PAGE * C])
        state_conv = din("state_conv", [NA, NS, CW - 1, C])
        cmk = din("cmk", [L, NS, NMEM, 256])
        cmv = din("cmv", [L, NS, NMEM, 256])
        ptab = din("ptab", [NS * NPAGE, 1], I32)
    memp = din("memp", [NMEM, D])
    g_mix_pre = din("norm_mix_pre", [L, D])
    g_mix_post = din("norm_mix_post", [L, D])
    g_ffn_pre = din("norm_ffn_pre", [L, D])
    g_ffn_post = din("norm_ffn_post", [L, D])
    w_in_a = din("w_in_a", [NA, D, 2 * C + 256], F32R)
    conv_w = din("conv_w", [NA, CW, C])
    conv_b = din("conv_b", [NA, C])
    conv_ln_g = din("conv_ln_g", [NA, C])
    conv_ln_b = din("conv_ln_b", [NA, C])
    w_in_b = din("w_in_b", [L - NA, D, D], F32R)
    sb_bias = din("sb_bias", [1, (L - NA) * NH])
    kv_norm_g = din("kv_norm_g", [1, D])
    w_kv = din("w_kv", [D, 2 * C], F32R)
    mem_norm_g = din("mem_norm_g", [L, D])
    w_mem_kv = din("w_mem_kv", [L, D, 512], F32R)
    w_out = din("w_out", [L, D, D], F32R)
    w_up = din("w_ffn_up", [L, D, 2 * FF], F32R)
    w_down = din("w_ffn_down", [L, FF, D], F32R)

    y_prompt = dout("y_prompt", [T, D])
    y_sample = dout("y_sample", [NS, D])
    k_prompt = dout("k_prompt", [T, C])
    v_prompt = dout("v_prompt", [T, C])
    conv_prompt = dout("conv_prompt", [NA, CW - 1, C])
    mk_p = dout("mk_p", [L, NMEM, 256])
    mv_p = dout("mv_p", [L, NMEM, 256])
    k_sample = dout("k_sample", [NS, C])
    v_sample = dout("v_sample", [NS, C])
    conv_sample = dout("conv_sample", [NA, NS, CW - 1, C])
    dbg = dout("dbg", [TW, D]) if DBG_TAP is not None else None

    with contextlib.ExitStack() as st:
        B = Builder(nc, st)
        S = B.S
        banks = B.banks

        ones_r = B.sb("ones_r", [128, 128], F32R)
        ones_f = B.sb("ones_f", [128, 128], F32)
        ident = B.sb("ident", [128, 128], F32)
        ident_b = B.sb("ident_b", [128, 128], BF16)
        u_incl = B.sb("u_incl", [128, 128], BF16)
        u_lt = B.sb("u_lt", [128, 128], BF16)
        masks = B.sb("masks", [128, 4, 512], BF16)
        bconst = Buf("const")
        S.op("pool", lambda e: e.memset(ones_f[:], 1.0), w=[bconst])
        S.op("dve", lambda e: e.tensor_copy(out=ones_r[:], in_=ones_f[:]), r=[bconst], w=[bconst])
        S.op("pool", lambda e: e.memset(ident[:], 0.0), w=[bconst])
        S.op("pool", lambda e: e.affine_select(out=ident[:], in_=ident[:], pattern=[[-1, 128]], compare_op=ALU.not_equal,
                                               fill=1.0, base=0, channel_multiplier=1), r=[bconst], w=[bconst])
        S.op("pool", lambda e: e.tensor_copy(out=ident_b[:], in_=ident[:]), r=[bconst], w=[bconst])
        S.op("pool", lambda e: e.memset(u_incl[:], 1.0), w=[bconst])
        S.op("pool", lambda e: e.affine_select(out=u_incl[:], in_=u_incl[:], pattern=[[-1, 128]], compare_op=ALU.is_ge,
                                               fill=0.0, base=0, channel_multiplier=1), r=[bconst], w=[bconst])
        S.op("pool", lambda e: e.memset(u_lt[:], 1.0), w=[bconst])
        S.op("pool", lambda e: e.affine_select(out=u_lt[:], in_=u_lt[:], pattern=[[1, 128]], compare_op=ALU.is_gt,
                                               fill=0.0, base=0, channel_multiplier=-1), r=[bconst], w=[bconst])
        S.op("pool", lambda e: e.memset(masks[:], 1.0), w=[bconst])
        for dd in range(4):
            S.op("pool", lambda e, dd=dd: e.affine_select(out=masks[:, dd, :], in_=masks[:, dd, :], pattern=[[1, 512]],
                                                          compare_op=ALU.is_gt, fill=0.0, base=-128 * dd,
                                                          channel_multiplier=-1), r=[bconst], w=[bconst])

        vec_srcs = [
            ("mix_pre", g_mix_pre.rearrange("l (kc p) -> (l kc) p", p=128)),
            ("mix_post", g_mix_post.rearrange("l (kc p) -> (l kc) p", p=128)),
            ("ffn_pre", g_ffn_pre.rearrange("l (kc p) -> (l kc) p", p=128)),
            ("ffn_post", g_ffn_post.rearrange("l (kc p) -> (l kc) p", p=128)),
            ("mem_g", mem_norm_g.rearrange("l (kc p) -> (l kc) p", p=128)),
            ("kv_g", kv_norm_g.rearrange("l (kc p) -> (l kc) p", p=128)),
            ("conv_b", conv_b.rearrange("l (cc p) -> (l cc) p", p=128)),
            ("ln_g", conv_ln_g.rearrange("l (cc p) -> (l cc) p", p=128)),
            ("ln_b", conv_ln_b.rearrange("l (cc p) -> (l cc) p", p=128)),
            ("conv_w", conv_w.rearrange("l k (cc p) -> (l k cc) p", p=128)),
        ]
        nvec = sum(a.shape[0] for _, a in vec_srcs)
        G = B.sb("G", [128, nvec], F32)
        bG = Buf("G")
        voff = {}
        vstage = B.sb("vstage", [128, 128], F32)
        bvs = Buf("vstage")
        off = 0
        for name, a in (vec_srcs if DBG_STAGE >= 1 else []):
            voff[name] = off
            nrows = a.shape[0]
            for r0 in range(0, nrows, 128):
                n = min(128, nrows - r0)
                bk = B.bank()
                S.op("sp", lambda e, a=a, r0=r0, n=n: e.dma_start(out=vstage[0:n, :], in_=a[r0:r0 + n, :]), w=[bvs], dma=True)
                S.op("pe", lambda e, bk=bk, n=n: e.transpose(out=bk.ap[:, 0:n], in_=vstage[0:n, :], identity=ident[0:n, 0:n]),
                     r=[bvs, bconst], w=[bk.buf])
                S.op("dve", lambda e, bk=bk, n=n, o=off + r0: e.tensor_copy(out=G[:, o:o + n], in_=bk.ap[:, 0:n]),
                     r=[bk.buf], w=[bG])
            off += nrows

        def gcol(name, idx):
            o = voff[name] + idx
            return G[:, o:o + 1]

        sbb_row = B.sb("sbb_row", [1, (L - NA) * NH], F32)
        sbb = B.sb("sbb", [128, (L - NA) * NH], F32)
        bsbb = Buf("sbb")
        if DBG_STAGE >= 2:
            S.op("sp", lambda e: e.dma_start(out=sbb_row[:], in_=sb_bias[:, :]), w=[bsbb], dma=True)
            bk = B.bank()
            S.op("pe", lambda e, bk=bk: e.matmul(bk.ap[:, 0:(L - NA) * NH], lhsT=ones_f[0:1, :], rhs=sbb_row[:], start=True, stop=True),
                 r=[bsbb, bconst], w=[bk.buf])
            S.op("dve", lambda e, bk=bk: e.tensor_copy(out=sbb[:], in_=bk.ap[:, 0:(L - NA) * NH]), r=[bk.buf], w=[bsbb])

        KT = B.sb("KT", [128, CC, T], BF16)
        VV = B.sb("VV", [128, T // 128, C], BF16)
        bKT = [Buf(f"KT{t}") for t in range(NTILE)]
        bVV = [Buf(f"VV{t}") for t in range(NTILE)]
        mkT = B.sb("mkT", [128, L, 2, NMEM], BF16)
        mvv = B.sb("mvv", [128, L, 2, 256], BF16)
        bmk = [Buf(f"mk{l}") for l in range(L)]
        bmv = [Buf(f"mv{l}") for l in range(L)]
        uext = B.sb("uext", [128, CC, 30 + TW], F32)
        buext = [Buf(f"uext{c}") for c in range(CC)]
        halo = B.sb("halo", [128, NA, CC, 30], F32)
        bhalo = [Buf(f"halo{l}") for l in range(NA)]
        small = B.sb("small", [128, 16], F32)
        bsmall = [Buf(f"small{i}") for i in range(4)]

        xres = []
        for kc in range(KC):
            sx = Slot(B.sb(f"xres{kc}", [128, 512], F32), 200 + kc)
            sx.pool = []
            xres.append(sx)

        def load_xT(rows_ap, ntok, w, xT=None):
            nsub = (w + 127) // 128
            stage = B.alloc(2 * nsub)
            for s in range(nsub):
                n = min(128, w - s * 128)
                for hf in range(2):
                    sl = stage[2 * s + hf]
                    S.op("sp", lambda e, sl=sl, s=s, n=n, hf=hf: e.dma_start(
                        out=sl.f()[0:n, :], in_=rows_ap[s * 128:s * 128 + n, hf * 512:(hf + 1) * 512]), w=[sl.buf], dma=True)
            if xT is None:
                xT = B.alloc(KC)
            for kc in range(KC):
                bk = B.bank()

                def tr(e, kc=kc, bk=bk):
                    ins = None
                    for s in range(nsub):
                        n = min(128, w - s * 128)
                        sl = stage[2 * s + kc // 4]
                        ins = e.transpose(out=bk.ap[:, s * 128:s * 128 + n], in_=sl.f()[0:n, (kc % 4) * 128:(kc % 4) * 128 + 128],
                                          identity=ident[0:n, 0:n])
                    return ins
                S.op("pe", tr, r=[stage[2 * s + kc // 4].buf for s in range(nsub)] + [bconst], w=[bk.buf])
                en = B.ev()
                S.op(en, _copy(en, xT[kc].f(w), bk.ap[:, 0:w]), r=[bk.buf], w=[xT[kc].buf])
            B.release(stage)
            return xT

        def store_T(src_slots, w, dst_rows_ap, ncols):
            nsub = (w + 127) // 128
            nch = ncols // 128
            for s in range(nsub):
                n = min(128, w - s * 128)
                for g0 in range(0, nch, 4):
                    ng = min(4, nch - g0)
                    bk = B.bank()

                    def tr(e, s=s, n=n, g0=g0, ng=ng, bk=bk):
                        ins = None
                        for j in range(ng):
                            ins = e.transpose(out=bk.ap[0:n, j * 128:(j + 1) * 128],
                                              in_=src_slots[g0 + j].f(w)[:, s * 128:s * 128 + n], identity=ident[:])
                        return ins
                    S.op("pe", tr, r=[src_slots[g0 + j].buf for j in range(ng)] + [bconst], w=[bk.buf])
                    stg = B.alloc()
                    en = B.ev()
                    S.op(en, _copy(en, stg.f()[0:n, 0:ng * 128], bk.ap[0:n, 0:ng * 128]), r=[bk.buf], w=[stg.buf])
                    S.op("sp", lambda e, stg=stg, s=s, n=n, g0=g0, ng=ng: e.dma_start(
                        out=dst_rows_ap[s * 128:s * 128 + n, g0 * 128:(g0 + ng) * 128], in_=stg.f()[0:n, 0:ng * 128]),
                        r=[stg.buf], dma=True)
                    B.release(stg)

        def sumsq_bank(srcs, w, acc):
            n = len(srcs)
            tmp = B.allocR(2)
            for i, (ap, buf) in enumerate(srcs):
                t = tmp[i % 2]
                S.op("act", lambda e, t=t, ap=ap: e.activation(out=t.r(w), in_=ap, func=AF.Square), r=[buf], w=[t.buf])
                S.op("pe", lambda e, t=t, i=i: e.matmul(acc.ap[:, 0:w], lhsT=ones_r[:], rhs=t.r(w), start=(i == 0), stop=(i == n - 1)),
                     r=[t.buf, bconst], w=[acc.buf])
            B.release(tmp)

        def rstd_of(acc, w, dim, eps):
            r = B.alloc()
            S.op("act", lambda e: e.activation(out=r.f(w), in_=acc.ap[:, 0:w], func=AF.Sqrt, scale=1.0 / dim, bias=eps),
                 r=[acc.buf], w=[r.buf])
            S.op("dve", lambda e: e.reciprocal(out=r.f(w), in_=r.f(w)), r=[r.buf], w=[r.buf])
            return r

        def rmsnorm(xT, w, gname, gidx0, rstd=None):
            if rstd is None:
                acc = banks[6]
                sumsq_bank([(s.f(w), s.buf) for s in xT], w, acc)
                rstd = rstd_of(acc, w, D, RMS_EPS)
            h = B.allocR(len(xT))
            for kc in range(len(xT)):
                en = "dve"
                S.op(en, lambda e, kc=kc: e.scalar_tensor_tensor(out=h[kc].r(w), in0=xT[kc].f(w), scalar=gcol(gname, gidx0 + kc),
                                                                  in1=rstd.f(w), op0=ALU.mult, op1=ALU.mult),
                     r=[xT[kc].buf, rstd.buf, bG], w=[h[kc].buf])
            return h, rstd

        def linear(xin, w, Wd, col0, noc, consume):
            KCn = len(xin)
            grp = max(1, min(4, WBUF_COLS // (KCn * 128)))
            for g0 in range(0, noc, grp):
                ng = min(grp, noc - g0)
                wb = B.wbuf()
                ncols = ng * 128
                src = Wd[:, col0 + g0 * 128: col0 + g0 * 128 + ncols].rearrange("(kc p) n -> p kc n", p=128)
                dst = wb.t[:, 0:KCn * ncols].rearrange("p (kc n) -> p kc n", kc=KCn)
                S.op("sp", lambda e, dst=dst, src=src: e.dma_start(out=dst, in_=src), w=[wb.buf], dma=True)
                for j in range(ng):
                    bk = B.bank()

                    def mm(e, dst=dst, j=j, bk=bk):
                        ins = None
                        for kc in range(KCn):
                            ins = e.matmul(bk.ap[:, 0:w], lhsT=dst[:, kc, j * 128:(j + 1) * 128], rhs=xin[kc].r(w),
                                           start=(kc == 0), stop=(kc == KCn - 1))
                        return ins
                    S.op("pe", mm, r=[wb.buf] + [s.buf for s in xin], w=[bk.buf])
                    consume(g0 + j, bk)

        def linear_tok(xin, w, Wd, col0, ncols, consume):
            KCn = len(xin)
            wb = B.wbuf()
            src = Wd[:, col0:col0 + ncols].rearrange("(kc p) n -> p kc n", p=128)
            dst = wb.t[:, 0:KCn * ncols].rearrange("p (kc n) -> p kc n", kc=KCn)
            S.op("sp", lambda e: e.dma_start(out=dst, in_=src), w=[wb.buf], dma=True)
            nsub = (w + 127) // 128
            for s in range(nsub):
                n = min(128, w - s * 128)
                bk = B.bank()

                def mm(e, s=s, n=n, bk=bk):
                    ins = None
                    for kc in range(KCn):
                        ins = e.matmul(bk.ap[0:n, 0:ncols], lhsT=xin[kc].r(w)[:, s * 128:s * 128 + n], rhs=dst[:, kc, :],
                                       start=(kc == 0), stop=(kc == KCn - 1))
                    return ins
                S.op("pe", mm, r=[wb.buf] + [x.buf for x in xin], w=[bk.buf])
                consume(s, n, bk)

        def post_norm_add(xT, w, r, gname, gidx0):
            acc = banks[7]
            sumsq_bank([(s_.f(w), s_.buf) for s_ in r], w, acc)
            rstd = rstd_of(acc, w, D, RMS_EPS)
            for kc in range(KC):
                S.op("dve", lambda e, kc=kc: e.scalar_tensor_tensor(out=r[kc].f(w), in0=r[kc].f(w), scalar=gcol(gname, gidx0 + kc),
                                                                     in1=rstd.f(w), op0=ALU.mult, op1=ALU.mult),
                     r=[r[kc].buf, rstd.buf, bG], w=[r[kc].buf])
                S.op("pool", lambda e, kc=kc: e.tensor_tensor(out=xT[kc].f(w), in0=xT[kc].f(w), in1=r[kc].f(w), op=ALU.add),
                     r=[xT[kc].buf, r[kc].buf], w=[xT[kc].buf])
            B.release(rstd)

        def proj_post_add(xT, w, mixin, Wd, gname, gidx0):
            r = B.alloc(KC)

            def cons(oc, bk):
                en = B.ev()
                S.op(en, _copy(en, r[oc].f(w), bk.ap[:, 0:w]), r=[bk.buf], w=[r[oc].buf])
            linear(mixin, w, Wd, 0, KC, cons)
            post_norm_add(xT, w, r, gname, gidx0)
            B.release(r)

        def ffn(xT, w, l):
            h, rstd = rmsnorm(xT, w, "ffn_pre", l * KC)
            B.release(rstd)
            r = B.alloc(KC)
            HF = FC // 2
            for half in range(2):
                act = B.allocR(HF)
                pend = {}

                def cons(oc, bk, act=act, pend=pend):
                    if oc < HF:
                        t = B.alloc()
                        S.op("act", lambda e: e.activation(out=t.f(w), in_=bk.ap[:, 0:w], func=AF.Silu), r=[bk.buf], w=[t.buf])
                        pend[oc] = t
                    else:
                        j = oc - HF
                        t = pend.pop(j)
                        S.op("dve", lambda e: e.tensor_tensor(out=act[j].r(w), in0=t.f(w), in1=bk.ap[:, 0:w], op=ALU.mult),
                             r=[t.buf, bk.buf], w=[act[j].buf])
                        B.release(t)
                c0 = half * HF * 128
                for g0 in range(0, HF, 4):
                    ng = min(4, HF - g0)
                    linear(h, w, w_up[l], c0 + g0 * 128, ng, lambda oc, bk, g0=g0, cons=cons: cons(g0 + oc, bk))
                    linear(h, w, w_up[l], FF + c0 + g0 * 128, ng, lambda oc, bk, g0=g0, cons=cons: cons(HF + g0 + oc, bk))

                def consd(oc, bk, half=half):
                    if half == 0:
                        en = B.ev()
                        S.op(en, _copy(en, r[oc].f(w), bk.ap[:, 0:w]), r=[bk.buf], w=[r[oc].buf])
                    else:
                        S.op("dve", lambda e: e.tensor_tensor(out=r[oc].f(w), in0=r[oc].f(w), in1=bk.ap[:, 0:w], op=ALU.add),
                             r=[r[oc].buf, bk.buf], w=[r[oc].buf])
                linear(act, w, w_down[l][c0:c0 + HF * 128, :], 0, KC, consd)
                B.release(act)
            B.release(h)
            post_norm_add(xT, w, r, "ffn_post", l * KC)
            B.release(r)

        def mem_attention(qm, w, mk_ap, mv_ap, bufs_mkv, out_slots, cols=None):
            if cols is None:
                cols = [(s_ * 128, min(128, w - s_ * 128)) for s_ in range((w + 127) // 128)]
            for (q0, n) in cols:
                obk = [banks[6], banks[7]]
                for hh in range(4):
                    c, po = hh // 2, (hh % 2) * 64
                    sbk = B.bank()
                    S.op("pe", lambda e, c=c, po=po, sbk=sbk, q0=q0, n=n: e.matmul(
                        sbk.ap[0:n, 0:NMEM], lhsT=qm[c].b(w)[po:po + 64, q0:q0 + n], rhs=mk_ap(c)[po:po + 64, :],
                        start=True, stop=True), r=[qm[c].buf] + bufs_mkv, w=[sbk.buf])
                    sm = bsmall[hh]
                    c0 = hh * 4
                    S.op("dve", lambda e, sbk=sbk, n=n, c0=c0: e.reduce_max(out=small[0:n, c0:c0 + 1], in_=sbk.ap[0:n, 0:NMEM], axis=AX.X),
                         r=[sbk.buf], w=[sm])
                    S.op("dve", lambda e, n=n, c0=c0: e.tensor_scalar(out=small[0:n, c0 + 1:c0 + 2], in0=small[0:n, c0:c0 + 1],
                                                                      scalar1=-0.125, scalar2=None, op0=ALU.mult),
                         r=[sm], w=[sm])
                    p = B.alloc()
                    S.op("act", lambda e, sbk=sbk, n=n, c0=c0, p=p: e.activation(
                        out=p.f()[0:n, 0:NMEM], in_=sbk.ap[0:n, 0:NMEM], func=AF.Exp, bias=small[0:n, c0 + 1:c0 + 2], scale=0.125,
                        accum_out=small[0:n, c0 + 2:c0 + 3]), r=[sbk.buf, sm], w=[p.buf, sm])
                    S.op("dve", lambda e, n=n, c0=c0: e.reciprocal(out=small[0:n, c0 + 3:c0 + 4], in_=small[0:n, c0 + 2:c0 + 3]),
                         r=[sm], w=[sm])
                    pn = B.alloc()
                    S.op("dve", lambda e, n=n, c0=c0, p=p, pn=pn: e.tensor_scalar(
                        out=pn.b()[0:n, 0:NMEM], in0=p.f()[0:n, 0:NMEM], scalar1=small[0:n, c0 + 3:c0 + 4], scalar2=None, op0=ALU.mult),
                        r=[p.buf, sm], w=[pn.buf])
                    B.release(p)
                    tbk = B.bank()
                    tb = tbk.ap.bitcast(BF16)

                    def tr(e, pn=pn, tb=tb, n=n):
                        e.transpose(out=tb[:, 0:n], in_=pn.b()[0:n, 0:128], identity=ident_b[0:n, 0:n])
                        return e.transpose(out=tb[:, 128:128 + n], in_=pn.b()[0:n, 128:256], identity=ident_b[0:n, 0:n])
                    S.op("pe", tr, r=[pn.buf, bconst], w=[tbk.buf])
                    pT = B.alloc()
                    en = B.ev()
                    S.op(en, _copy(en, pT.b()[:, 0:256].rearrange("p (a b) -> p a b", a=2)[:, :, 0:n],
                                   tb[:, 0:256].rearrange("p (a b) -> p a b", a=2)[:, :, 0:n]), r=[tbk.buf], w=[pT.buf])
                    B.release(pn)

                    def pv(e, pT=pT, c=c, po=po, hh=hh, n=n, obk=obk):
                        e.matmul(obk[c].ap[po:po + 64, 0:n], lhsT=mv_ap(0)[:, hh * 64:(hh + 1) * 64], rhs=pT.b()[:, 0:n],
                                 start=True, stop=False)
                        return e.matmul(obk[c].ap[po:po + 64, 0:n], lhsT=mv_ap(1)[:, hh * 64:(hh + 1) * 64], rhs=pT.b()[:, 128:128 + n],
                                        start=False, stop=True)
                    S.op("pe", pv, r=[pT.buf] + bufs_mkv, w=[obk[c].buf])
                    B.release(pT)
                for c in range(2):
                    en = B.ev()
                    S.op(en, _copy(en, out_slots[c].r(w)[:, q0:q0 + n], obk[c].ap[:, 0:n]), r=[obk[c].buf], w=[out_slots[c].buf])

        def conv_ln(yr, yc, w, l, mix):
            acc1, acc2 = banks[6], banks[7]
            for c in range(CC):
                S.op("pe", lambda e, c=c: e.matmul(acc1.ap[:, 0:w], lhsT=ones_r[:], rhs=yr[c].r(w), start=(c == 0), stop=(c == CC - 1)),
                     r=[yr[c].buf, bconst], w=[acc1.buf])
            sumsq_bank([(s_.f(w), s_.buf) for s_ in yr], w, acc2)
            mean = B.alloc()
            msq = B.alloc()
            var = B.alloc()
            S.op("act", lambda e: e.activation(out=mean.f(w), in_=acc1.ap[:, 0:w], func=AF.Copy, scale=1.0 / C), r=[acc1.buf], w=[mean.buf])
            S.op("pool", lambda e: e.tensor_tensor(out=msq.f(w), in0=mean.f(w), in1=mean.f(w), op=ALU.mult), r=[mean.buf], w=[msq.buf])
            S.op("dve", lambda e: e.scalar_tensor_tensor(out=var.f(w), in0=acc2.ap[:, 0:w], scalar=1.0 / C, in1=msq.f(w),
                                                         op0=ALU.mult, op1=ALU.subtract), r=[acc2.buf, msq.buf], w=[var.buf])
            S.op("act", lambda e: e.activation(out=var.f(w), in_=var.f(w), func=AF.Sqrt, scale=1.0, bias=LN_EPS), r=[var.buf], w=[var.buf])
            S.op("dve", lambda e: e.reciprocal(out=var.f(w), in_=var.f(w)), r=[var.buf], w=[var.buf])
            for c in range(CC):
                en = "pool"
                S.op(en, lambda e, c=c: e.tensor_tensor(out=yc[c].f(w), in0=yr[c].f(w), in1=mean.f(w), op=ALU.subtract),
                     r=[yr[c].buf, mean.buf], w=[yc[c].buf])
                en2 = "dve"
                S.op(en2, lambda e, c=c, l=l: e.scalar_tensor_tensor(out=yc[c].f(w), in0=yc[c].f(w), scalar=gcol("ln_g", l * CC + c),
                                                                     in1=var.f(w), op0=ALU.mult, op1=ALU.mult),
                     r=[yc[c].buf, var.buf, bG], w=[yc[c].buf])
                S.op("act", lambda e, c=c, l=l: e.activation(out=mix[c].r(w), in_=yc[c].f(w), func=AF.Silu,
                                                             bias=gcol("ln_b", l * CC + c), scale=1.0),
                     r=[yc[c].buf, bG], w=[mix[c].buf])
            B.release([mean, msq, var])

        if DBG_STAGE >= 3:
            memT = load_xT(memp, NMEM, NMEM)
        if DBG_STAGE >= 4:
            accm = banks[6]
            sumsq_bank([(s.f(NMEM), s.buf) for s in memT], NMEM, accm)
            rstd_m = rstd_of(accm, NMEM, D, RMS_EPS)
        for l in range((L if DBG_STAGE >= 7 else 1) if DBG_STAGE >= 5 else 0):
            mn, _ = rmsnorm(memT, NMEM, "mem_g", l * KC, rstd=rstd_m)

            def cons_mk(oc, bk, l=l):
                en = B.ev()
                S.op(en, _copy(en, mkT[:, l, oc, :], bk.ap[:, 0:NMEM]), r=[bk.buf], w=[bmk[l]])
            linear(mn, NMEM, w_mem_kv[l], 0, 2, cons_mk)

            for grp in range((2 if DBG_SUB >= 2 else 1) if DBG_STAGE >= 6 else 0):
                def cons_tok(s, n, bk, l=l, grp=grp):
                    stg = B.alloc()
                    S.op("act", lambda e: e.activation(out=stg.f(256), in_=bk.ap[:, 0:256], func=AF.Copy), r=[bk.buf], w=[stg.buf])
                    dstd = mk_p if grp == 0 else mv_p
                    if DBG_SUB >= 1:
                        S.op("sp", lambda e: e.dma_start(out=dstd[l, s * 128:(s + 1) * 128, :], in_=stg.f(256)), r=[stg.buf], dma=True)
                    if grp == 1:
                        S.op("dve", lambda e: e.tensor_copy(out=mvv[:, l, s, :], in_=stg.f(256)), r=[stg.buf], w=[bmv[l]])
                    B.release(stg)
                linear_tok(mn, NMEM, w_mem_kv[l], grp * 256, 256, cons_tok)
            B.release(mn)
        if DBG_STAGE >= 5:
            B.release(rstd_m)
            B.release(memT)

        for ti in range(DBG_NTILE):
            t0 = ti * TW
            w = TW
            xT = xres
            load_xT(xp[t0:t0 + TW, :], TW, TW, xT=xres)
            for l in range(DBG_L):
                h, rstd = rmsnorm(xT, w, "mix_pre", l * KC)
                B.release(rstd)
                mix = B.allocR(KC)
                qm = B.alloc(2)
                if l < NA:
                    sig = B.alloc(CC)

                    def cons_gate(oc, bk, sig=sig):
                        S.op("act", lambda e: e.activation(out=sig[oc].f(w), in_=bk.ap[:, 0:w], func=AF.Sigmoid), r=[bk.buf], w=[sig[oc].buf])
                    linear(h, w, w_in_a[l], C, CC, cons_gate)
                    if ti == 0:
                        S.op("pool", lambda e: e.memset(uext[:, :, 0:30], 0.0), w=buext)
                    else:
                        S.op("pool", lambda e, l=l: e.tensor_copy(out=uext[:, :, 0:30], in_=halo[:, l, :, :]), r=[bhalo[l]], w=buext)

                    def cons_a(oc, bk, sig=sig):
                        S.op("dve", lambda e: e.tensor_tensor(out=uext[:, oc, 30:30 + w], in0=sig[oc].f(w), in1=bk.ap[:, 0:w], op=ALU.mult),
                             r=[sig[oc].buf, bk.buf], w=[buext[oc]])
                    linear(h, w, w_in_a[l], 0, CC, cons_a)
                    B.release(sig)

                    def cons_qm(oc, bk, qm=qm):
                        en = B.ev()
                        S.op(en, _copy(en, qm[oc].b(w), bk.ap[:, 0:w]), r=[bk.buf], w=[qm[oc].buf])
                    linear(h, w, w_in_a[l], 2 * C, 2, cons_qm)
                    B.release(h)
                    S.op("pool", lambda e, l=l: e.tensor_copy(out=halo[:, l, :, :], in_=uext[:, :, w:w + 30]), r=buext, w=[bhalo[l]])
                    if ti == NTILE - 1:
                        bk = B.bank()

                        bk2 = B.bank()

                        def trh1(e, l=l, bk=bk):
                            ins = None
                            for c in range(4):
                                ins = e.transpose(out=bk.ap[0:30, c * 128:(c + 1) * 128], in_=halo[:, l, c, :], identity=ident[:])
                            return ins

                        def trh2(e, l=l, bk2=bk2):
                            ins = None
                            for c in range(4, 6):
                                ins = e.transpose(out=bk2.ap[0:30, (c - 4) * 128:(c - 3) * 128], in_=halo[:, l, c, :], identity=ident[:])
                            return ins
                        S.op("pe", trh1, r=[bhalo[l], bconst], w=[bk.buf])
                        S.op("pe", trh2, r=[bhalo[l], bconst], w=[bk2.buf])
                        stg = B.alloc(2)
                        S.op("dve", lambda e, bk=bk, stg=stg: e.tensor_copy(out=stg[0].f()[0:30, :], in_=bk.ap[0:30, :]), r=[bk.buf], w=[stg[0].buf])
                        S.op("dve", lambda e, bk2=bk2, stg=stg: e.tensor_copy(out=stg[1].f()[0:30, 0:256], in_=bk2.ap[0:30, 0:256]), r=[bk2.buf], w=[stg[1].buf])
                        S.op("sp", lambda e, l=l, stg=stg: e.dma_start(out=conv_prompt[l, :, 0:512], in_=stg[0].f()[0:30, :]), r=[stg[0].buf], dma=True)
                        S.op("sp", lambda e, l=l, stg=stg: e.dma_start(out=conv_prompt[l, :, 512:768], in_=stg[1].f()[0:30, 0:256]), r=[stg[1].buf], dma=True)
                        B.release(stg)
                    yc = B.alloc(CC)
                    yr = B.allocR(CC)
                    for k in range(CW):
                        for c in range(CC):
                            en = "dve"
                            wc = gcol("conv_w", (l * CW + k) * CC + c)
                            if k == 0:
                                S.op(en, lambda e, c=c, wc=wc, l=l: e.tensor_scalar(out=yc[c].f(w), in0=uext[:, c, 0:w], scalar1=wc,
                                                                                    scalar2=gcol("conv_b", l * CC + c), op0=ALU.mult, op1=ALU.add),
                                     r=[buext[c], bG], w=[yc[c].buf])
                            else:
                                outv = yr[c].r(w) if k == CW - 1 else yc[c].f(w)
                                S.op(en, lambda e, c=c, wc=wc, k=k, outv=outv: e.scalar_tensor_tensor(
                                    out=outv, in0=uext[:, c, k:k + w], scalar=wc, in1=yc[c].f(w), op0=ALU.mult, op1=ALU.add),
                                    r=[buext[c], yc[c].buf, bG], w=[yr[c].buf if k == CW - 1 else yc[c].buf])
                    conv_ln(yr, yc, w, l, mix)
                    B.release(yc)
                    B.release(yr)
                else:
                    lb = l - NA
                    qT = B.alloc(CC)

                    def cons_q(oc, bk, qT=qT, qm=qm):
                        dst = qT[oc] if oc < CC else qm[oc - CC]
                        en = B.ev()
                        S.op(en, _copy(en, dst.b(w), bk.ap[:, 0:w]), r=[bk.buf], w=[dst.buf])
                    linear(h, w, w_in_b[lb], 0, CC + 2, cons_q)
                    B.release(h)
                    kb_hi = (t0 + w) // 128 - 1
                    kvbufs = [bKT[t] for t in range(ti + 1)] + [bVV[t] for t in range(ti + 1)]

                    def head_stream(hd, ibk, obk, zbanks):
                        c, po = hd // 2, (hd % 2) * 64
                        bias = sbb[:, lb * NH + hd: lb * NH + hd + 1]
                        first = True
                        for kb in range(kb_hi, -1, -1):
                            diag = (kb + 1) * 128 > t0
                            dd = kb - t0 // 128
                            zbk = zbanks
                            S.op("pe", lambda e, kb=kb, zbk=zbk: e.matmul(
                                zbk.ap[:, 0:w], lhsT=KT[po:po + 64, c, kb * 128:(kb + 1) * 128], rhs=qT[c].b(w)[po:po + 64, :],
                                start=True, stop=True), r=[qT[c].buf] + kvbufs, w=[zbk.buf])
                            s1 = B.alloc()
                            s2 = B.alloc()
                            ee, wt, sp, ex = s1.b(w), s1.b2(w), s2.b(w), s2.b2(w)
                            S.op("act", lambda e, zbk=zbk, ee=ee: e.activation(out=ee, in_=zbk.ap[:, 0:w], func=AF.Exp, bias=bias, scale=0.125),
                                 r=[zbk.buf, bsbb], w=[s1.buf])
                            if diag:
                                S.op("dve", lambda e, ee=ee, dd=dd: e.tensor_tensor(out=ee, in0=ee, in1=masks[:, dd, 0:w], op=ALU.mult),
                                     r=[s1.buf, bconst], w=[s1.buf])
                            S.op("act", lambda e, ee=ee, sp=sp: e.activation(out=sp, in_=ee, func=AF.Ln, bias=1.0, scale=1.0),
                                 r=[s1.buf], w=[s2.buf])
                            yield
                            S.op("pe", lambda e, sp=sp, first=first: e.matmul(ibk.ap[:, 0:w], lhsT=u_incl[:], rhs=sp, start=first, stop=False,
                                                                             skip_group_check=True),
                                 r=[s2.buf, bconst], w=[ibk.buf])
                            yield
                            S.op("act", lambda e, ex=ex: e.activation(out=ex, in_=ibk.ap[:, 0:w], func=AF.Exp, scale=-1.0),
                                 r=[ibk.buf], w=[s2.buf])
                            S.op("dve", lambda e, ee=ee, ex=ex, wt=wt: e.tensor_tensor(out=wt, in0=ee, in1=ex, op=ALU.mult),
                                 r=[s1.buf, s2.buf], w=[s1.buf])
                            yield
                            if kb > 0:
                                S.op("pe", lambda e, sp=sp: e.matmul(ibk.ap[:, 0:w], lhsT=u_lt[:], rhs=sp, start=False, stop=False,
                                                                     skip_group_check=True),
                                     r=[s2.buf, bconst], w=[ibk.buf])
                            S.op("pe", lambda e, wt=wt, kb=kb, first=first: e.matmul(
                                obk.ap[po:po + 64, 0:w], lhsT=VV[:, kb, hd * 64:(hd + 1) * 64], rhs=wt, start=first, stop=(kb == 0),
                                skip_group_check=True), r=[s1.buf] + kvbufs, w=[obk.buf])
                            B.release([s1, s2])
                            first = False
                            yield

                    for rnd in range(3):
                        heads = [rnd * 4 + i for i in range(4)]
                        obks = [banks[4], banks[5]]
                        zb = [banks[6], banks[7]]
                        gens = [head_stream(hd, banks[i], obks[i // 2], zb[i % 2]) for i, hd in enumerate(heads)]
                        alive = list(gens)
                        for g_ in gens[2:]:
                            next(g_)
                            next(g_)
                        while alive:
                            nxt = []
                            for g in alive:
                                try:
                                    next(g)
                                    nxt.append(g)
                                except StopIteration:
                                    pass
                            alive = nxt
                        for i in range(2):
                            cch = rnd * 2 + i
                            en = B.ev()
                            S.op(en, _copy(en, mix[cch].r(w), obks[i].ap[:, 0:w]), r=[obks[i].buf], w=[mix[cch].buf])
                    B.release(qT)
                mem_attention(qm, w, lambda c, l=l: mkT[:, l, c, :], lambda mh, l=l: mvv[:, l, mh, :], [bmk[l], bmv[l]], mix[CC:CC + 2])
                B.release(qm)
                if DBG_TAP is not None and DBG_TAP == (ti, l):
                    store_T(mix, w, dbg, D)
                proj_post_add(xT, w, mix, w_out[l], "mix_post", l * KC)
                B.release(mix)
                ffn(xT, w, l)
                if l == NA - 1:
                    hk, rstd = rmsnorm(xT, w, "kv_g", 0)
                    B.release(rstd)

                    def cons_kT(oc, bk, ti=ti, t0=t0):
                        en = B.ev()
                        S.op(en, _copy(en, KT[:, oc, t0:t0 + w], bk.ap[:, 0:w]), r=[bk.buf], w=[bKT[ti]])
                    linear(hk, w, w_kv, 0, CC, cons_kT)
                    for grp in range(6):
                        def cons_kv(s, n, bk, grp=grp, ti=ti, t0=t0):
                            stg = B.alloc()
                            S.op("act", lambda e: e.activation(out=stg.f(256), in_=bk.ap[:, 0:256], func=AF.Copy), r=[bk.buf], w=[stg.buf])
                            r0 = t0 + s * 128
                            dstd = k_prompt if grp < 3 else v_prompt
                            c0 = (grp % 3) * 256
                            S.op("sp", lambda e: e.dma_start(out=dstd[r0:r0 + 128, c0:c0 + 256], in_=stg.f(256)), r=[stg.buf], dma=True)
                            if grp >= 3:
                                S.op("dve", lambda e: e.tensor_copy(out=VV[:, ti * 4 + s, c0:c0 + 256], in_=stg.f(256)), r=[stg.buf], w=[bVV[ti]])
                            B.release(stg)
                        linear_tok(hk, w, w_kv, grp * 256, 256, cons_kv)
                    B.release(hk)
            store_T(xT, w, y_prompt[t0:t0 + w, :], D)


        if SAMPLE:
            w = NS
            xsT = []
            for kc in range(KC):
                sx = Slot(B.sb(f"xsT{kc}", [128, 16], F32), 300 + kc)
                sx.pool = []
                xsT.append(sx)
            idx = B.sb("pidx", [128, 2], I32)
            bidx = Buf("pidx")
            for g in range(2):
                S.op("sp", lambda e, g=g: e.dma_start(out=idx[:, g:g + 1], in_=ptab[g * 128:(g + 1) * 128, :]), w=[bidx], dma=True)
            idxf = B.sb("pidxf", [128, 2], F32)
            iof = B.sb("iotaf", [128, PAGE // 2], F32)
            idxallf = B.sb("idxallf", [128, 2, PAGE // 2], F32)
            idxall = B.sb("idxall", [128, 2, PAGE // 2], I32)
            S.op("pool", lambda e: e.iota(out=iof[:], pattern=[[1, PAGE // 2]], base=0, channel_multiplier=0,
                                          allow_small_or_imprecise_dtypes=True), w=[bconst])
            S.op("dve", lambda e: e.tensor_copy(out=idxf[:], in_=idx[:]), r=[bidx], w=[bidx])
            S.op("dve", lambda e: e.tensor_scalar(out=idxf[:], in0=idxf[:], scalar1=float(PAGE // 2), scalar2=None, op0=ALU.mult), r=[bidx], w=[bidx])
            for g in range(2):
                S.op("dve", lambda e, g=g: e.tensor_scalar(out=idxallf[:, g, :], in0=iof[:], scalar1=idxf[:, g:g + 1], scalar2=None, op0=ALU.add),
                     r=[bidx, bconst], w=[bidx])
            S.op("dve", lambda e: e.tensor_copy(out=idxall[:], in_=idxallf[:]), r=[bidx], w=[bidx])
            ck2 = cache_k.rearrange("n (ch f) -> (n ch) f", f=2 * C)
            cv2 = cache_v.rearrange("n (ch f) -> (n ch) f", f=2 * C)
            tri = B.sb("tri", [128, 128], F32)
            sel = B.sb("sel", [128, 2, NS], F32)
            negc = B.sb("negc", [128, 16], F32)
            bnegc = Buf("negc")
            S.op("pool", lambda e: e.memset(tri[:], 1.0), w=[bconst])
            S.op("pool", lambda e: e.affine_select(out=tri[:], in_=tri[:], pattern=[[-1, 128]], compare_op=ALU.is_gt, fill=0.0,
                                                   base=0, channel_multiplier=1), r=[bconst], w=[bconst])
            S.op("pool", lambda e: e.memset(tri[64:128, 0:64], 0.0), r=[bconst], w=[bconst])
            S.op("pool", lambda e: e.memset(sel[:], 0.0), r=[bconst], w=[bconst])
            for g in range(2):
                S.op("pool", lambda e, g=g: e.memset(sel[0:64, g, 2 * g:2 * g + 1], 1.0), r=[bconst], w=[bconst])
                S.op("pool", lambda e, g=g: e.memset(sel[64:128, g, 2 * g + 1:2 * g + 2], 1.0), r=[bconst], w=[bconst])
            KTf = KT[:].rearrange("p c t -> p (c t)").bitcast(F32)
            VVf = VV[:].rearrange("p t c -> p (t c)").bitcast(F32)
            Kb = [KTf[:, 0:1536], KTf[:, 1536:3072]]
            EE = KTf[:, 3072:4608]
            SX = KTf[:, 4608:6144]
            Vb = [VVf[:, 0:1536], VVf[:, 1536:3072]]
            ring = Kb + Vb
            uxf = uext[:].rearrange("p c t -> p (c t)").bitcast(BF16)
            vbb = [uxf[:, i * 1536:(i + 1) * 1536] for i in range(4)]
            bvbb = [Buf(f"vbb{i}") for i in range(4)]
            sel_b = B.sb("sel_b", [128, 2, NS], BF16)
            S.op("pool", lambda e: e.tensor_copy(out=sel_b[:], in_=sel[:]), r=[bconst], w=[bconst])
            S2 = VVf[:, 3072:4608]
            QB = VVf[:, 4608:5376]
            bKb = [Buf("Kb0"), Buf("Kb1")]
            bVb = [Buf("Vb0"), Buf("Vb1")]
            bring = bKb + bVb
            bEE, bSX, bS2, bQB = Buf("EE"), Buf("SX"), Buf("S2"), Buf("QB")
            S.op("pool", lambda e: e.memset(QB, 0.0), w=bKb + bVb + bvbb + [bEE, bSX, bS2, bQB] + bKT + bVV + buext)

            load_xT(xs, NS, NS, xT=xsT)
            for l in range(L):
                h, rstd = rmsnorm(xsT, w, "mix_pre", l * KC)
                B.release(rstd)
                mix = B.allocR(KC)
                qm = B.alloc(2)
                if l < NA:
                    sig = B.alloc(CC)
                    us = B.alloc(CC)

                    def cons_gate(oc, bk, sig=sig):
                        S.op("act", lambda e: e.activation(out=sig[oc].f(w), in_=bk.ap[:, 0:w], func=AF.Sigmoid), r=[bk.buf], w=[sig[oc].buf])
                    linear(h, w, w_in_a[l], C, CC, cons_gate)

                    def cons_a(oc, bk, sig=sig, us=us):
                        S.op("dve", lambda e: e.tensor_tensor(out=us[oc].f(w), in0=sig[oc].f(w), in1=bk.ap[:, 0:w], op=ALU.mult),
                             r=[sig[oc].buf, bk.buf], w=[us[oc].buf])
                    linear(h, w, w_in_a[l], 0, CC, cons_a)
                    B.release(sig)

                    def cons_qm(oc, bk, qm=qm):
                        en = B.ev()
                        S.op(en, _copy(en, qm[oc].b(w), bk.ap[:, 0:w]), r=[bk.buf], w=[qm[oc].buf])
                    linear(h, w, w_in_a[l], 2 * C, 2, cons_qm)
                    B.release(h)
                    S.op("sp", lambda e, l=l: e.dma_start(out=conv_sample[l, :, 0:CW - 2, :], in_=state_conv[l, :, 1:CW - 1, :]), dma=True)
                    store_T(us, w, conv_sample[l, :, CW - 2, :], C)
                    stc = B.alloc(CC)
                    for b in range(NS):
                        stg = B.alloc(2)
                        S.op("sp", lambda e, l=l, b=b, stg=stg: e.dma_start(out=stg[0].f()[0:CW - 1, :], in_=state_conv[l, b, :, 0:512]),
                             w=[stg[0].buf], dma=True)
                        S.op("sp", lambda e, l=l, b=b, stg=stg: e.dma_start(out=stg[1].f()[0:CW - 1, 0:256], in_=state_conv[l, b, :, 512:768]),
                             w=[stg[1].buf], dma=True)
                        bka, bkb = B.bank(), B.bank()

                        def trs(e, stg=stg, bka=bka):
                            ins = None
                            for c in range(4):
                                ins = e.transpose(out=bka.ap[:, c * 32:c * 32 + 30], in_=stg[0].f()[0:CW - 1, c * 128:(c + 1) * 128],
                                                  identity=ident[0:CW - 1, 0:CW - 1])
                            return ins

                        def trs2(e, stg=stg, bkb=bkb):
                            ins = None
                            for c in range(2):
                                ins = e.transpose(out=bkb.ap[:, c * 32:c * 32 + 30], in_=stg[1].f()[0:CW - 1, c * 128:(c + 1) * 128],
                                                  identity=ident[0:CW - 1, 0:CW - 1])
                            return ins
                        S.op("pe", trs, r=[stg[0].buf, bconst], w=[bka.buf])
                        S.op("pe", trs2, r=[stg[1].buf, bconst], w=[bkb.buf])
                        for c in range(CC):
                            bk_ = bka if c < 4 else bkb
                            cl = c % 4
                            en = B.ev()
                            S.op(en, _copy(en, stc[c].f(120)[:, b * 30:(b + 1) * 30], bk_.ap[:, cl * 32:cl * 32 + 30]), r=[bk_.buf], w=[stc[c].buf])
                        B.release(stg)
                    yc = []
                    yr = B.allocR(CC)
                    o_w = voff["conv_w"] + l * CW * CC
                    for c in range(CC):
                        tmp = B.alloc()
                        ysm = B.alloc()
                        yc.append(B.alloc())
                        wck = G[:, o_w:o_w + CW * CC].rearrange("p (k c) -> p c k", c=CC)[:, c, 0:CW - 1]
                        stv = stc[c].f(120).rearrange("p (b k) -> p b k", b=NS)
                        tv = tmp.f(120).rearrange("p (b k) -> p b k", b=NS)
                        S.op("dve", lambda e, tv=tv, stv=stv, wck=wck: e.tensor_tensor(
                            out=tv, in0=stv, in1=wck.unsqueeze(1).broadcast_to([128, NS, CW - 1]), op=ALU.mult),
                            r=[stc[c].buf, bG], w=[tmp.buf])
                        S.op("dve", lambda e, tv=tv, ysm=ysm: e.reduce_sum(out=ysm.f(w), in_=tv, axis=AX.X), r=[tmp.buf], w=[ysm.buf])
                        S.op("dve", lambda e, c=c, l=l, ysm=ysm: e.scalar_tensor_tensor(
                            out=yc[c].f(w), in0=us[c].f(w), scalar=gcol("conv_w", (l * CW + CW - 1) * CC + c), in1=ysm.f(w),
                            op0=ALU.mult, op1=ALU.add), r=[us[c].buf, ysm.buf, bG], w=[yc[c].buf])
                        S.op("dve", lambda e, c=c, l=l: e.tensor_scalar(out=yr[c].r(w), in0=yc[c].f(w), scalar1=gcol("conv_b", l * CC + c),
                                                                        scalar2=None, op0=ALU.add), r=[yc[c].buf, bG], w=[yr[c].buf])
                        B.release([tmp, ysm, stc[c], us[c]])
                    conv_ln(yr, yc, w, l, mix)
                    B.release(yc)
                    B.release(yr)
                else:
                    lb = l - NA
                    qs = B.alloc(CC)

                    def cons_q(oc, bk, qs=qs, qm=qm):
                        en = B.ev()
                        if oc < CC:
                            S.op(en, _copy(en, qs[oc].f(w), bk.ap[:, 0:w]), r=[bk.buf], w=[qs[oc].buf])
                        else:
                            S.op(en, _copy(en, qm[oc - CC].b(w), bk.ap[:, 0:w]), r=[bk.buf], w=[qm[oc - CC].buf])
                    linear(h, w, w_in_b[lb], 0, CC + 2, cons_q)
                    B.release(h)
                    EEv = EE.rearrange("p (t h) -> p t h", h=NH)
                    ob = [banks[6], banks[7]]
                    nmm = 0
                    for g in range(2):
                        bqa, bqb = B.bank(), B.bank()
                        for c in range(CC):
                            qrep = B.alloc()
                            for hb in range(2):
                                S.op("dve", lambda e, c=c, hb=hb, g=g, qrep=qrep: e.tensor_scalar(
                                    out=qrep.f(128)[:, hb * 64:(hb + 1) * 64], in0=ones_f[:, 0:64],
                                    scalar1=qs[c].f(w)[:, 2 * g + hb:2 * g + hb + 1], scalar2=None, op0=ALU.mult),
                                    r=[qs[c].buf, bconst], w=[qrep.buf])
                            bq = bqa if c < 4 else bqb
                            S.op("pe", lambda e, c=c, bq=bq, qrep=qrep: e.matmul(bq.ap[:, (c % 4) * 128:(c % 4 + 1) * 128], lhsT=qrep.f(128),
                                                                                 rhs=ident[:], start=True, stop=True, skip_group_check=True),
                                 r=[qrep.buf, bconst], w=[bq.buf])
                            B.release(qrep)
                        S.op("dve", lambda e, bqa=bqa: e.tensor_copy(out=QB[:, 0:512], in_=bqa.ap[:, 0:512]), r=[bqa.buf], w=[bQB])
                        S.op("dve", lambda e, bqb=bqb: e.tensor_copy(out=QB[:, 512:768], in_=bqb.ap[:, 0:256]), r=[bqb.buf], w=[bQB])
                        def k_gather(ch2, g=g):
                            S.op("pool", lambda e: e.indirect_dma_start(
                                out=ring[ch2 % 4], out_offset=None, in_=ck2[:, :],
                                in_offset=bass.IndirectOffsetOnAxis(ap=idxall[:, g, ch2:ch2 + 1], axis=0)), r=[bidx], w=[bring[ch2 % 4]], dma=True)
                        for ch2 in range(3):
                            k_gather(ch2)
                        for ch in range(PAGE // 2):
                            t0 = 2 * ch
                            kb, bkb_ = ring[ch % 4], bring[ch % 4]
                            kv3 = kb.rearrange("p (t f) -> p t f", t=2)
                            S.op("pool" if ch % 2 == 0 else "dve",
                                 lambda e, kv3=kv3: e.tensor_tensor(out=kv3, in0=kv3, in1=QB.unsqueeze(1).broadcast_to([128, 2, C]),
                                                                    op=ALU.mult), r=[bkb_, bQB], w=[bkb_])
                            if ch + 3 < PAGE // 2:
                                k_gather(ch + 3)
                            S.op("dve", lambda e, kb=kb, t0=t0: e.reduce_sum(out=EE[:, t0 * NH:(t0 + 2) * NH],
                                                                             in_=kb.rearrange("p (a d) -> p a d", d=64), axis=AX.X),
                                 r=[bkb_], w=[bEE])
                        for hd in range(NH):
                            S.op("act", lambda e, hd=hd: e.activation(out=EEv[:, :, hd], in_=EEv[:, :, hd], func=AF.Exp,
                                                                      bias=sbb[:, lb * NH + hd:lb * NH + hd + 1], scale=0.125),
                                 r=[bEE, bsbb], w=[bEE])
                        S.op("act", lambda e: e.activation(out=SX, in_=EE, func=AF.Ln, bias=1.0, scale=1.0), r=[bEE], w=[bSX])
                        cur, bcur, nxt, bnxt = SX, bSX, S2, bS2
                        for sft in (1, 2, 4, 8, 16, 32, 64):
                            cv = cur.rearrange("p (t h) -> p t h", h=NH)
                            nv = nxt.rearrange("p (t h) -> p t h", h=NH)
                            S.op("dve", lambda e, cv=cv, nv=nv, sft=sft: e.tensor_tensor(out=nv[:, 0:PAGE - sft, :], in0=cv[:, 0:PAGE - sft, :],
                                                                                       in1=cv[:, sft:PAGE, :], op=ALU.add),
                                 r=[bcur], w=[bnxt])
                            S.op("pool", lambda e, cv=cv, nv=nv, sft=sft: e.tensor_copy(out=nv[:, PAGE - sft:PAGE, :], in_=cv[:, PAGE - sft:PAGE, :]),
                                 r=[bcur], w=[bnxt])
                            cur, bcur, nxt, bnxt = nxt, bnxt, cur, bcur
                        cv = cur.rearrange("p (t h) -> p t h", h=NH)
                        nv = nxt.rearrange("p (t h) -> p t h", h=NH)
                        bc = B.bank()
                        S.op("pe", lambda e, bc=bc, cv=cv: e.matmul(bc.ap[:, 0:NH], lhsT=tri[:], rhs=cv[:, 0, :], start=True, stop=True),
                             r=[bcur, bconst], w=[bc.buf])
                        S.op("dve", lambda e, bc=bc: e.tensor_scalar(out=negc[:, 0:NH], in0=bc.ap[:, 0:NH], scalar1=-1.0, scalar2=None, op0=ALU.mult),
                             r=[bc.buf], w=[bnegc])
                        for hd in range(NH):
                            S.op("act", lambda e, hd=hd, cv=cv, nv=nv: e.activation(out=nv[:, :, hd], in_=cv[:, :, hd], func=AF.Exp, scale=-1.0,
                                                                                   bias=negc[:, hd:hd + 1]), r=[bcur, bnegc], w=[bnxt])
                        S.op("dve", lambda e, nxt=nxt: e.tensor_tensor(out=EE, in0=EE, in1=nxt, op=ALU.mult), r=[bEE, bnxt], w=[bEE])
                        for ch in range(PAGE // 2):
                            t0 = 2 * ch
                            vb, bvb_ = ring[ch % 4], bring[ch % 4]
                            vq, bvq = vbb[ch % 4], bvbb[ch % 4]
                            S.op("pool", lambda e, vb=vb, ch=ch, g=g: e.indirect_dma_start(
                                out=vb, out_offset=None, in_=cv2[:, :],
                                in_offset=bass.IndirectOffsetOnAxis(ap=idxall[:, g, ch:ch + 1], axis=0)), r=[bidx], w=[bvb_], dma=True)
                            v4 = vb.rearrange("p (t h d) -> p t h d", t=2, h=NH)
                            q4 = vq.rearrange("p (t h d) -> p t h d", t=2, h=NH)
                            S.op("dve", lambda e, v4=v4, q4=q4, t0=t0: e.tensor_tensor(
                                out=q4, in0=v4, in1=EEv[:, t0:t0 + 2, :].unsqueeze(3).broadcast_to([128, 2, NH, 64]), op=ALU.mult),
                                r=[bvb_, bEE], w=[bvq])

                            def pvmm(e, vq=vq, g=g, first=(nmm == 0), last=(g == 1 and ch == PAGE // 2 - 1)):
                                ins = None
                                for t in range(2):
                                    for hf in range(2):
                                        ins = e.matmul(ob[hf].ap[0:NS, 0:384], lhsT=sel_b[:, g, :], rhs=vq[:, t * C + hf * 384:t * C + (hf + 1) * 384],
                                                       start=(first and t == 0), stop=(last and t == 1), skip_group_check=True)
                                return ins
                            S.op("pe", pvmm, r=[bvq, bconst], w=[ob[0].buf, ob[1].buf])
                            nmm += 1
                    B.release(qs)
                    osb = B.alloc(2)
                    for hf in range(2):
                        S.op("dve", lambda e, hf=hf: e.tensor_copy(out=osb[hf].f()[0:NS, 0:384], in_=ob[hf].ap[0:NS, 0:384]), r=[ob[hf].buf], w=[osb[hf].buf])
                    bt = B.bank()

                    def tro(e, bt=bt, osb=osb):
                        ins = None
                        for c in range(CC):
                            ins = e.transpose(out=bt.ap[:, c * NS:(c + 1) * NS], in_=osb[c // 3].f()[0:NS, (c % 3) * 128:(c % 3 + 1) * 128],
                                              identity=ident[0:NS, 0:NS])
                        return ins
                    S.op("pe", tro, r=[osb[0].buf, osb[1].buf, bconst], w=[bt.buf])
                    for c in range(CC):
                        S.op("dve", lambda e, c=c, bt=bt: e.tensor_copy(out=mix[c].r(w), in_=bt.ap[:, c * NS:(c + 1) * NS]), r=[bt.buf], w=[mix[c].buf])
                    B.release(osb)
                for b in range(NS):
                    stgk, stgv, mks, mvs = B.alloc(), B.alloc(), B.alloc(), B.alloc()
                    S.op("sp", lambda e, l=l, b=b, stgk=stgk: e.dma_start(out=stgk.f().rearrange("p (mh f) -> p mh f", mh=2),
                                                                         in_=cmk[l, b].rearrange("(mh p) f -> p mh f", p=128)), w=[stgk.buf], dma=True)
                    S.op("sp", lambda e, l=l, b=b, stgv=stgv: e.dma_start(out=stgv.f().rearrange("p (mh f) -> p mh f", mh=2),
                                                                         in_=cmv[l, b].rearrange("(mh p) f -> p mh f", p=128)), w=[stgv.buf], dma=True)
                    for c in range(2):
                        bk = B.bank()

                        def trk(e, c=c, bk=bk, stgk=stgk):
                            ins = None
                            for mh in range(2):
                                ins = e.transpose(out=bk.ap[:, mh * 128:(mh + 1) * 128], in_=stgk.f()[:, mh * 256 + c * 128:mh * 256 + (c + 1) * 128],
                                                  identity=ident[:])
                            return ins
                        S.op("pe", trk, r=[stgk.buf, bconst], w=[bk.buf])
                        en = B.ev()
                        S.op(en, _copy(en, mks.b()[:, c * 256:(c + 1) * 256], bk.ap[:, 0:256]), r=[bk.buf], w=[mks.buf])
                    S.op("pool", lambda e, stgv=stgv, mvs=mvs: e.tensor_copy(out=mvs.b(), in_=stgv.f()), r=[stgv.buf], w=[mvs.buf])
                    mem_attention(qm, w, lambda c, mks=mks: mks.b()[:, c * 256:(c + 1) * 256], lambda mh, mvs=mvs: mvs.b()[:, mh * 256:(mh + 1) * 256],
                                  [mks.buf, mvs.buf], mix[CC:CC + 2], cols=[(b, 1)])
                    B.release([stgk, stgv, mks, mvs])
                B.release(qm)
                proj_post_add(xsT, w, mix, w_out[l], "mix_post", l * KC)
                B.release(mix)
                ffn(xsT, w, l)
                if l == NA - 1:
                    hk, rstd = rmsnorm(xsT, w, "kv_g", 0)
                    B.release(rstd)
                    for grp in range(6):
                        def cons_kvs(s_, n, bk, grp=grp):
                            stg = B.alloc()
                            S.op("act", lambda e: e.activation(out=stg.f(256)[0:n, :], in_=bk.ap[0:n, 0:256], func=AF.Copy), r=[bk.buf], w=[stg.buf])
                            dstd = k_sample if grp < 3 else v_sample
                            c0 = (grp % 3) * 256
                            S.op("sp", lambda e: e.dma_start(out=dstd[0:n, c0:c0 + 256], in_=stg.f(256)[0:n, :]), r=[stg.buf], dma=True)
                            B.release(stg)
                        linear_tok(hk, w, w_kv, grp * 256, 256, cons_kvs)
                    B.release(hk)
            store_T(xsT, w, y_sample, D)

        S.emit(st)
    return nc


_NC_CACHE = {}


def kernel(**inputs):
    n = 8
    if "nc" not in _NC_CACHE:
        _NC_CACHE["nc"] = build_program()
    nc = _NC_CACHE["nc"]
    f = lambda a: np.ascontiguousarray(np.asarray(a, dtype=np.float32))
    if SAMPLE:
        ck = f(inputs["cache_k"]).reshape(NPHYS, PAGE * C)
        cv = f(inputs["cache_v"]).reshape(NPHYS, PAGE * C)
    shared = {
        "norm_mix_pre": f(inputs["norm_mix_pre"]), "norm_mix_post": f(inputs["norm_mix_post"]),
        "norm_ffn_pre": f(inputs["norm_ffn_pre"]), "norm_ffn_post": f(inputs["norm_ffn_post"]),
        "w_in_a": f(inputs["w_in_a"]), "conv_w": f(inputs["conv_w"]), "conv_b": f(inputs["conv_b"]),
        "conv_ln_g": f(inputs["conv_ln_g"]), "conv_ln_b": f(inputs["conv_ln_b"]),
        "w_in_b": f(inputs["w_in_b"]), "sb_bias": f(inputs["sb_bias"]).reshape(1, -1),
        "kv_norm_g": f(inputs["kv_norm_g"]).reshape(1, D), "w_kv": f(inputs["w_kv"]),
        "mem_norm_g": f(inputs["mem_norm_g"]), "w_mem_kv": f(inputs["w_mem_kv"]), "w_out": f(inputs["w_out"]),
        "w_ffn_up": f(inputs["w_ffn_up"]), "w_ffn_down": f(inputs["w_ffn_down"]),
    }
    xpr = f(inputs["x_prompt"])
    xsm = f(inputs["x_sample"])
    stc = f(inputs["state_conv"])
    cmk = f(inputs["cache_mem_k"])
    cmv = f(inputs["cache_mem_v"])
    pt = np.ascontiguousarray(np.asarray(inputs["page_table"], dtype=np.int32))
    mp = f(inputs["mem_prompt"])
    in_maps = []
    for c in range(n):
        m = dict(shared)
        m["xp"] = xpr[c]
        if SAMPLE:
            m["cache_k"] = ck
            m["cache_v"] = cv
            m["xs"] = np.ascontiguousarray(xsm[4 * c:4 * c + 4, 0, :])
            m["state_conv"] = np.ascontiguousarray(stc[:, 4 * c:4 * c + 4])
            m["cmk"] = np.ascontiguousarray(cmk[:, 4 * c:4 * c + 4].reshape(L, NS, NMEM, 256))
            m["cmv"] = np.ascontiguousarray(cmv[:, 4 * c:4 * c + 4].reshape(L, NS, NMEM, 256))
            m["ptab"] = np.ascontiguousarray(pt[4 * c:4 * c + 4].reshape(NS * NPAGE, 1))
        m["memp"] = mp[c]
        in_maps.append(m)
    if DBG_CORES < n:
        res = run_bass_kernel_spmd(nc, in_maps[:DBG_CORES], core_ids=list(range(DBG_CORES)))
        R = list(res.results) + [res.results[0]] * (n - DBG_CORES)
    else:
        res = run_bass_kernel_spmd(nc, in_maps, core_ids=list(range(n)))
        R = res.results
    y_prompt = np.stack([R[c]["y_prompt"] for c in range(n)], 0)
    y_sample = np.concatenate([R[c]["y_sample"] for c in range(n)], 0).reshape(32, 1, D)
    k_prompt = np.stack([R[c]["k_prompt"] for c in range(n)], 0).reshape(n, T, NH, 64)
    v_prompt = np.stack([R[c]["v_prompt"] for c in range(n)], 0).reshape(n, T, NH, 64)
    conv_prompt = np.stack([R[c]["conv_prompt"] for c in range(n)], 1)
    mk = np.stack([R[c]["mk_p"] for c in range(n)], 1).reshape(L, n, NMEM, 4, 64)
    mv = np.stack([R[c]["mv_p"] for c in range(n)], 1).reshape(L, n, NMEM, 4, 64)
    k_sample = np.concatenate([R[c]["k_sample"] for c in range(n)], 0).reshape(32, 1, NH, 64)
    v_sample = np.concatenate([R[c]["v_sample"] for c in range(n)], 0).reshape(32, 1, NH, 64)
    conv_sample = np.concatenate([R[c]["conv_sample"] for c in range(n)], 1)
    return (y_prompt, y_sample, k_prompt, v_prompt, conv_prompt, mk, mv, k_sample, v_sample, conv_sample)
```
